# Optimizing a Trainium2 kernel written in Bass

```python
import math
import jax
import jax.numpy as jnp
from jax import lax
import numpy as np

D_MODEL = 1024
BATCH = 16
SEQ = 256
DEPTH = 1
DEC_BATCH = 2
DEC_SEQ = 1024
PAST_LEN = 512

GRID_W = 64
GLA_HEADS = 4
GLA_DK = 64
GLA_DV = 128
GLA_QK = GLA_HEADS * GLA_DK
GLA_WIDTH = GLA_HEADS * GLA_DV
GLA_GATE_RANK = 16
GLA_GATE_TEMP = 16.0
GLA_CHUNK = 64
DIFF_HEADS = 4
DIFF_DH = 64
DIFF_DV = 2 * DIFF_DH
DIFF_QK = DIFF_HEADS * 2 * DIFF_DH
DIFF_WIDTH = DIFF_HEADS * DIFF_DV
MIX_WIDTH = GLA_WIDTH + DIFF_WIDTH
ROPE_PAIRS = DIFF_DH // 4
ROPE_BASE = 10000.0
Q_BLOCK = 128
EPS = 1e-6
PROJ_SIZES = (GLA_QK, GLA_QK, GLA_WIDTH, GLA_GATE_RANK, GLA_GATE_RANK, DIFF_QK, DIFF_QK, DIFF_WIDTH, MIX_WIDTH)
PROJ_WIDTH = 3616

kernel_name = 'hybrid_gla_diffattn_prefix_dit_step'


def rmsnorm(x, gain):
    x32 = x.astype(jnp.float32)
    y = x32 * lax.rsqrt(jnp.mean(x32 * x32, axis=-1, keepdims=True) + EPS)
    return (y * gain.astype(jnp.float32)).astype(x.dtype)


def adaln_params(cvec, w_mod_l, b_mod_l):
    m = jnp.einsum('...d,de->...e', jax.nn.silu(cvec), w_mod_l) + b_mod_l
    shift, scale, gate = jnp.split(m, 3, axis=-1)
    return shift, scale, gate


def axial_rope_tables(rows, dtype):
    row = jnp.repeat(jnp.arange(rows), GRID_W).astype(jnp.float32)
    col = jnp.tile(jnp.arange(GRID_W), rows).astype(jnp.float32)
    inv_freq = ROPE_BASE ** (-jnp.arange(ROPE_PAIRS, dtype=jnp.float32) / ROPE_PAIRS)
    ang_r = row[:, None] * inv_freq[None, :]
    ang_c = col[:, None] * inv_freq[None, :]
    return (jnp.cos(ang_r).astype(dtype), jnp.sin(ang_r).astype(dtype),
            jnp.cos(ang_c).astype(dtype), jnp.sin(ang_c).astype(dtype))


def rotate_pairs(v, cos, sin):
    v1, v2 = jnp.split(v, 2, axis=-1)
    cos = cos[None, :, None, None, :]
    sin = sin[None, :, None, None, :]
    return jnp.concatenate([v1 * cos - v2 * sin, v1 * sin + v2 * cos], axis=-1)


def apply_axial_rope(x, tables):
    cos_r, sin_r, cos_c, sin_c = tables
    xr, xc = jnp.split(x, 2, axis=-1)
    return jnp.concatenate([rotate_pairs(xr, cos_r, sin_r), rotate_pairs(xc, cos_c, sin_c)], axis=-1)


def project(h, w_in_l, w_alpha_l, b_alpha_l):
    bsz, n, _ = h.shape
    p = jnp.einsum('bld,dp->blp', h, w_in_l)
    idx = [int(i) for i in np.cumsum(PROJ_SIZES)[:-1]]
    gq, gk, gv, lr_f, lr_b, dq, dk, dv, gate = jnp.split(p, idx, axis=-1)
    gq = gq.reshape(bsz, n, GLA_HEADS, GLA_DK) * (GLA_DK ** -0.5)
    gk = gk.reshape(bsz, n, GLA_HEADS, GLA_DK)
    gv = gv.reshape(bsz, n, GLA_HEADS, GLA_DV)
    z_f = jnp.einsum('blr,rk->blk', lr_f, w_alpha_l[0]) + b_alpha_l[0]
    z_b = jnp.einsum('blr,rk->blk', lr_b, w_alpha_l[1]) + b_alpha_l[1]
    g_f = (jax.nn.log_sigmoid(z_f.astype(jnp.float32)) / GLA_GATE_TEMP).reshape(bsz, n, GLA_HEADS, GLA_DK)
    g_b = (jax.nn.log_sigmoid(z_b.astype(jnp.float32)) / GLA_GATE_TEMP).reshape(bsz, n, GLA_HEADS, GLA_DK)
    dq = dq.reshape(bsz, n, DIFF_HEADS, 2, DIFF_DH)
    dk = dk.reshape(bsz, n, DIFF_HEADS, 2, DIFF_DH)
    dv = dv.reshape(bsz, n, DIFF_HEADS, DIFF_DV)
    return gq, gk, gv, g_f, g_b, dq, dk, dv, gate


def gla_chunk_scan(q, k, v, g, s0):
    bsz, n, nh, dkk = q.shape
    dvv = v.shape[-1]
    nc = n // GLA_CHUNK

    def to_chunks(a):
        return a.astype(jnp.float32).reshape(bsz, nc, GLA_CHUNK, nh, a.shape[-1]).swapaxes(0, 1)

    causal = jnp.tril(jnp.ones((GLA_CHUNK, GLA_CHUNK), dtype=bool))[None, :, :, None, None]

    def step(s, inp):
        qc, kc, vc, gc = inp
        b = jnp.cumsum(gc, axis=1)
        o_inter = jnp.einsum('bchk,bhkv->bchv', qc * jnp.exp(b), s)
        diff = b[:, :, None] - b[:, None, :]
        decay = jnp.where(causal, jnp.exp(jnp.minimum(diff, 0.0)), 0.0)
        att = jnp.einsum('bthk,btshk,bshk->bhts', qc, decay, kc)
        o_intra = jnp.einsum('bhts,bshv->bthv', att, vc)
        b_last = b[:, -1]
        s_new = jnp.exp(b_last)[..., None] * s + jnp.einsum(
            'bshk,bshv->bhkv', kc * jnp.exp(b_last[:, None] - b), vc)
        return s_new, o_inter + o_intra

    s_fin, o = lax.scan(step, s0.astype(jnp.float32), (to_chunks(q), to_chunks(k), to_chunks(v), to_chunks(g)))
    o = o.swapaxes(0, 1).reshape(bsz, n, nh, dvv)
    return o, s_fin


def gla_bidirectional(q, k, v, g_f, g_b, s0_f, s0_b):
    o_f, s_f = gla_chunk_scan(q, k, v, g_f, s0_f)
    o_b, s_b = gla_chunk_scan(jnp.flip(q, 1), jnp.flip(k, 1), jnp.flip(v, 1), jnp.flip(g_b, 1), s0_b)
    return o_f + jnp.flip(o_b, 1), s_f, s_b


def diff_lambda_value(lam_params, lam_init):
    lp = lam_params.astype(jnp.float32)
    return jnp.exp(jnp.sum(lp[0] * lp[1])) - jnp.exp(jnp.sum(lp[2] * lp[3])) + lam_init


def diff_attention(q, k, v, lam):
    bsz, nq, nh, _, dh = q.shape
    nb = nq // Q_BLOCK
    qb = jnp.moveaxis(q.reshape(bsz, nb, Q_BLOCK, nh, 2, dh), 1, 0)
    scale = dh ** -0.5

    def block(qi):
        s = jnp.einsum('bqhid,bkhid->bihqk', qi, k).astype(jnp.float32) * scale
        p = jax.nn.softmax(s, axis=-1)
        a = p[:, 0] - lam * p[:, 1]
        return jnp.einsum('bhqk,bkhv->bqhv', a.astype(v.dtype), v)

    o = lax.map(block, qb)
    return jnp.moveaxis(o, 0, 1).reshape(bsz, nq, nh, v.shape[-1])


def merge_heads(o_gla, o_diff, gate, gla_gain, diff_gain, lam_init, w_out_l):
    bsz, n = gate.shape[0], gate.shape[1]
    og = rmsnorm(o_gla.astype(gate.dtype), gla_gain).reshape(bsz, n, GLA_WIDTH)
    od = (rmsnorm(o_diff, diff_gain) * (1.0 - lam_init)).reshape(bsz, n, DIFF_WIDTH)
    o = jnp.concatenate([og, od], axis=-1) * jax.nn.silu(gate)
    return jnp.einsum('blm,md->bld', o, w_out_l)


def setup_inputs(seed: int = 0) -> dict:
    key = jax.random.key(seed)
    ks = jax.random.split(key, 19)

    def nrm(k, shape, s):
        return s * jax.random.normal(k, shape, jnp.float32)

    return {
        'x_prompt': nrm(ks[0], (BATCH, SEQ, D_MODEL), 1.0),
        'x_sample': nrm(ks[1], (DEC_BATCH, DEC_SEQ, D_MODEL), 1.0),
        'cache_diff_k': nrm(ks[2], (DEC_BATCH, DEPTH, PAST_LEN, DIFF_HEADS, 2, DIFF_DH), 1.0),
        'cache_diff_v': nrm(ks[3], (DEC_BATCH, DEPTH, PAST_LEN, DIFF_HEADS, DIFF_DV), 1.0),
        'state_gla_fwd': nrm(ks[4], (DEC_BATCH, DEPTH, GLA_HEADS, GLA_DK, GLA_DV), 1.0),
        'state_gla_bwd': nrm(ks[5], (DEC_BATCH, DEPTH, GLA_HEADS, GLA_DK, GLA_DV), 1.0),
        'c': nrm(ks[6], (DEC_BATCH, D_MODEL), 1.0),
        'c_ctx': nrm(ks[7], (D_MODEL,), 1.0),
        'norm_gain': 1.0 + nrm(ks[8], (DEPTH, D_MODEL), 0.02),
        'w_mod': nrm(ks[9], (DEPTH, D_MODEL, 3 * D_MODEL), D_MODEL ** -0.5),
        'b_mod': nrm(ks[10], (DEPTH, 3 * D_MODEL), 0.02),
        'w_in': nrm(ks[11], (DEPTH, D_MODEL, PROJ_WIDTH), D_MODEL ** -0.5),
        'w_gla_alpha': nrm(ks[12], (DEPTH, 2, GLA_GATE_RANK, GLA_QK), GLA_GATE_RANK ** -0.5),
        'b_gla_alpha': 1.0 + nrm(ks[13], (DEPTH, 2, GLA_QK), 0.5),
        'diff_lambda': nrm(ks[14], (DEPTH, 4, DIFF_DH), 0.1),
        'gla_head_gain': 1.0 + nrm(ks[15], (DEPTH, GLA_DV), 0.02),
        'diff_head_gain': 1.0 + nrm(ks[16], (DEPTH, DIFF_DV), 0.02),
        'w_out': nrm(ks[17], (DEPTH, MIX_WIDTH, D_MODEL), MIX_WIDTH ** -0.5),
        'final_gain': 1.0 + nrm(ks[18], (D_MODEL,), 0.02),
    }


def reference(x_prompt, x_sample, cache_diff_k, cache_diff_v, state_gla_fwd, state_gla_bwd, c, c_ctx,
              norm_gain, w_mod, b_mod, w_in, w_gla_alpha, b_gla_alpha, diff_lambda,
              gla_head_gain, diff_head_gain, w_out, final_gain):
    n_lat = x_sample.shape[1]
    ROWS = n_lat // GRID_W
    rope_tables = axial_rope_tables(ROWS, x_sample.dtype)
    bsz_p = x_prompt.shape[0]
    zero_state = jnp.zeros((bsz_p, GLA_HEADS, GLA_DK, GLA_DV), jnp.float32)

    xp = x_prompt
    xs = x_sample
    new_k, new_v, new_sf, new_sb = [], [], [], []
    for l in range(DEPTH):
        lam_init = 0.8 - 0.6 * math.exp(-0.3 * l)
        lam = diff_lambda_value(diff_lambda[l], lam_init)

        sh, sc, gt = adaln_params(c_ctx, w_mod[l], b_mod[l])
        h = rmsnorm(xp, norm_gain[l]) * (1.0 + sc) + sh
        gq, gk, gv, g_f, g_b, dq, dk, dv, gate = project(h, w_in[l], w_gla_alpha[l], b_gla_alpha[l])
        o_gla, s_f, s_b = gla_bidirectional(gq, gk, gv, g_f, g_b, zero_state, zero_state)
        o_diff = diff_attention(dq, dk, dv, lam)
        xp = xp + gt * merge_heads(o_gla, o_diff, gate, gla_head_gain[l], diff_head_gain[l], lam_init, w_out[l])
        new_k.append(dk)
        new_v.append(dv)
        new_sf.append(s_f.astype(x_prompt.dtype))
        new_sb.append(s_b.astype(x_prompt.dtype))

        sh, sc, gt = adaln_params(c, w_mod[l], b_mod[l])
        h = rmsnorm(xs, norm_gain[l]) * (1.0 + sc[:, None, :]) + sh[:, None, :]
        gq, gk, gv, g_f, g_b, dq, dk, dv, gate = project(h, w_in[l], w_gla_alpha[l], b_gla_alpha[l])
        o_gla, _, _ = gla_bidirectional(gq, gk, gv, g_f, g_b, state_gla_fwd[:, l], state_gla_bwd[:, l])
        dq = apply_axial_rope(dq, rope_tables)
        dk = apply_axial_rope(dk, rope_tables)
        k_all = jnp.concatenate([cache_diff_k[:, l], dk], axis=1)
        v_all = jnp.concatenate([cache_diff_v[:, l], dv], axis=1)
        o_diff = diff_attention(dq, k_all, v_all, lam)
        xs = xs + gt[:, None, :] * merge_heads(o_gla, o_diff, gate, gla_head_gain[l], diff_head_gain[l], lam_init, w_out[l])

    y_prompt = rmsnorm(xp, final_gain)
    y_sample = rmsnorm(xs, final_gain)
    new_diff_k = jnp.stack(new_k, axis=1)
    new_diff_v = jnp.stack(new_v, axis=1)
    new_gla_fwd = jnp.stack(new_sf, axis=1)
    new_gla_bwd = jnp.stack(new_sb, axis=1)
    return (y_prompt, y_sample, new_diff_k, new_diff_v, new_gla_fwd, new_gla_bwd)
```

```python
import math
from contextlib import ExitStack

import numpy as np
import concourse.bass as bass
import concourse.mybir as mybir
from concourse.bass_utils import run_bass_kernel_spmd

F32 = mybir.dt.float32
BF16 = mybir.dt.bfloat16
AF = mybir.ActivationFunctionType
ALU = mybir.AluOpType
AX = mybir.AxisListType

N_CORES = 8
D = 1024
EPS = 1e-6
NTOK = 1536
NTILE = 12
PROJ = 3616
LAM_INIT = 0.2
SCALE_GLA = 0.125
SCALE_ATT = 0.125

C_LF01, C_LB01, C_LF16, C_LB16, C_MSF16, C_MSB16, C_ID, C_R, C_ONES = [i * 128 for i in range(9)]
C_CI16 = 9 * 128
C_MSF16F = 9 * 128 + 2
C_MSB16F = C_MSF16F + 128
C_ONE16 = C_MSB16F + 128
NCONST = C_ONE16 + 1
ARENA_BYTES = 204 * 1024
GTENG = "act"
KH_ENG = "dve"
QK_ENG = "pool"
SV_ENG = "act"
OG_ENG = "dve"
SP_SLOTS = 6


class Tr:
    __slots__ = ("w", "r", "excl")

    def __init__(self, excl=False):
        self.w = None
        self.r = {}
        self.excl = excl


class TT:
    __slots__ = ("ap", "tr")

    def __init__(self, ap, tr):
        self.ap = ap
        self.tr = tr

    def __getitem__(self, idx):
        return TT(self.ap[idx], self.tr)

    def v(self, fn):
        return TT(fn(self.ap), self.tr)


class Prog:
    def __init__(self, nc, es, n_slots=6):
        self.nc = nc
        self.es = es
        self.eng = {"pe": nc.tensor, "act": nc.scalar, "dve": nc.vector, "pool": nc.gpsimd, "sp": nc.sync}
        self.sem = {}
        self.count = {}
        self.known = {k: {} for k in self.eng}
        for k in self.eng:
            self.sem[k] = es.enter_context(nc.semaphore("s_" + k))
            self.count[k] = 0
        self.slots = {}
        self.slot_next = {}
        for q in ("sp", "pool", "act"):
            self.slots[q] = []
            for i in range(SP_SLOTS if q == 'sp' else n_slots):
                key = "d_%s_%d" % (q, i)
                self.sem[key] = es.enter_context(nc.semaphore(key))
                self.count[key] = 0
                self.slots[q].append(key)
            self.slot_next[q] = 0
        self.n_wait = 0
        self.n_ins = 0
        self.freed_events = {}
        self.tcount = 0
        self.regions = {}

    def init_arena(self, nbytes):
        self.arena = self.nc.alloc_sbuf_tensor("arena", [128, nbytes], mybir.dt.uint8)
        self.free_list = [[0, nbytes, {}]]
        self.peak = 0
        self.used = 0

    def sb(self, shape, dt, name=None):
        esz = 4 if dt == F32 else 2
        per_part = esz
        for s in shape[1:]:
            per_part *= s
        size = (per_part + 63) // 64 * 64
        fl = self.free_list
        for i in range(len(fl)):
            tot = 0
            j = i
            while j < len(fl) and (j == i or fl[j][0] == fl[j - 1][1]):
                tot += fl[j][1] - fl[j][0]
                if tot >= size:
                    break
                j += 1
            if tot >= size:
                start = fl[i][0]
                evs = {}
                need = size
                k = i
                new_items = []
                while need > 0:
                    s0, e0, ev = fl[k]
                    for kk, vv in ev.items():
                        evs[kk] = max(evs.get(kk, 0), vv)
                    take = min(need, e0 - s0)
                    need -= take
                    if take < e0 - s0:
                        new_items.append([s0 + take, e0, ev])
                    k += 1
                fl[i:k] = new_items
                tr = Tr()
                tr.r = evs
                v = self.arena[:, start:start + per_part].bitcast(dt)
                if len(shape) > 2:
                    names = " ".join("d%d" % n for n in range(1, len(shape)))
                    kw = {"d%d" % n: shape[n] for n in range(1, len(shape) - 1)}
                    v = v.rearrange("p (%s) -> p %s" % (names, names), **kw)
                if shape[0] < 128:
                    v = v[0:shape[0]]
                t = TT(v, tr)
                self.regions[id(tr)] = (start, start + size)
                self.used += size
                self.peak = max(self.peak, self.used)
                return t
        raise RuntimeError("arena out of memory allocating %s %s (used=%d)" % (name, shape, self.used))

    def free(self, *tiles):
        for t in tiles:
            s0, e0 = self.regions.pop(id(t.tr))
            ev = dict(t.tr.r)
            if t.tr.w is not None:
                k, v = t.tr.w
                ev[k] = max(ev.get(k, 0), v)
            self.free_list.append([s0, e0, ev])
            self.used -= e0 - s0
        self.free_list.sort(key=lambda x: x[0])

    def _wait(self, eng, key, val):
        if self.known[eng].get(key, 0) >= val:
            return
        self.eng[eng].wait_ge(self.sem[key], val)
        self.known[eng][key] = val
        self.n_wait += 1

    def _deps(self, eng, reads, writes):
        evs = {}
        for t in reads:
            if t.tr.w is not None:
                k, v = t.tr.w
                evs[k] = max(evs.get(k, 0), v)
            if t.tr.excl:
                for k, v in t.tr.r.items():
                    if k != eng:
                        evs[k] = max(evs.get(k, 0), v)
        for t in writes:
            if t.tr.w is not None:
                k, v = t.tr.w
                evs[k] = max(evs.get(k, 0), v)
            for k, v in t.tr.r.items():
                evs[k] = max(evs.get(k, 0), v)
        for k, v in evs.items():
            if k == "pe" and eng == "pe":
                continue
            self._wait(eng, k, v)

    def _commit(self, ev, reads, writes):
        k, v = ev
        for t in reads:
            t.tr.r[k] = max(t.tr.r.get(k, 0), v)
        for t in writes:
            t.tr.w = ev
            t.tr.r = {}

    def op(self, eng, fn, reads=(), writes=(), inc=True):
        self._deps(eng, reads, writes)
        ins = fn(self.eng[eng])
        self.n_ins += 1
        if inc:
            self.count[eng] += 1
            ins.then_inc(self.sem[eng], 1)
            self._commit((eng, self.count[eng]), reads, writes)
        else:
            assert eng == "pe"
            self._commit((eng, self.count[eng] + 1), reads, writes)

    def dma(self, q, out_ap, in_ap, reads=(), writes=()):
        self._deps(q, reads, writes)
        i = self.slot_next[q]
        self.slot_next[q] = (i + 1) % len(self.slots[q])
        key = self.slots[q][i]
        if self.count[key] > 0:
            self._wait(q, key, 16 * self.count[key])
        ins = self.eng[q].dma_start(out=out_ap, in_=in_ap)
        self.count[key] += 1
        ins.then_inc(self.sem[key], 16)
        self.n_ins += 1
        self._commit((key, 16 * self.count[key]), reads, writes)

    def finish(self, eng="sp"):
        for q in self.slots:
            for key in self.slots[q]:
                if self.count[key] > 0:
                    self._wait(eng, key, 16 * self.count[key])


class PsumPool:
    def __init__(self, nc, n=8):
        self.banks = []
        for i in range(n):
            h = nc.alloc_psum_tensor("psb%d" % i, [128, 512], F32)
            self.banks.append(TT(h[:, :], Tr(excl=True)))
        self.free = list(range(n))

    def get(self):
        i = self.free.pop(0)
        b = self.banks[i]
        b_id = i
        t = TT(b.ap, b.tr)
        return t, b_id

    def put(self, b_id):
        self.free.append(b_id)


class _Stop(Exception):
    pass


def build_program(stop=99):
    nc = bass.Bass("TRN2", target_bir_lowering=False)

    def din(name, shape):
        return nc.dram_tensor(name, list(shape), F32, kind="ExternalInput").ap()

    def dout(name, shape):
        return nc.dram_tensor(name, list(shape), F32, kind="ExternalOutput").ap()

    xT_d = din("xT", [8, 128, NTOK])
    xtm_d = din("xtm", [768, D])
    cvT_d = din("cvT", [128, 8, 2])
    gainT_d = din("gainT", [128, 8])
    wmod_d = din("wmod", [D, 3 * D])
    bmod_d = din("bmod", [3 * D])
    win_d = din("win", [D, PROJ])
    wout_d = din("wout", [D, D])
    wlr_d = din("wlr", [128, 8 * 32])
    wal_d = din("walpha", [17, 2, 256])
    dlam_d = din("dlam", [256])
    hgain_d = din("hgain", [D])
    fgain_d = din("fgain", [D])
    ckT_d = din("ckT", [4, 128, 512])
    cv_d = din("cv", [512, 512])
    s0_d = din("s0", [2, 2, 128, 128])
    sel_d = din("sel", [128, 8])
    cos_d = din("ropecos", [128, 1024])
    sin_d = din("ropesin", [128, 1024])
    const_d = din("consts", [128, NCONST])

    yp_d = dout("yp", [512, D])
    ys_d = dout("ys", [256, D])
    ndk_d = dout("ndk", [512, 512])
    ndv_d = dout("ndv", [512, 512])
    ns_d = dout("ns", [2, 2, 2, 128, 128])

    es = ExitStack()
    try:
      with es:
        P = Prog(nc, es)
        P.init_arena(ARENA_BYTES)
        PS = PsumPool(nc)

        def ckpt(k):
            if stop <= k:
                P.finish("sp")
                print("[build] STOP at %s instructions=%d" % (k, P.n_ins))
                raise _Stop()

        def op(eng, fn, reads=(), writes=(), inc=True):
            P.op(eng, fn, reads, writes, inc)

        def run_streams(gens):
            gens = list(gens)
            while gens:
                for g in list(gens):
                    try:
                        next(g)
                    except StopIteration:
                        gens.remove(g)

        PS_RESERVE = [0]

        def get_banks(n):
            while len(PS.free) < n + PS_RESERVE[0]:
                yield
            return [PS.get() for _ in range(n)]

        consts = P.sb([128, NCONST], F32, "consts")
        constb = P.sb([128, NCONST], BF16, "constb")
        P.dma("sp", consts.ap, const_d, writes=[consts])
        op("dve", lambda e: e.tensor_copy(out=constb.ap, in_=consts.ap), [consts], [constb])
        epsb = P.sb([128, 1], F32, "epsb")
        op("dve", lambda e: e.memset(epsb.ap, EPS), [], [epsb])
        junk = P.sb([128, D], F32, "junk")

        cvT = P.sb([128, 8, 2], F32, "cvT")
        P.dma("sp", cvT.ap, cvT_d, writes=[cvT])
        gainT = P.sb([128, 8], F32, "gainT")
        P.dma("sp", gainT.ap, gainT_d, writes=[gainT])
        bmod2 = P.sb([2, 3 * D], F32, "bmod2")
        P.dma("sp", bmod2.ap, bmod_d.partition_broadcast(2), writes=[bmod2])
        cs = P.sb([128, 8, 2], BF16, "cs")
        op("act", lambda e: e.activation(out=cs.ap, in_=cvT.ap, func=AF.Silu), [cvT], [cs])
        mrow = P.sb([2, 3 * D], F32, "mrow")
        wmb = [P.sb([128, 8, 512], BF16, "wmb%d" % i) for i in range(4)]
        wbufs = [P.sb([128, 8, 512], BF16, "wbuf%d" % i) for i in range(3)]
        wAB = P.sb([128, 8, 544], BF16, "wAB")
        wctr = [0]

        def load_w(src_d, c0, n, wb=None, after=()):
            if wb is None:
                wb = wbufs[wctr[0] % len(wbufs)]
                wctr[0] += 1
            src = src_d[:, c0:c0 + n].rearrange("(k p) n -> p k n", p=128)
            P.dma("pool", wb.ap[:, :, 0:n], src, reads=list(after), writes=[wb])
            return wb

        xu = [P.sb([128, 8, 512], F32, "xu%d" % b) for b in range(3)]
        sqb = [P.sb([128, 8, 512], BF16, "sqb%d" % i) for i in range(2)]
        rs1 = [P.sb([128, 512], F32, "rs1_%d" % i) for i in range(2)]
        rstd = [P.sb([128, 512], F32, "rstd%d" % i) for i in range(2)]
        xT_v = xT_d.rearrange("k p t -> p k t")
        xdone = TT(None, Tr())
        for b in range(3):
            P.dma("sp", xu[b].ap, xT_v[:, :, b * 512:(b + 1) * 512], writes=[xu[b]] + ([xdone] if b == 2 else []))
        for nb in range(4):
            load_w(wmod_d, nb * 512, 512, wmb[nb], after=[xdone])

        def mod_rows(wb, nb):
            (ps, pid), = yield from get_banks(1)
            for kc in range(8):
                op("pe", lambda e: e.matmul(ps.ap[0:2, :], lhsT=cs.ap[:, kc, :], rhs=wb.ap[:, kc, :],
                                            start=(kc == 0), stop=(kc == 7)), [cs, wb], [ps], inc=(kc == 7))
            yield
            op("dve", lambda e: e.tensor_tensor(out=mrow.ap[:, nb * 512:(nb + 1) * 512], in0=ps.ap[0:2, :],
                                                in1=bmod2.ap[:, nb * 512:(nb + 1) * 512], op=ALU.add), [ps, bmod2], [mrow])
            PS.put(pid)
            yield

        def stream_x():
            for b in range(3):
                xs = xu[b]
                sq = sqb[b % 2]
                for hf in range(2):
                    op("act", lambda e: e.activation(out=sq.ap[:, hf * 4:(hf + 1) * 4, :], in_=xs.ap[:, hf * 4:(hf + 1) * 4, :], func=AF.Square), [xs], [sq])
                yield
                (ps, pid), = yield from get_banks(1)
                for kc in range(8):
                    op("pe", lambda e: e.matmul(ps.ap, lhsT=constb.ap[:, C_ONES:C_ONES + 128], rhs=sq.ap[:, kc, :],
                                                start=(kc == 0), stop=(kc == 7)), [constb, sq], [ps], inc=(kc == 7))
                yield
                op("act", lambda e: e.activation(out=rs1[b % 2].ap, in_=ps.ap, func=AF.Ln, bias=epsb.ap, scale=1.0 / D),
                   [ps, epsb], [rs1[b % 2]])
                PS.put(pid)
                op("act", lambda e: e.activation(out=rstd[b % 2].ap, in_=rs1[b % 2].ap, func=AF.Exp, scale=-0.5), [rs1[b % 2]], [rstd[b % 2]])
                yield
                for hf in range(2):
                    op("dve", lambda e: e.tensor_tensor(out=xs.ap[:, hf * 4:(hf + 1) * 4, :], in0=xs.ap[:, hf * 4:(hf + 1) * 4, :],
                                                        in1=rstd[b % 2].ap.unsqueeze(1).to_broadcast([128, 4, 512]), op=ALU.mult),
                       [xs, rstd[b % 2]], [xs])
                yield

        def stream_m():
            for nb in range(4):
                yield from mod_rows(wmb[nb], nb)

        run_streams([stream_x()])
        run_streams([stream_m()])
        P.free(sqb[0], sqb[1], rs1[0], rs1[1], rstd[0], rstd[1], *wmb)

        modT = P.sb([128, 2, 8, 2], F32, "modT")
        ps, pid = PS.get()
        for which in range(2):
            for kc in range(8):
                o = (which * 8 + kc) * 2
                c0 = which * D + kc * 128
                op("pe", lambda e: e.transpose(ps.ap[:, o:o + 2], mrow.ap[0:2, c0:c0 + 128], consts.ap[0:2, C_ID:C_ID + 2]),
                   [mrow, consts], [ps], inc=(which == 1 and kc == 7))
        op("dve", lambda e: e.tensor_copy(out=modT.ap.rearrange("p a b c -> p (a b c)"), in_=ps.ap[:, 0:32]), [ps], [modT])
        PS.put(pid)
        aT = P.sb([128, 8, 2], F32, "aT")
        for i in range(2):
            op("dve", lambda e: e.scalar_tensor_tensor(out=aT.ap[:, :, i], in0=modT.ap[:, 1, :, i], scalar=1.0,
                                                       in1=gainT.ap, op0=ALU.add, op1=ALU.mult), [modT, gainT], [aT])
        gt_bc = [P.sb([128, D], F32, "gtbc%d" % i) for i in range(2)]

        def gate_rows():
            for nb in (4, 5):
                yield from mod_rows(gate_wb[nb - 4], nb)

        def gate_bcast():
            for i in range(2):
                for nb in range(2):
                    (ps, pid), = yield from get_banks(1)
                    op("pe", lambda e: e.matmul(ps.ap, lhsT=selr.ap[:, i, :], rhs=mrow.ap[0:2, 2 * D + nb * 512:2 * D + (nb + 1) * 512],
                                                start=True, stop=True), [selr, mrow], [ps])
                    op("act", lambda e: e.copy(out=gt_bc[i].ap[:, nb * 512:(nb + 1) * 512], in_=ps.ap), [ps], [gt_bc[i]])
                    PS.put(pid)
            P.free(mrow, bmod2, selr, cvT, cs)

        hT2 = [P.sb([128, 8, 512], BF16, "hT%d" % b) for b in range(3)]
        hTa = [TT(hT2[b].ap, Tr()) for b in range(3)]
        hTd = [TT(hT2[b].ap, Tr()) for b in range(3)]
        for b in range(3):
            hTa[b].tr.r = dict(hT2[b].tr.r)
            hTd[b].tr.r = dict(hT2[b].tr.r)
        for b in range(3):
            ii = 0 if b < 1 else 1
            hb = hT2[b]
            for kc in range(8):
                if kc % 3 == 0:
                    op("act", lambda e: e.activation(out=hb.ap[:, kc, :], in_=xu[b].ap[:, kc, :], func=AF.Identity,
                                                     bias=modT.ap[:, 0, kc, ii:ii + 1], scale=aT.ap[:, kc, ii:ii + 1]), [xu[b], modT, aT], [hTa[b]])
                else:
                    op("dve", lambda e: e.tensor_scalar(out=hb.ap[:, kc, :], in0=xu[b].ap[:, kc, :], scalar1=aT.ap[:, kc, ii:ii + 1],
                                                        scalar2=modT.ap[:, 0, kc, ii:ii + 1], op0=ALU.mult, op1=ALU.add), [xu[b], modT, aT], [hTd[b]])
        P.free(*xu)
        P.free(gainT, aT)
        ckpt(1)

        dlam = P.sb([128, 256], F32, "dlam")
        P.dma("sp", dlam.ap, dlam_d.partition_broadcast(128), writes=[dlam])
        lam2 = P.sb([128, 2], F32, "lam2")
        lscr = P.sb([128, 64], F32, "lscr")
        for i in range(2):
            op("dve", lambda e: e.tensor_tensor(out=lscr.ap, in0=dlam.ap[:, (2 * i) * 64:(2 * i + 1) * 64],
                                                in1=dlam.ap[:, (2 * i + 1) * 64:(2 * i + 2) * 64], op=ALU.mult), [dlam], [lscr])
            op("dve", lambda e: e.tensor_reduce(out=lam2.ap[:, i:i + 1], in_=lscr.ap, axis=AX.X, op=ALU.add), [lscr], [lam2])
        lam2e = P.sb([128, 2], F32, "lam2e")
        op("act", lambda e: e.activation(out=lam2e.ap, in_=lam2.ap, func=AF.Exp), [lam2], [lam2e])
        nlam = P.sb([128, 1], F32, "nlam")
        op("dve", lambda e: e.tensor_tensor(out=nlam.ap, in0=lam2e.ap[:, 1:2], in1=lam2e.ap[:, 0:1], op=ALU.subtract), [lam2e], [nlam])
        op("dve", lambda e: e.tensor_scalar(out=nlam.ap, in0=nlam.ap, scalar1=-LAM_INIT, scalar2=None, op0=ALU.add), [nlam], [nlam])
        P.free(dlam, lam2, lscr, lam2e)

        hgain = P.sb([128, D], F32, "hgain")
        P.dma("sp", hgain.ap, hgain_d.partition_broadcast(128), writes=[hgain])
        op("dve", lambda e: e.tensor_scalar(out=hgain.ap[:, 512:1024], in0=hgain.ap[:, 512:1024], scalar1=1.0 - LAM_INIT,
                                            scalar2=None, op0=ALU.mult), [hgain], [hgain])

        selr = P.sb([2, 2, 128], F32, "selr")
        op("dve", lambda e: e.memset(selr.ap[:, 0, :], 0.0), [], [selr])
        op("dve", lambda e: e.memset(selr.ap[0:1, 0, :], 1.0), [], [selr])
        op("dve", lambda e: e.memset(selr.ap[:, 1, :], 1.0), [], [selr])
        op("dve", lambda e: e.memset(selr.ap[0:1, 1, :], 0.0), [], [selr])
        def proj_fm(wb, c0, m, ranges):
            outs = [PS.get() for _ in ranges]
            for kc in range(8):
                for ri, (tok0, ntok) in enumerate(ranges):
                    hb = hT2[tok0 // 512]
                    o = tok0 % 512
                    assert o + ntok <= 512
                    deps = [hTa[tok0 // 512], hTd[tok0 // 512]]
                    out_ps = outs[ri][0]
                    op("pe", lambda e: e.matmul(out_ps.ap[0:m, 0:ntok], lhsT=wb.ap[:, kc, c0:c0 + m], rhs=hb.ap[:, kc, o:o + ntok],
                                                start=(kc == 0), stop=(kc == 7)), [wb] + deps, [out_ps], inc=(kc == 7))
            return outs

        def proj_tm(wb, c0, n, T, out_ps):
            hb = hT2[T // 4]
            o = (T % 4) * 128
            for kc in range(8):
                op("pe", lambda e: e.matmul(out_ps.ap[:, 0:n], lhsT=hb.ap[:, kc, o:o + 128],
                                            rhs=wb.ap[:, kc, c0:c0 + n], start=(kc == 0), stop=(kc == 7)),
                   [wb, hTa[T // 4], hTd[T // 4]], [out_ps], inc=(kc == 7))

        evac_ctr = [0]

        def evac(out_t, out_ap, in_ap, in_t, scale=None):
            evac_ctr[0] += 1
            if evac_ctr[0] % 2 == 0:
                if scale is None:
                    op("act", lambda e: e.copy(out=out_ap, in_=in_ap), [in_t], [out_t])
                else:
                    op("act", lambda e: e.mul(out_ap, in_ap, scale), [in_t], [out_t])
            else:
                if scale is None:
                    op("dve", lambda e: e.tensor_copy(out=out_ap, in_=in_ap), [in_t], [out_t])
                else:
                    op("dve", lambda e: e.tensor_scalar(out=out_ap, in0=in_ap, scalar1=scale, scalar2=None, op0=ALU.mult),
                       [in_t], [out_t])

        OWN_RANGES = [(0, 512), (512, 256)]
        ALL_RANGES = [(0, 512), (512, 512), (1024, 512)]

        lrT = P.sb([17, 2, NTOK], BF16, "lrT")
        op("dve", lambda e: e.memset(lrT.ap, 1.0), [], [lrT])
        wal = P.sb([17, 2, 256], BF16, "wal")
        P.dma("pool", wal.ap, wal_d, writes=[wal])

        ckpt(1.5)
        gqT = [P.sb([128, 768], BF16, "gqT%d" % j) for j in range(2)]
        gkT = [P.sb([128, 768], BF16, "gkT%d" % j) for j in range(2)]
        gk_tm = [P.sb([128, 256], F32, "gktm%d" % t) for t in range(NTILE)]
        wb = wAB
        P.dma("pool", wb.ap[:, :, 0:512], win_d[:, 0:512].rearrange("(k p) n -> p k n", p=128), writes=[wb])
        P.dma("pool", wb.ap[:, :, 512:544], wlr_d.rearrange("p (k n) -> p k n", k=8), writes=[wb])
        gate_wb = [load_w(wmod_d, nb * 512, 512) for nb in (4, 5)]
        lr_tm = [P.sb([128, 32], BF16, "lrtm%d" % t) for t in range(NTILE)]
        gkb = [P.sb([128, 256], BF16, "gkb%d" % i) for i in range(3)]
        pend_tr = []
        pend_lr = []
        def fm_gq():
            for j in range(2):
                outs = proj_fm(wb, j * 128, 128, OWN_RANGES)
                for (ps, pid), (t0, nt) in zip(outs, OWN_RANGES):
                    evac(gqT[j], gqT[j].ap[:, t0:t0 + nt], ps.ap[:, 0:nt], ps, scale=SCALE_GLA)
                    PS.put(pid)

        for T in range(NTILE):
            if T == 6:
                fm_gq()
            ps, pid = PS.get()
            proj_tm(wb, 256, 288, T, ps)
            evac(gk_tm[T], gk_tm[T].ap, ps.ap[:, 0:256], ps)
            evac(lr_tm[T], lr_tm[T].ap, ps.ap[:, 256:288], ps)
            if T < 6:
                kb_ = gkb[T % 3]
                evac(kb_, kb_.ap, ps.ap[:, 0:256], ps)

                def do_tr(T=T, kb_=kb_):
                    pst, ptid = PS.get()
                    for j in range(2):
                        op("pe", lambda e: e.matmul(pst.ap[:, j * 128:(j + 1) * 128], lhsT=kb_.ap[:, j * 128:(j + 1) * 128],
                                                    rhs=constb.ap[:, C_ID:C_ID + 128], start=True, stop=True), [kb_, constb], [pst], inc=(j == 1))
                    for j in range(2):
                        evac(gkT[j], gkT[j].ap[:, T * 128:(T + 1) * 128], pst.ap[:, j * 128:(j + 1) * 128], pst)
                    PS.put(ptid)
                pend_tr.append(do_tr)
            PS.put(pid)
            if len(pend_tr) > 2 or (T >= 6 and pend_tr):
                pend_tr.pop(0)()
            if T % 4 == 3:
                def do_lr(T=T):
                    for d in range(2):
                        psl, pl = PS.get()
                        for t4 in range(4):
                            Tt = T - 3 + t4
                            op("pe", lambda e: e.matmul(psl.ap[0:16, t4 * 128:(t4 + 1) * 128], lhsT=lr_tm[Tt].ap[:, d * 16:(d + 1) * 16],
                                                        rhs=constb.ap[:, C_ID:C_ID + 128], start=True, stop=True), [lr_tm[Tt], constb], [psl],
                               inc=(t4 == 3))
                        evac(lrT, lrT.ap[0:16, d, (T - 3) * 128:(T + 1) * 128], psl.ap[0:16, 0:512], psl)
                        PS.put(pl)
                pend_lr.append([do_lr, 2])
            for it in list(pend_lr):
                it[1] -= 1
                if it[1] < 0 or T == NTILE - 1:
                    pend_lr.remove(it)
                    it[0]()
        P.free(wb, *lr_tm, *gkb)

        gv_tm = [P.sb([128, 512], BF16, "gvtm%d" % t) for t in range(NTILE)]
        wb = load_w(win_d, 512, 512)
        for _ in gate_rows():
            pass
        for T in range(NTILE):
            ps, pid = PS.get()
            proj_tm(wb, 0, 512, T, ps)
            evac(gv_tm[T], gv_tm[T].ap, ps.ap, ps)
            PS.put(pid)

        ckpt(1.7)
        rcos = P.sb([128, 1024], F32, "rcos")
        rsin = P.sb([128, 1024], F32, "rsin")
        P.dma("sp", rcos.ap, cos_d, writes=[rcos])
        P.dma("sp", rsin.ap, sin_d, writes=[rsin])
        NR = 4
        rx = [P.sb([128, 512], BF16, "rx%d" % i) for i in range(NR)]
        rt1 = [P.sb([128, 512], F32, "rt1_%d" % i) for i in range(NR)]
        rt2 = [P.sb([128, 512], F32, "rt2_%d" % i) for i in range(2)]
        rctr = [0]
        rbctr = [0]
        rope_pend = []

        def evac_rope(out_t, out_ap, ps, c0, nt):
            i = rctr[0] % NR
            rctr[0] += 1
            op("act", lambda e: e.copy(out=rx[i].ap[:, 0:nt], in_=ps.ap[:, 0:nt]), [ps], [rx[i]])
            op("dve", lambda e: e.tensor_tensor(out=rt1[i].ap[:, 0:nt], in0=ps.ap[:, 0:nt], in1=rcos.ap[:, c0:c0 + nt], op=ALU.mult),
               [ps, rcos], [rt1[i]])

            def part_b():
                k = rbctr[0] % 2
                rbctr[0] += 1
                ps2, pid2 = PS.get()
                op("pe", lambda e: e.matmul(ps2.ap[:, 0:nt], lhsT=constb.ap[:, C_R:C_R + 128], rhs=rx[i].ap[:, 0:nt], start=True, stop=True),
                   [constb, rx[i]], [ps2])
                op("dve", lambda e: e.tensor_tensor(out=rt2[k].ap[:, 0:nt], in0=ps2.ap[:, 0:nt], in1=rsin.ap[:, c0:c0 + nt], op=ALU.mult),
                   [ps2, rsin], [rt2[k]])
                PS.put(pid2)
                op("dve", lambda e: e.tensor_tensor(out=out_ap, in0=rt1[i].ap[:, 0:nt], in1=rt2[k].ap[:, 0:nt], op=ALU.add), [rt1[i], rt2[k]], [out_t])
            rope_pend.append(part_b)

        def rope_flush():
            while rope_pend:
                rope_pend.pop(0)()

        dqT = [P.sb([128, 768], BF16, "dqT%d" % h) for h in range(4)]
        wb = load_w(win_d, 1056, 512)
        for h in range(4):
            outs = proj_fm(wb, h * 128, 128, OWN_RANGES)
            rope_flush()
            for (ps, pid), (t0, nt) in zip(outs, OWN_RANGES):
                if t0 < 512:
                    evac(dqT[h], dqT[h].ap[:, t0:t0 + nt], ps.ap[:, 0:nt], ps)
                else:
                    evac_rope(dqT[h], dqT[h].ap[:, t0:t0 + nt], ps, t0 - 512, nt)
                PS.put(pid)

        for _ in gate_bcast():
            pass
        ckpt(1.75)
        dkT = [P.sb([128, NTOK], BF16, "dkT%d" % h) for h in range(4)]
        ostage = [P.sb([128, 512], F32, "ostage%d" % i) for i in range(3)]
        octr = [0]
        wb = load_w(win_d, 1568, 512)
        SAMPLE_RANGES = [(512, 512), (1024, 512)]
        for h in range(4):
            outs = proj_fm(wb, h * 128, 128, SAMPLE_RANGES)
            rope_flush()
            for (ps, pid), (t0, nt) in zip(outs, SAMPLE_RANGES):
                evac_rope(dkT[h], dkT[h].ap[:, t0:t0 + nt], ps, t0 - 512, nt)
                PS.put(pid)
        dkb = [P.sb([128, 512], BF16, "dkb%d" % i) for i in range(4)]
        pend_trk = []
        for T in range(4):
            ps, pid = PS.get()
            proj_tm(wb, 0, 512, T, ps)
            rope_flush()
            st = ostage[octr[0] % 3]
            octr[0] += 1
            evac(st, st.ap, ps.ap, ps)
            kb_ = dkb[T]
            evac(kb_, kb_.ap, ps.ap, ps)
            PS.put(pid)
            P.dma("sp", ndk_d[T * 128:(T + 1) * 128, :], st.ap, reads=[st])

            def do_trk(T=T, kb_=kb_):
                pst, ptid = PS.get()
                for h in range(4):
                    op("pe", lambda e: e.matmul(pst.ap[:, h * 128:(h + 1) * 128], lhsT=kb_.ap[:, h * 128:(h + 1) * 128],
                                                rhs=constb.ap[:, C_ID:C_ID + 128], start=True, stop=True), [kb_, constb], [pst], inc=(h == 3))
                for h in range(4):
                    evac(dkT[h], dkT[h].ap[:, T * 128:(T + 1) * 128], pst.ap[:, h * 128:(h + 1) * 128], pst)
                PS.put(ptid)
            pend_trk.append(do_trk)

        ckpt(1.8)
        dv1 = [P.sb([128, 4, 130], BF16, "dv1_%d" % t) for t in range(NTILE)]
        wb = load_w(win_d, 2080, 512)
        for T in range(NTILE):
            if T >= 2 and pend_trk:
                pend_trk.pop(0)()
            op("dve", lambda e: e.memset(dv1[T].ap[:, :, 128:130], 1.0), [], [dv1[T]])
            ps, pid = PS.get()
            proj_tm(wb, 0, 512, T, ps)
            evac(dv1[T], dv1[T].ap[:, :, 0:128], ps.ap.rearrange("p (h v) -> p h v", h=4), ps)
            if T < 4:
                st = ostage[octr[0] % 3]
                octr[0] += 1
                evac(st, st.ap, ps.ap, ps)
                P.dma("sp", ndv_d[T * 128:(T + 1) * 128, :], st.ap, reads=[st])
            PS.put(pid)

        assert not pend_trk
        P.free(*dkb)

        G = [P.sb([128, D], F32, "G%d" % t) for t in range(6)]
        gsl = [P.sb([128, 512], F32, "gsl%d" % i) for i in range(2)]
        for nb in range(2):
            wb = load_w(win_d, 2592 + nb * 512, 512)
            for T in range(6):
                ps, pid = PS.get()
                proj_tm(wb, 0, 512, T, ps)
                g1 = gsl[T % 2]
                op("act", lambda e: e.activation(out=g1.ap, in_=ps.ap, func=AF.Silu), [ps], [g1])
                PS.put(pid)
                op("dve", lambda e: e.tensor_tensor(out=G[T].ap[:, nb * 512:(nb + 1) * 512], in0=g1.ap,
                                                    in1=hgain.ap[:, nb * 512:(nb + 1) * 512], op=ALU.mult), [g1, hgain], [G[T]])
        ckpt(2)
        for b in range(3):
            for s_ in (hTa[b].tr, hTd[b].tr):
                for k, v in list(s_.r.items()) + ([s_.w] if s_.w is not None else []):
                    hT2[b].tr.r[k] = max(hT2[b].tr.r.get(k, 0), v)
        P.free(*hT2)
        P.free(rcos, rsin, *rx, *rt1, *rt2, gsl[0], gsl[1], hgain, ostage[0], ostage[1], ostage[2], modT)

        P.free(*wbufs)

        ckpt(2.5)
        NSLOT = 6
        oT = [P.sb([128, 8, 128], BF16, "oT%d" % t) for t in range(6)]
        sel = P.sb([128, 8], F32, "sel")
        P.dma("sp", sel.ap, sel_d, writes=[sel])
        ktil = [[P.sb([128, 128], BF16, "ktil") for j in range(2)] for s in range(NSLOT)]
        S = {(s, j): P.sb([128, 128], F32, "S") for s in range(NSLOT) for j in range(2)}
        acc = {(s, j): P.sb([128, 128], F32, "acc") for s in range(2) for j in range(2)}
        e_tm = [P.sb([128, 256], F32, "etm") for s in range(NSLOT)]
        l_tm = [P.sb([128, 256], BF16, "ltm") for s in range(NSLOT)]
        ew = [P.sb([128, 256], F32, "ew") for s in range(NSLOT)]
        khat = [P.sb([128, 256], BF16, "khat") for s in range(NSLOT)]
        Dt = [P.sb([128, 4], F32, "Dt") for s in range(NSLOT)]
        EB = [P.sb([128, 2, 128], F32, "EB") for s in range(NSLOT)]
        EBn = [P.sb([128, 2, 128], F32, "EBn") for s in range(NSLOT)]
        qtil = {(s, d, j): P.sb([128, 128], BF16, "qtil") for s in range(6) for d in range(2) for j in range(2)}
        attm = {(s, d): P.sb([128, 4, 128], BF16, "attm") for s in range(6) for d in range(2)}
        sbf = {(s, d, j, c): P.sb([128, 128], BF16, "sbf") for s in range(6) for d in range(2) for j in range(2) for c in range(2)}

        def gla_unit(u, T, d, own, chain_cb):
            sl = T
            if own:
                msk16 = C_MSF16 if d == 0 else C_MSB16
            else:
                msk16 = C_MSF16F if d == 0 else C_MSB16F
            l16 = C_LF16 if d == 0 else C_LB16
            l01 = C_LF01 if d == 0 else C_LB01
            (psz, pz), = yield from get_banks(1)
            op("pe", lambda e: e.matmul(psz.ap[:, 0:256], lhsT=lrT.ap[0:17, d, T * 128:(T + 1) * 128], rhs=wal.ap[0:17, d, :],
                                        start=True, stop=True), [lrT, wal], [psz])
            yield
            op("act", lambda e: e.activation(out=e_tm[u].ap, in_=psz.ap[:, 0:256], func=AF.Exp, scale=-1.0), [psz], [e_tm[u]])
            PS.put(pz)
            op("act", lambda e: e.activation(out=l_tm[u].ap, in_=e_tm[u].ap, func=AF.Ln, bias=1.0, scale=1.0), [e_tm[u]], [l_tm[u]])
            yield
            (psw, pw), = yield from get_banks(1)
            op("pe", lambda e: e.matmul(psw.ap[:, 0:256], lhsT=constb.ap[:, msk16:msk16 + 128], rhs=l_tm[u].ap, start=True, stop=True),
               [constb, l_tm[u]], [psw])
            if not own:
                for j in range(2):
                    op("pe", lambda e: e.matmul(psw.ap[:, 256 + j:257 + j], lhsT=l_tm[u].ap[:, j * 128:(j + 1) * 128],
                                                rhs=constb.ap[:, C_ONE16:C_ONE16 + 1], start=True, stop=True), [constb, l_tm[u]], [psw])
            if own:
                (psb, pb), = yield from get_banks(1)
                for j in range(2):
                    op("pe", lambda e: e.matmul(psb.ap[:, j * 128:(j + 1) * 128], lhsT=l_tm[u].ap[:, j * 128:(j + 1) * 128],
                                                rhs=constb.ap[:, l16:l16 + 128], start=True, stop=True), [constb, l_tm[u]], [psb])
            yield
            op("act", lambda e: e.activation(out=ew[u].ap, in_=psw.ap[:, 0:256], func=AF.Exp), [psw], [ew[u]])
            if not own:
                op("act", lambda e: e.activation(out=Dt[u].ap[:, 0:2], in_=psw.ap[:, 256:258], func=AF.Exp), [psw], [Dt[u]])
            PS.put(pw)
            if own:
                op("act", lambda e: e.activation(out=EB[u].ap.rearrange("p a b -> p (a b)"), in_=psb.ap[:, 0:256], func=AF.Exp),
                   [psb], [EB[u]])
                op("act", lambda e: e.activation(out=EBn[u].ap.rearrange("p a b -> p (a b)"), in_=psb.ap[:, 0:256], func=AF.Exp, scale=-1.0),
                   [psb], [EBn[u]])
                PS.put(pb)
            yield
            op(KH_ENG, lambda e: e.tensor_tensor(out=khat[u].ap, in0=gk_tm[T].ap, in1=ew[u].ap, op=ALU.mult),
               [gk_tm[T], ew[u]], [khat[u]])
            if own:
                for j in range(2):
                    qt = qtil[(sl, d, j)]
                    kt = ktil[u][j]
                    op(QK_ENG, lambda e: e.tensor_tensor(out=qt.ap, in0=gqT[j].ap[:, T * 128:(T + 1) * 128], in1=EB[u].ap[:, j, :], op=ALU.mult),
                       [gqT[j], EB[u]], [qt])
                    op(QK_ENG, lambda e: e.tensor_tensor(out=kt.ap, in0=gkT[j].ap[:, T * 128:(T + 1) * 128], in1=EBn[u].ap[:, j, :], op=ALU.mult),
                       [gkT[j], EBn[u]], [kt])
            yield
            if not own:
                (psA0, pA0), = yield from get_banks(1)
                for j in range(2):
                    for hp in range(2):
                        h = 2 * j + hp
                        op("pe", lambda e: e.matmul(
                            psA0.ap[hp * 64:(hp + 1) * 64, j * 128:(j + 1) * 128], lhsT=khat[u].ap[:, h * 64:(h + 1) * 64],
                            rhs=gv_tm[T].ap[:, h * 128:(h + 1) * 128], start=True, stop=True), [khat[u], gv_tm[T]], [psA0])
                yield
                for j in range(2):
                    chain_cb(0, j, psA0.ap[:, j * 128:(j + 1) * 128], Dt[u].ap[:, j:j + 1], psA0, Dt[u])
                PS.put(pA0)
                yield
                return
            (psA0, pA0), (psA1, pA1) = yield from get_banks(2)
            psA = (psA0, psA1)
            for j in range(2):
                for c in range(2):
                    for hp in range(2):
                        h = 2 * j + hp
                        op("pe", lambda e: e.matmul(
                            psA[c].ap[hp * 64:(hp + 1) * 64, j * 128:(j + 1) * 128], lhsT=khat[u].ap[c * 64:(c + 1) * 64, h * 64:(h + 1) * 64],
                            rhs=gv_tm[T].ap[c * 64:(c + 1) * 64, h * 128:(h + 1) * 128], start=True, stop=True),
                            [khat[u], gv_tm[T]], [psA[c]])
            yield
            corder = (0, 1) if d == 0 else (1, 0)
            for c in corder:
                for j in range(2):
                    if own:
                        col = c * 64 + (63 if d == 0 else 0)
                        chain_cb(c, j, psA[c].ap[:, j * 128:(j + 1) * 128], EB[u].ap[:, j, col:col + 1], psA[c], EB[u])
                    else:
                        chain_cb(c, j, psA[c].ap[:, j * 128:(j + 1) * 128], Dt[u].ap[:, j * 2 + c:j * 2 + c + 1], psA[c], Dt[u])
            PS.put(pA0)
            PS.put(pA1)
            yield
            if own:
                (psa0, pa0), (psa1, pa1) = yield from get_banks(2)
                psa = (psa0, psa1)
                for j in range(2):
                    for hp in range(2):
                        op("pe", lambda e: e.matmul(psa[hp].ap[:, j * 128:(j + 1) * 128], lhsT=ktil[u][j].ap[hp * 64:(hp + 1) * 64, :],
                                                    rhs=qtil[(sl, d, j)].ap[hp * 64:(hp + 1) * 64, :], start=True, stop=True),
                           [ktil[u][j], qtil[(sl, d, j)]], [psa[hp]])
                yield
                am = attm[(sl, d)]
                amv = am.ap.rearrange("p (j hp) t -> p hp j t", hp=2)
                for hp in range(2):
                    op("dve", lambda e: e.tensor_tensor(out=amv[:, hp], in0=psa[hp].ap[:, 0:256].rearrange("p (j t) -> p j t", j=2),
                                                        in1=constb.ap[:, l01:l01 + 128].unsqueeze(1).to_broadcast([128, 2, 128]), op=ALU.mult),
                       [psa[hp], constb], [am])
                PS.put(pa0)
                PS.put(pa1)
                yield

        def chain_step(s, j, A_ap, D_ap, psA, Dtile, save_to=None):
            st = S[(s, j)]
            if save_to is not None:
                if SV_ENG == "act":
                    op("act", lambda e: e.copy(out=save_to.ap, in_=st.ap), [st], [save_to])
                else:
                    op(SV_ENG, lambda e: e.tensor_copy(out=save_to.ap, in_=st.ap), [st], [save_to])
            op("dve", lambda e: e.scalar_tensor_tensor(out=st.ap, in0=st.ap, scalar=D_ap, in1=A_ap, op0=ALU.mult, op1=ALU.add),
               [st, psA, Dtile], [st])

        ssq = [P.sb([128, 4], F32, "ssq") for i in range(2)]
        rsh = [P.sb([128, 4], F32, "rsh") for i in range(2)]
        on1 = [P.sb([128, 4, 128], F32, "on1") for i in range(2)]
        junk2 = [junk, P.sb([128, 512], F32, "junk2")]
        ogb = [P.sb([128, 512], BF16, "ogb%d" % i) for i in range(2)]
        hctr = [0]

        def head_post(srcs, T, half, release=None, tset=None):
            if tset is None:
                i = hctr[0] % 2
                hctr[0] += 1
            else:
                i = tset
            jk = junk2[i]
            jv = jk.ap[:, 0:512].rearrange("p (h v) -> p h v", h=4)

            def hsel(ap3, heads):
                if len(heads) == 4:
                    return ap3
                return ap3.rearrange("p (j hp) v -> p hp j v", hp=2)[:, heads[0] % 2]

            for (st, sap, heads) in srcs:
                op("act", lambda e: e.activation(out=hsel(jv, heads), in_=sap, func=AF.Square), [st], [jk])
            yield
            op("dve", lambda e: e.tensor_reduce(out=ssq[i].ap, in_=jv, axis=AX.X, op=ALU.add), [jk], [ssq[i]])
            yield
            op("act", lambda e: e.activation(out=ssq[i].ap, in_=ssq[i].ap, func=AF.Ln, bias=epsb.ap, scale=1.0 / 128), [ssq[i], epsb], [ssq[i]])
            op("act", lambda e: e.activation(out=rsh[i].ap, in_=ssq[i].ap, func=AF.Exp, scale=-0.5), [ssq[i]], [rsh[i]])
            yield
            for (st, sap, heads) in srcs:
                nh = len(heads)
                rv = rsh[i].ap if nh == 4 else rsh[i].ap.rearrange("p (j hp) -> p hp j", hp=2)[:, heads[0] % 2]
                op("dve", lambda e: e.tensor_tensor(out=hsel(on1[i].ap, heads), in0=sap, in1=rv.unsqueeze(2).to_broadcast([128, nh, 128]), op=ALU.mult),
                   [st, rsh[i]], [on1[i]])
            if release is not None:
                release()
            op(OG_ENG, lambda e: e.tensor_tensor(out=ogb[i].ap, in0=on1[i].ap.rearrange("p h v -> p (h v)"),
                                                in1=G[T].ap[:, half * 512:(half + 1) * 512], op=ALU.mult), [on1[i], G[T]], [ogb[i]])
            yield
            yield
            yield
            (pst, pt), = yield from get_banks(1)
            for h in range(4):
                op("pe", lambda e: e.matmul(pst.ap[:, h * 128:(h + 1) * 128], lhsT=ogb[i].ap[:, h * 128:(h + 1) * 128],
                                            rhs=constb.ap[:, C_ID:C_ID + 128], start=True, stop=True), [ogb[i], constb], [pst])
            yield
            op("act", lambda e: e.copy(out=oT[T].ap[:, half * 4:(half + 1) * 4, :], in_=pst.ap.rearrange("p (h t) -> p h t", h=4)),
               [pst], [oT[T]])
            PS.put(pt)

        def gla_out(T, tset=None):
            sl = T
            (pso0, po0), (pso1, po1) = yield from get_banks(2)
            pso = (pso0, pso1)
            for h in range(4):
                j, hp = h // 2, h % 2
                ob = pso[hp]
                mms = []
                for d in range(2):
                    mms.append((ob.ap[:, j * 128:(j + 1) * 128], attm[(sl, d)].ap[:, h, :], gv_tm[T].ap[:, h * 128:(h + 1) * 128],
                                [attm[(sl, d)], gv_tm[T]]))
                for d in range(2):
                    for c in range(2):
                        mms.append((ob.ap[c * 64:(c + 1) * 64, j * 128:(j + 1) * 128],
                                    qtil[(sl, d, j)].ap[hp * 64:(hp + 1) * 64, c * 64:(c + 1) * 64],
                                    sbf[(sl, d, j, c)].ap[hp * 64:(hp + 1) * 64, :], [qtil[(sl, d, j)], sbf[(sl, d, j, c)]]))
                for n, (o_ap, l_ap, r_ap, rd) in enumerate(mms):
                    op("pe", lambda e: e.matmul(o_ap, lhsT=l_ap, rhs=r_ap, start=(n == 0), stop=(n >= len(mms) - 2)), rd, [ob])
            yield

            def rel():
                PS.put(po0)
                PS.put(po1)
            yield from head_post([(pso[hp], pso[hp].ap[:, 0:256].rearrange("p (j v) -> p j v", j=2), [hp, hp + 2]) for hp in range(2)], T, 0, release=rel, tset=tset)

        def prompt_chain(slot, sq_i, d):
            tiles = (2 * sq_i, 2 * sq_i + 1)
            for j in range(2):
                op("dve", lambda e: e.memset(S[(slot, j)].ap, 0.0), [], [S[(slot, j)]])
            order = tiles if d == 0 else tiles[::-1]
            for T in order:
                yield from gla_unit(slot, T, d, True,
                                    lambda c, j, A, Dp, psA, Dtile: chain_step(slot, j, A, Dp, psA, Dtile, save_to=sbf[(T, d, j, c)]))
            for j in range(2):
                P.dma("sp", ns_d[d, sq_i, j], S[(slot, j)].ap, reads=[S[(slot, j)]])


        def sample_chain(slot, d):
            for j in range(2):
                P.dma("sp", S[(slot, j)].ap, s0_d[d, j], writes=[S[(slot, j)]])
                op("dve", lambda e: e.memset(acc[(d, j)].ap, 0.0), [], [acc[(d, j)]])
            nchunks = [0]

            def sel_acc():
                i = nchunks[0] // 4
                for j in range(2):
                    op("dve", lambda e: e.scalar_tensor_tensor(out=acc[(d, j)].ap, in0=S[(slot, j)].ap, scalar=sel.ap[:, d * 4 + i:d * 4 + i + 1],
                                                               in1=acc[(d, j)].ap, op0=ALU.mult, op1=ALU.add),
                       [S[(slot, j)], sel, acc[(d, j)]], [acc[(d, j)]])

            def cb_others(c, j, A, Dp, psA, Dtile):
                if j == 0 and nchunks[0] % 4 == 0:
                    sel_acc()
                chain_step(slot, j, A, Dp, psA, Dtile)
                if j == 1:
                    nchunks[0] += 2

            order = list(range(6, 12)) if d == 0 else list(range(11, 5, -1))
            for T in order:
                yield from gla_unit(slot, T, d, False, cb_others)
            assert nchunks[0] == 12
            sel_acc()
            for j in range(2):
                op("dve", lambda e: e.tensor_copy(out=S[(slot, j)].ap, in_=acc[(d, j)].ap), [acc[(d, j)]], [S[(slot, j)]])
            order = (4, 5) if d == 0 else (5, 4)
            for T in order:
                yield from gla_unit(slot, T, d, True,
                                    lambda c, j, A, Dp, psA, Dtile: chain_step(slot, j, A, Dp, psA, Dtile, save_to=sbf[(T, d, j, c)]))

        def prompt_outs():
            for T in range(4):
                yield from gla_out(T)

        def seq_stream(sq_i, slot0):
            subs = [prompt_chain(slot0, sq_i, 0), prompt_chain(slot0 + 1, sq_i, 1)]
            while subs:
                for g in list(subs):
                    try:
                        next(g)
                    except StopIteration:
                        subs.remove(g)
                yield
            for T in (2 * sq_i, 2 * sq_i + 1):
                yield from gla_out(T)

        def sample_stream():
            subs = [sample_chain(4, 0), sample_chain(5, 1)]
            while subs:
                for g in list(subs):
                    try:
                        next(g)
                    except StopIteration:
                        subs.remove(g)
                yield

        run_streams([seq_stream(0, 0), seq_stream(1, 2), sample_stream()])

        keep45 = [gv_tm[4], gv_tm[5]] + [v for k, v in list(qtil.items()) + list(attm.items()) + list(sbf.items()) if k[0] in (4, 5)]
        keep45_ids = set(id(t) for t in keep45)
        P.free(*[t for t in list(gqT) + list(gkT) + list(gk_tm) + list(gv_tm) + [lrT, wal] if id(t) not in keep45_ids], *[k for kk in ktil for k in kk], *S.values(), *acc.values(), sel, *e_tm, *l_tm, *ew, *khat,
               *Dt, *EB, *EBn, *[t for t in list(qtil.values()) + list(attm.values()) + list(sbf.values()) if id(t) not in keep45_ids])
        ckpt(3)
        ssq.append(P.sb([128, 4], F32, "ssq"))
        rsh.append(P.sb([128, 4], F32, "rsh"))
        on1.append(P.sb([128, 4, 128], F32, "on1"))
        junk2.append(P.sb([128, 512], F32, "junk2b"))
        ogb.append(P.sb([128, 512], BF16, "ogb2"))

        ckT = [P.sb([128, 512], BF16, "ckT%d" % h) for h in range(4)]
        for h in range(4):
            P.dma("pool", ckT[h].ap, ckT_d[h], writes=[ckT[h]])
        cv1 = [P.sb([128, 4, 130], BF16, "cv1_%d" % t) for t in range(4)]
        for t in range(4):
            op("dve", lambda e: e.memset(cv1[t].ap[:, :, 128:130], 1.0), [], [cv1[t]])
            P.dma("pool", cv1[t].ap[:, :, 0:128], cv_d[t * 128:(t + 1) * 128, :].rearrange("p (h v) -> p h v", h=4), writes=[cv1[t]])
        woutb = P.sb([128, 8, D], BF16, "woutb")
        for nb in range(2):
            P.dma("pool", woutb.ap[:, :, nb * 512:(nb + 1) * 512],
                  wout_d[:, nb * 512:(nb + 1) * 512].rearrange("(k p) n -> p k n", p=128), writes=[woutb])

        Ebuf = [P.sb([128, 2, 256], BF16, "E%d" % i) for i in range(10)]
        ectr = [0]
        od = [[P.sb([128, 4, 128], F32, "od") for qb in range(2)] for i in range(3)]
        junk3 = P.sb([128, D], F32, "junk3")
        rr = [P.sb([128, 4], F32, "rr") for i in range(2)]
        t1 = [P.sb([128, 128], F32, "t1") for i in range(2)]
        fgain = P.sb([128, D], F32, "fgain")
        P.dma("sp", fgain.ap, fgain_d.partition_broadcast(128), writes=[fgain])
        xt = [P.sb([128, D], F32, "xt%d" % i) for i in range(2)]
        xn = [P.sb([128, D], F32, "xn%d" % i) for i in range(2)]
        fss = [P.sb([128, 1], F32, "fss") for i in range(2)]
        frs = [P.sb([128, 1], F32, "frs") for i in range(2)]

        def final_tile(T):
            x_t = xt[T % 2]
            x_n = xn[T % 2]
            f_s, f_r = fss[T % 2], frs[T % 2]
            ii = 0 if T < 4 else 1
            P.dma("sp", x_t.ap, xtm_d[T * 128:(T + 1) * 128, :], writes=[x_t])
            for _ in range(3):
                yield
            for nb in range(2):
                (ps, pid), = yield from get_banks(1)
                for mc in range(8):
                    op("pe", lambda e: e.matmul(ps.ap, lhsT=oT[T].ap[:, mc, :], rhs=woutb.ap[:, mc, nb * 512:(nb + 1) * 512],
                                                start=(mc == 0), stop=(mc == 7)), [oT[T], woutb], [ps], inc=(mc == 7))
                yield
                op("dve", lambda e: e.tensor_tensor(out=x_n.ap[:, nb * 512:(nb + 1) * 512], in0=ps.ap, in1=gt_bc[ii].ap[:, nb * 512:(nb + 1) * 512], op=ALU.mult),
                   [ps, gt_bc[ii]], [x_n])
                PS.put(pid)
            op("dve", lambda e: e.tensor_tensor(out=x_n.ap, in0=x_n.ap, in1=x_t.ap, op=ALU.add), [x_n, x_t], [x_n])
            yield
            op("act", lambda e: e.activation(out=junk3.ap, in_=x_n.ap, func=AF.Square, accum_out=f_s.ap), [x_n], [junk3, f_s])
            yield
            op("act", lambda e: e.activation(out=f_s.ap, in_=f_s.ap, func=AF.Ln, bias=epsb.ap, scale=1.0 / D), [f_s, epsb], [f_s])
            op("act", lambda e: e.activation(out=f_r.ap, in_=f_s.ap, func=AF.Exp, scale=-0.5), [f_s], [f_r])
            yield
            op("dve", lambda e: e.scalar_tensor_tensor(out=x_n.ap, in0=x_n.ap, scalar=f_r.ap[:, 0:1], in1=fgain.ap, op0=ALU.mult, op1=ALU.mult),
               [x_n, f_r, fgain], [x_n])
            dst = yp_d[T * 128:(T + 1) * 128, :] if T < 4 else ys_d[(T - 4) * 128:(T - 3) * 128, :]
            P.dma("sp", dst, x_n.ap, reads=[x_n])

        ctxs = []
        for sq_i in range(2):
            ctxs.append((sq_i * 256, [("d", T) for T in (2 * sq_i, 2 * sq_i + 1)], (2 * sq_i, 2 * sq_i + 1)))
        ctxs.append((512, [("c", t) for t in range(4)] + [("d", T) for T in range(4, 12)], (4, 5)))

        def kblock(h, kind, t):
            if kind == "c":
                return (ckT[h].ap[:, t * 128:(t + 1) * 128], ckT[h], cv1[t].ap[:, h, 0:129], cv1[t])
            return (dkT[h].ap[:, t * 128:(t + 1) * 128], dkT[h], dv1[t].ap[:, h, 0:129], dv1[t])

        def attention_all():
            jobs = []
            for ci, (q0, kbs, tiles) in enumerate(ctxs):
                for h in range(4):
                    for p in range(len(kbs) // 2):
                        jobs.append((ci, h, p))
            Es = {}

            def qk(job):
                ci, h, p = job
                q0, kbs, tiles = ctxs[ci]
                q_ap = dqT[h].ap[:, q0:q0 + 256]
                Em = []
                for m in range(2):
                    (pss, psid), = yield from get_banks(1)
                    for i in range(2):
                        k_ap, k_t, _, _ = kblock(h, *kbs[2 * p + i])
                        op("pe", lambda e: e.matmul(pss.ap[:, i * 256:(i + 1) * 256], lhsT=k_ap[m * 64:(m + 1) * 64, :],
                                                    rhs=q_ap[m * 64:(m + 1) * 64, :], start=True, stop=True), [k_t, dqT[h]], [pss], inc=(i == 1))
                    E = Ebuf[ectr[0] % len(Ebuf)]
                    ectr[0] += 1
                    op("act", lambda e: e.activation(out=E.ap.rearrange("p i q -> p (i q)"), in_=pss.ap, func=AF.Exp, scale=SCALE_ATT),
                       [pss], [E])
                    PS.put(psid)
                    Em.append(E)
                Es[job] = Em

            obanks = None
            DEPTH = 3
            for k in range(min(DEPTH, len(jobs))):
                yield from qk(jobs[k])
            for n, job in enumerate(jobs):
                ci, h, p = job
                q0, kbs, tiles = ctxs[ci]
                nkb = len(kbs)
                if n + DEPTH < len(jobs):
                    yield from qk(jobs[n + DEPTH])
                if p == 0:
                    obanks = yield from get_banks(2)
                for i in range(2):
                    kb = 2 * p + i
                    _, _, v_ap, v_t = kblock(h, *kbs[kb])
                    for qb in range(2):
                        for m in range(2):
                            ob, _ = obanks[qb]
                            E = Es[job][m]
                            op("pe", lambda e: e.matmul(ob.ap[:, m * 129:(m + 1) * 129], lhsT=E.ap[:, i, qb * 128:(qb + 1) * 128], rhs=v_ap,
                                                        start=(kb == 0 and m == 0), stop=(kb == nkb - 1 and m == 1), skip_group_check=True),
                               [E, v_t], [ob], inc=(kb == nkb - 1 or (i == 1 and qb == 1 and m == 1)))
                del Es[job]
                yield
                if p == nkb // 2 - 1:
                    oo = od[ci]
                    osb = osbs[nctr[0] % 2]
                    nctr[0] += 1
                    for qb in range(2):
                        ob, idb = obanks[qb]
                        op("act", lambda e: e.copy(out=osb.ap[:, qb * 2:qb * 2 + 2, 0:129], in_=ob.ap[:, 0:258].rearrange("p (m c) -> p m c", m=2)),
                           [ob], [osb])
                        PS.put(idb)
                    for qb in range(2):
                        r_, t_ = rr[qb], t1[qb]
                        op("dve", lambda e: e.reciprocal(out=r_.ap[:, 0:1], in_=osb.ap[:, qb * 2, 128:129]), [osb], [r_])
                        op("dve", lambda e: e.reciprocal(out=r_.ap[:, 1:2], in_=osb.ap[:, qb * 2 + 1, 128:129]), [osb], [r_])
                        op("dve", lambda e: e.tensor_tensor(out=r_.ap[:, 2:3], in0=r_.ap[:, 1:2], in1=nlam.ap, op=ALU.mult), [r_, nlam], [r_])
                        op("dve", lambda e: e.tensor_scalar(out=t_.ap, in0=osb.ap[:, qb * 2, 0:128], scalar1=r_.ap[:, 0:1], scalar2=None, op0=ALU.mult),
                           [osb, r_], [t_])
                        op("dve", lambda e: e.scalar_tensor_tensor(out=oo[qb].ap[:, h, :], in0=osb.ap[:, qb * 2 + 1, 0:128], scalar=r_.ap[:, 2:3], in1=t_.ap,
                                                                   op0=ALU.mult, op1=ALU.add), [osb, r_, t_], [oo[qb]])
                    if h == 3:
                        for qb in range(2):
                            post_q.append((oo[qb], tiles[qb]))
                    yield

        post_q = []
        osbs = [P.sb([128, 4, 130], F32, "osb%d" % i) for i in range(2)]
        nctr = [0]

        def post_stream():
            def one_tile(o_t, T):
                yield from head_post([(o_t, o_t.ap, [0, 1, 2, 3])], T, 1, tset=T % 2)
                while T >= 4 and T not in gla_done:
                    yield
                yield from final_tile(T)

            done = 0
            while done < 6:
                if not post_q:
                    yield
                    continue
                subs = []
                while post_q and len(subs) < 2:
                    o_t, T = post_q.pop(0)
                    subs.append(one_tile(o_t, T))
                    done += 1
                while subs:
                    for g in list(subs):
                        try:
                            next(g)
                        except StopIteration:
                            subs.remove(g)
                    yield

        gla_done = set()

        def gla45():
            for T in (4, 5):
                yield from gla_out(T, tset=2)
                gla_done.add(T)

        run_streams([attention_all(), gla45(), post_stream()])
        P.free(*keep45)

        P.finish("sp")
        print("[build] instructions=%d waits=%d sbuf_peak=%d" % (P.n_ins, P.n_wait, P.peak))
    except _Stop:
        pass
    return nc


def _consts():
    c = np.zeros((128, NCONST), np.float32)
    s = np.arange(128)[:, None]
    t = np.arange(128)[None, :]
    same = (s // 64) == (t // 64)
    c[:, C_LF01:C_LF01 + 128] = (same & (s <= t))
    c[:, C_LB01:C_LB01 + 128] = (same & (s >= t))
    c[:, C_LF16:C_LF16 + 128] = (same & (s <= t)) * (-1.0 / 16)
    c[:, C_LB16:C_LB16 + 128] = (same & (s >= t)) * (-1.0 / 16)
    c[:, C_MSF16:C_MSF16 + 128] = (same & (s > t)) * (-1.0 / 16)
    c[:, C_MSB16:C_MSB16 + 128] = (same & (s < t)) * (-1.0 / 16)
    c[:, C_ID:C_ID + 128] = np.eye(128)
    R = np.zeros((128, 128), np.float32)
    for p in range(128):
        d = p % 32
        if d < 16:
            R[p + 16, p] = -1.0
        else:
            R[p - 16, p] = 1.0
    c[:, C_R:C_R + 128] = R
    c[:, C_ONES:C_ONES + 128] = 1.0
    c[:, C_CI16] = (np.arange(128) < 64) * (-1.0 / 16)
    c[:, C_CI16 + 1] = (np.arange(128) >= 64) * (-1.0 / 16)
    c[:, C_MSF16F:C_MSF16F + 128] = (s > t) * (-1.0 / 16)
    c[:, C_MSB16F:C_MSB16F + 128] = (s < t) * (-1.0 / 16)
    c[:, C_ONE16] = -1.0 / 16
    return c


def _rope_tables(tok_idx):
    row = (tok_idx // 64).astype(np.float32)
    col = (tok_idx % 64).astype(np.float32)
    inv_freq = (np.float32(10000.0) ** (-np.arange(16, dtype=np.float32) / np.float32(16))).astype(np.float32)
    cos = np.zeros((128, len(tok_idx)), np.float32)
    sin = np.zeros((128, len(tok_idx)), np.float32)
    for p in range(128):
        d = p % 64
        pos = row if d < 32 else col
        ang = (pos * inv_freq[d % 16]).astype(np.float32)
        cos[p] = np.cos(ang)
        sin[p] = np.sin(ang)
    return cos, sin


def make_in_maps(x_prompt, x_sample, cache_diff_k, cache_diff_v, state_gla_fwd, state_gla_bwd, c, c_ctx,
                 norm_gain, w_mod, b_mod, w_in, w_gla_alpha, b_gla_alpha, diff_lambda,
                 gla_head_gain, diff_head_gain, w_out, final_gain):
    f = lambda a: np.ascontiguousarray(np.asarray(a, dtype=np.float32))
    x_prompt, x_sample = f(x_prompt), f(x_sample)
    consts = _consts()
    wal = np.concatenate([np.transpose(f(w_gla_alpha)[0], (1, 0, 2)), f(b_gla_alpha)[0][None]], axis=0)
    hgain = np.concatenate([np.tile(f(gla_head_gain)[0], 4), np.tile(f(diff_head_gain)[0], 4)])
    shared = {
        "gainT": f(f(norm_gain)[0].reshape(8, 128).T),
        "wmod": f(w_mod)[0], "bmod": f(b_mod)[0], "win": f(w_in)[0], "wout": f(w_out)[0],
        "wlr": f(f(w_in)[0][:, 1024:1056].reshape(8, 128, 32).transpose(1, 0, 2).reshape(128, 256)),
        "walpha": f(wal), "dlam": f(diff_lambda)[0].reshape(256), "hgain": f(hgain), "fgain": f(final_gain),
        "consts": consts,
    }
    maps = []
    for j in range(N_CORES):
        b, r = j // 4, j % 4
        own = np.arange(r * 256, (r + 1) * 256)
        oth = np.concatenate([np.arange(0, r * 256), np.arange((r + 1) * 256, 1024)])
        tok = np.concatenate([own, oth])
        X = np.concatenate([x_prompt[2 * j], x_prompt[2 * j + 1], x_sample[b][tok]], axis=0)
        cv2 = np.stack([f(c_ctx), f(c)[b]], axis=0)
        cos, sin = _rope_tables(tok)
        sel = np.zeros((128, 8), np.float32)
        sel[:, r] = 1.0
        sel[:, 4 + (3 - r)] = 1.0
        m = dict(shared)
        m.update({
            "xT": f(X.T.reshape(8, 128, NTOK)),
            "xtm": f(X[:768]),
            "cvT": f(cv2.T.reshape(8, 128, 2).transpose(1, 0, 2)),
            "ckT": f(np.transpose(f(cache_diff_k)[b, 0].reshape(512, 4, 128), (1, 2, 0))),
            "cv": f(f(cache_diff_v)[b, 0].reshape(512, 512)),
            "s0": f(np.stack([f(state_gla_fwd)[b, 0].reshape(2, 128, 128), f(state_gla_bwd)[b, 0].reshape(2, 128, 128)], axis=0)),
            "sel": sel, "ropecos": cos, "ropesin": sin,
        })
        maps.append(m)
    return maps


def assemble(results):
    y_prompt = np.zeros((16, 256, D), np.float32)
    y_sample = np.zeros((2, 1024, D), np.float32)
    ndk = np.zeros((16, 1, 256, 4, 2, 64), np.float32)
    ndv = np.zeros((16, 1, 256, 4, 128), np.float32)
    nsf = np.zeros((16, 1, 4, 64, 128), np.float32)
    nsb = np.zeros((16, 1, 4, 64, 128), np.float32)
    for j, res in enumerate(results):
        b, r = j // 4, j % 4
        y_prompt[2 * j:2 * j + 2] = np.asarray(res["yp"]).reshape(2, 256, D)
        y_sample[b, r * 256:(r + 1) * 256] = np.asarray(res["ys"])
        ndk[2 * j:2 * j + 2, 0] = np.asarray(res["ndk"]).reshape(2, 256, 4, 2, 64)
        ndv[2 * j:2 * j + 2, 0] = np.asarray(res["ndv"]).reshape(2, 256, 4, 128)
        ns = np.asarray(res["ns"])
        nsf[2 * j:2 * j + 2, 0] = ns[0].reshape(2, 4, 64, 128)
        nsb[2 * j:2 * j + 2, 0] = ns[1].reshape(2, 4, 64, 128)
    return (y_prompt, y_sample, ndk, ndv, nsf, nsb)


_NC_CACHE = {}


def kernel(**inputs):
    maps = make_in_maps(**inputs)
    if "nc" not in _NC_CACHE:
        _NC_CACHE["nc"] = build_program()
    nc = _NC_CACHE["nc"]
    res = run_bass_kernel_spmd(nc, maps, core_ids=list(range(N_CORES)))
    return assemble(res.results)
```

```python
import math
from contextlib import ExitStack

import numpy as np
import concourse.bass as bass
import concourse.mybir as mybir
from concourse.bass_utils import run_bass_kernel_spmd

F32 = mybir.dt.float32
BF16 = mybir.dt.bfloat16
AF = mybir.ActivationFunctionType
ALU = mybir.AluOpType
AX = mybir.AxisListType

N_CORES = 8
D = 1024
EPS = 1e-6
NTOK = 1536
NTILE = 12
PROJ = 3616
LAM_INIT = 0.2
SCALE_GLA = 0.125
SCALE_ATT = 0.125

C_LF01, C_LB01, C_LF16, C_LB16, C_MSF16, C_MSB16, C_ID, C_R, C_ONES = [i * 128 for i in range(9)]
C_CI16 = 9 * 128
C_MSF16F = 9 * 128 + 2
C_MSB16F = C_MSF16F + 128
C_ONE16 = C_MSB16F + 128
NCONST = C_ONE16 + 1
ARENA_BYTES = 204 * 1024
GTENG = "act"
KH_ENG = "dve"
QK_ENG = "pool"
SV_ENG = "act"
OG_ENG = "dve"
SP_SLOTS = 6


class Tr:
    __slots__ = ("w", "r", "excl")

    def __init__(self, excl=False):
        self.w = None
        self.r = {}
        self.excl = excl


class TT:
    __slots__ = ("ap", "tr")

    def __init__(self, ap, tr):
        self.ap = ap
        self.tr = tr

    def __getitem__(self, idx):
        return TT(self.ap[idx], self.tr)

    def v(self, fn):
        return TT(fn(self.ap), self.tr)


class Prog:
    def __init__(self, nc, es, n_slots=6):
        self.nc = nc
        self.es = es
        self.eng = {"pe": nc.tensor, "act": nc.scalar, "dve": nc.vector, "pool": nc.gpsimd, "sp": nc.sync}
        self.sem = {}
        self.count = {}
        self.known = {k: {} for k in self.eng}
        for k in self.eng:
            self.sem[k] = es.enter_context(nc.semaphore("s_" + k))
            self.count[k] = 0
        self.slots = {}
        self.slot_next = {}
        for q in ("sp", "pool", "act"):
            self.slots[q] = []
            for i in range(SP_SLOTS if q == 'sp' else n_slots):
                key = "d_%s_%d" % (q, i)
                self.sem[key] = es.enter_context(nc.semaphore(key))
                self.count[key] = 0
                self.slots[q].append(key)
            self.slot_next[q] = 0
        self.n_wait = 0
        self.n_ins = 0
        self.freed_events = {}
        self.tcount = 0
        self.regions = {}

    def init_arena(self, nbytes):
        self.arena = self.nc.alloc_sbuf_tensor("arena", [128, nbytes], mybir.dt.uint8)
        self.free_list = [[0, nbytes, {}]]
        self.peak = 0
        self.used = 0

    def sb(self, shape, dt, name=None):
        esz = 4 if dt == F32 else 2
        per_part = esz
        for s in shape[1:]:
            per_part *= s
        size = (per_part + 63) // 64 * 64
        fl = self.free_list
        for i in range(len(fl)):
            tot = 0
            j = i
            while j < len(fl) and (j == i or fl[j][0] == fl[j - 1][1]):
                tot += fl[j][1] - fl[j][0]
                if tot >= size:
                    break
                j += 1
            if tot >= size:
                start = fl[i][0]
                evs = {}
                need = size
                k = i
                new_items = []
                while need > 0:
                    s0, e0, ev = fl[k]
                    for kk, vv in ev.items():
                        evs[kk] = max(evs.get(kk, 0), vv)
                    take = min(need, e0 - s0)
                    need -= take
                    if take < e0 - s0:
                        new_items.append([s0 + take, e0, ev])
                    k += 1
                fl[i:k] = new_items
                tr = Tr()
                tr.r = evs
                v = self.arena[:, start:start + per_part].bitcast(dt)
                if len(shape) > 2:
                    names = " ".join("d%d" % n for n in range(1, len(shape)))
                    kw = {"d%d" % n: shape[n] for n in range(1, len(shape) - 1)}
                    v = v.rearrange("p (%s) -> p %s" % (names, names), **kw)
                if shape[0] < 128:
                    v = v[0:shape[0]]
                t = TT(v, tr)
                self.regions[id(tr)] = (start, start + size)
                self.used += size
                self.peak = max(self.peak, self.used)
                return t
        raise RuntimeError("arena out of memory allocating %s %s (used=%d)" % (name, shape, self.used))

    def free(self, *tiles):
        for t in tiles:
            s0, e0 = self.regions.pop(id(t.tr))
            ev = dict(t.tr.r)
            if t.tr.w is not None:
                k, v = t.tr.w
                ev[k] = max(ev.get(k, 0), v)
            self.free_list.append([s0, e0, ev])
            self.used -= e0 - s0
        self.free_list.sort(key=lambda x: x[0])

    def _wait(self, eng, key, val):
        if self.known[eng].get(key, 0) >= val:
            return
        self.eng[eng].wait_ge(self.sem[key], val)
        self.known[eng][key] = val
        self.n_wait += 1

    def _deps(self, eng, reads, writes):
        evs = {}
        for t in reads:
            if t.tr.w is not None:
                k, v = t.tr.w
                evs[k] = max(evs.get(k, 0), v)
            if t.tr.excl:
                for k, v in t.tr.r.items():
                    if k != eng:
                        evs[k] = max(evs.get(k, 0), v)
        for t in writes:
            if t.tr.w is not None:
                k, v = t.tr.w
                evs[k] = max(evs.get(k, 0), v)
            for k, v in t.tr.r.items():
                evs[k] = max(evs.get(k, 0), v)
        for k, v in evs.items():
            if k == "pe" and eng == "pe":
                continue
            self._wait(eng, k, v)

    def _commit(self, ev, reads, writes):
        k, v = ev
        for t in reads:
            t.tr.r[k] = max(t.tr.r.get(k, 0), v)
        for t in writes:
            t.tr.w = ev
            t.tr.r = {}

    def op(self, eng, fn, reads=(), writes=(), inc=True):
        self._deps(eng, reads, writes)
        ins = fn(self.eng[eng])
        self.n_ins += 1
        if inc:
            self.count[eng] += 1
            ins.then_inc(self.sem[eng], 1)
            self._commit((eng, self.count[eng]), reads, writes)
        else:
            assert eng == "pe"
            self._commit((eng, self.count[eng] + 1), reads, writes)

    def dma(self, q, out_ap, in_ap, reads=(), writes=()):
        self._deps(q, reads, writes)
        i = self.slot_next[q]
        self.slot_next[q] = (i + 1) % len(self.slots[q])
        key = self.slots[q][i]
        if self.count[key] > 0:
            self._wait(q, key, 16 * self.count[key])
        ins = self.eng[q].dma_start(out=out_ap, in_=in_ap)
        self.count[key] += 1
        ins.then_inc(self.sem[key], 16)
        self.n_ins += 1
        self._commit((key, 16 * self.count[key]), reads, writes)

    def finish(self, eng="sp"):
        for q in self.slots:
            for key in self.slots[q]:
                if self.count[key] > 0:
                    self._wait(eng, key, 16 * self.count[key])


class PsumPool:
    def __init__(self, nc, n=8):
        self.banks = []
        for i in range(n):
            h = nc.alloc_psum_tensor("psb%d" % i, [128, 512], F32)
            self.banks.append(TT(h[:, :], Tr(excl=True)))
        self.free = list(range(n))

    def get(self):
        i = self.free.pop(0)
        b = self.banks[i]
        b_id = i
        t = TT(b.ap, b.tr)
        return t, b_id

    def put(self, b_id):
        self.free.append(b_id)


class _Stop(Exception):
    pass


def build_program(stop=99):
    nc = bass.Bass("TRN2", target_bir_lowering=False)

    def din(name, shape):
        return nc.dram_tensor(name, list(shape), F32, kind="ExternalInput").ap()

    def dout(name, shape):
        return nc.dram_tensor(name, list(shape), F32, kind="ExternalOutput").ap()

    xT_d = din("xT", [8, 128, NTOK])
    xtm_d = din("xtm", [768, D])
    cvT_d = din("cvT", [128, 8, 2])
    gainT_d = din("gainT", [128, 8])
    wmod_d = din("wmod", [D, 3 * D])
    bmod_d = din("bmod", [3 * D])
    win_d = din("win", [D, PROJ])
    wout_d = din("wout", [D, D])
    wlr_d = din("wlr", [128, 8 * 32])
    wal_d = din("walpha", [17, 2, 256])
    dlam_d = din("dlam", [256])
    hgain_d = din("hgain", [D])
    fgain_d = din("fgain", [D])
    ckT_d = din("ckT", [4, 128, 512])
    cv_d = din("cv", [512, 512])
    s0_d = din("s0", [2, 2, 128, 128])
    sel_d = din("sel", [128, 8])
    cos_d = din("ropecos", [128, 1024])
    sin_d = din("ropesin", [128, 1024])
    const_d = din("consts", [128, NCONST])

    yp_d = dout("yp", [512, D])
    ys_d = dout("ys", [256, D])
    ndk_d = dout("ndk", [512, 512])
    ndv_d = dout("ndv", [512, 512])
    ns_d = dout("ns", [2, 2, 2, 128, 128])

    es = ExitStack()
    try:
      with es:
        P = Prog(nc, es)
        P.init_arena(ARENA_BYTES)
        PS = PsumPool(nc)

        def ckpt(k):
            if stop <= k:
                P.finish("sp")
                print("[build] STOP at %s instructions=%d" % (k, P.n_ins))
                raise _Stop()

        def op(eng, fn, reads=(), writes=(), inc=True):
            P.op(eng, fn, reads, writes, inc)

        def run_streams(gens):
            gens = list(gens)
            while gens:
                for g in list(gens):
                    try:
                        next(g)
                    except StopIteration:
                        gens.remove(g)

        PS_RESERVE = [0]

        def get_banks(n):
            while len(PS.free) < n + PS_RESERVE[0]:
                yield
            return [PS.get() for _ in range(n)]

        consts = P.sb([128, NCONST], F32, "consts")
        constb = P.sb([128, NCONST], BF16, "constb")
        P.dma("sp", consts.ap, const_d, writes=[consts])
        op("dve", lambda e: e.tensor_copy(out=constb.ap, in_=consts.ap), [consts], [constb])
        epsb = P.sb([128, 1], F32, "epsb")
        op("dve", lambda e: e.memset(epsb.ap, EPS), [], [epsb])
        junk = P.sb([128, D], F32, "junk")

        cvT = P.sb([128, 8, 2], F32, "cvT")
        P.dma("sp", cvT.ap, cvT_d, writes=[cvT])
        gainT = P.sb([128, 8], F32, "gainT")
        P.dma("sp", gainT.ap, gainT_d, writes=[gainT])
        bmod2 = P.sb([2, 3 * D], F32, "bmod2")
        P.dma("sp", bmod2.ap, bmod_d.partition_broadcast(2), writes=[bmod2])
        cs = P.sb([128, 8, 2], BF16, "cs")
        op("act", lambda e: e.activation(out=cs.ap, in_=cvT.ap, func=AF.Silu), [cvT], [cs])
        mrow = P.sb([2, 3 * D], F32, "mrow")
        wmb = [P.sb([128, 8, 512], BF16, "wmb%d" % i) for i in range(4)]
        wbufs = [P.sb([128, 8, 512], BF16, "wbuf%d" % i) for i in range(3)]
        wAB = P.sb([128, 8, 544], BF16, "wAB")
        wctr = [0]

        def load_w(src_d, c0, n, wb=None, after=()):
            if wb is None:
                wb = wbufs[wctr[0] % len(wbufs)]
                wctr[0] += 1
            src = src_d[:, c0:c0 + n].rearrange("(k p) n -> p k n", p=128)
            P.dma("pool", wb.ap[:, :, 0:n], src, reads=list(after), writes=[wb])
            return wb

        xu = [P.sb([128, 8, 512], F32, "xu%d" % b) for b in range(3)]
        sqb = [P.sb([128, 8, 512], BF16, "sqb%d" % i) for i in range(2)]
        rs1 = [P.sb([128, 512], F32, "rs1_%d" % i) for i in range(2)]
        rstd = [P.sb([128, 512], F32, "rstd%d" % i) for i in range(2)]
        xT_v = xT_d.rearrange("k p t -> p k t")
        xdone = TT(None, Tr())
        for b in range(3):
            P.dma("sp", xu[b].ap, xT_v[:, :, b * 512:(b + 1) * 512], writes=[xu[b]] + ([xdone] if b == 2 else []))
        for nb in range(4):
            load_w(wmod_d, nb * 512, 512, wmb[nb], after=[xdone])

        def mod_rows(wb, nb):
            (ps, pid), = yield from get_banks(1)
            for kc in range(8):
                op("pe", lambda e: e.matmul(ps.ap[0:2, :], lhsT=cs.ap[:, kc, :], rhs=wb.ap[:, kc, :],
                                            start=(kc == 0), stop=(kc == 7)), [cs, wb], [ps], inc=(kc == 7))
            yield
            op("dve", lambda e: e.tensor_tensor(out=mrow.ap[:, nb * 512:(nb + 1) * 512], in0=ps.ap[0:2, :],
                                                in1=bmod2.ap[:, nb * 512:(nb + 1) * 512], op=ALU.add), [ps, bmod2], [mrow])
            PS.put(pid)
            yield

        def stream_x():
            for b in range(3):
                xs = xu[b]
                sq = sqb[b % 2]
                for hf in range(2):
                    op("act", lambda e: e.activation(out=sq.ap[:, hf * 4:(hf + 1) * 4, :], in_=xs.ap[:, hf * 4:(hf + 1) * 4, :], func=AF.Square), [xs], [sq])
                yield
                (ps, pid), = yield from get_banks(1)
                for kc in range(8):
                    op("pe", lambda e: e.matmul(ps.ap, lhsT=constb.ap[:, C_ONES:C_ONES + 128], rhs=sq.ap[:, kc, :],
                                                start=(kc == 0), stop=(kc == 7)), [constb, sq], [ps], inc=(kc == 7))
                yield
                op("act", lambda e: e.activation(out=rs1[b % 2].ap, in_=ps.ap, func=AF.Ln, bias=epsb.ap, scale=1.0 / D),
                   [ps, epsb], [rs1[b % 2]])
                PS.put(pid)
                op("act", lambda e: e.activation(out=rstd[b % 2].ap, in_=rs1[b % 2].ap, func=AF.Exp, scale=-0.5), [rs1[b % 2]], [rstd[b % 2]])
                yield
                for hf in range(2):
                    op("dve", lambda e: e.tensor_tensor(out=xs.ap[:, hf * 4:(hf + 1) * 4, :], in0=xs.ap[:, hf * 4:(hf + 1) * 4, :],
                                                        in1=rstd[b % 2].ap.unsqueeze(1).to_broadcast([128, 4, 512]), op=ALU.mult),
                       [xs, rstd[b % 2]], [xs])
                yield

        def stream_m():
            for nb in range(4):
                yield from mod_rows(wmb[nb], nb)

        run_streams([stream_x()])
        run_streams([stream_m()])
        P.free(sqb[0], sqb[1], rs1[0], rs1[1], rstd[0], rstd[1], *wmb)

        modT = P.sb([128, 2, 8, 2], F32, "modT")
        ps, pid = PS.get()
        for which in range(2):
            for kc in range(8):
                o = (which * 8 + kc) * 2
                c0 = which * D + kc * 128
                op("pe", lambda e: e.transpose(ps.ap[:, o:o + 2], mrow.ap[0:2, c0:c0 + 128], consts.ap[0:2, C_ID:C_ID + 2]),
                   [mrow, consts], [ps], inc=(which == 1 and kc == 7))
        op("dve", lambda e: e.tensor_copy(out=modT.ap.rearrange("p a b c -> p (a b c)"), in_=ps.ap[:, 0:32]), [ps], [modT])
        PS.put(pid)
        aT = P.sb([128, 8, 2], F32, "aT")
        for i in range(2):
            op("dve", lambda e: e.scalar_tensor_tensor(out=aT.ap[:, :, i], in0=modT.ap[:, 1, :, i], scalar=1.0,
                                                       in1=gainT.ap, op0=ALU.add, op1=ALU.mult), [modT, gainT], [aT])
        gt_bc = [P.sb([128, D], F32, "gtbc%d" % i) for i in range(2)]

        def gate_rows():
            for nb in (4, 5):
                yield from mod_rows(gate_wb[nb - 4], nb)

        def gate_bcast():
            for i in range(2):
                for nb in range(2):
                    (ps, pid), = yield from get_banks(1)
                    op("pe", lambda e: e.matmul(ps.ap, lhsT=selr.ap[:, i, :], rhs=mrow.ap[0:2, 2 * D + nb * 512:2 * D + (nb + 1) * 512],
                                                start=True, stop=True), [selr, mrow], [ps])
                    op("act", lambda e: e.copy(out=gt_bc[i].ap[:, nb * 512:(nb + 1) * 512], in_=ps.ap), [ps], [gt_bc[i]])
                    PS.put(pid)
            P.free(mrow, bmod2, selr, cvT, cs)

        hT2 = [P.sb([128, 8, 512], BF16, "hT%d" % b) for b in range(3)]
        hTa = [TT(hT2[b].ap, Tr()) for b in range(3)]
        hTd = [TT(hT2[b].ap, Tr()) for b in range(3)]
        for b in range(3):
            hTa[b].tr.r = dict(hT2[b].tr.r)
            hTd[b].tr.r = dict(hT2[b].tr.r)
        for b in range(3):
            ii = 0 if b < 1 else 1
            hb = hT2[b]
            for kc in range(8):
                if kc % 3 == 0:
                    op("act", lambda e: e.activation(out=hb.ap[:, kc, :], in_=xu[b].ap[:, kc, :], func=AF.Identity,
                                                     bias=modT.ap[:, 0, kc, ii:ii + 1], scale=aT.ap[:, kc, ii:ii + 1]), [xu[b], modT, aT], [hTa[b]])
                else:
                    op("dve", lambda e: e.tensor_scalar(out=hb.ap[:, kc, :], in0=xu[b].ap[:, kc, :], scalar1=aT.ap[:, kc, ii:ii + 1],
                                                        scalar2=modT.ap[:, 0, kc, ii:ii + 1], op0=ALU.mult, op1=ALU.add), [xu[b], modT, aT], [hTd[b]])
        P.free(*xu)
        P.free(gainT, aT)
        ckpt(1)

        dlam = P.sb([128, 256], F32, "dlam")
        P.dma("sp", dlam.ap, dlam_d.partition_broadcast(128), writes=[dlam])
        lam2 = P.sb([128, 2], F32, "lam2")
        lscr = P.sb([128, 64], F32, "lscr")
        for i in range(2):
            op("dve", lambda e: e.tensor_tensor(out=lscr.ap, in0=dlam.ap[:, (2 * i) * 64:(2 * i + 1) * 64],
                                                in1=dlam.ap[:, (2 * i + 1) * 64:(2 * i + 2) * 64], op=ALU.mult), [dlam], [lscr])
            op("dve", lambda e: e.tensor_reduce(out=lam2.ap[:, i:i + 1], in_=lscr.ap, axis=AX.X, op=ALU.add), [lscr], [lam2])
        lam2e = P.sb([128, 2], F32, "lam2e")
        op("act", lambda e: e.activation(out=lam2e.ap, in_=lam2.ap, func=AF.Exp), [lam2], [lam2e])
        nlam = P.sb([128, 1], F32, "nlam")
        op("dve", lambda e: e.tensor_tensor(out=nlam.ap, in0=lam2e.ap[:, 1:2], in1=lam2e.ap[:, 0:1], op=ALU.subtract), [lam2e], [nlam])
        op("dve", lambda e: e.tensor_scalar(out=nlam.ap, in0=nlam.ap, scalar1=-LAM_INIT, scalar2=None, op0=ALU.add), [nlam], [nlam])
        P.free(dlam, lam2, lscr, lam2e)

        hgain = P.sb([128, D], F32, "hgain")
        P.dma("sp", hgain.ap, hgain_d.partition_broadcast(128), writes=[hgain])
        op("dve", lambda e: e.tensor_scalar(out=hgain.ap[:, 512:1024], in0=hgain.ap[:, 512:1024], scalar1=1.0 - LAM_INIT,
                                            scalar2=None, op0=ALU.mult), [hgain], [hgain])

        selr = P.sb([2, 2, 128], F32, "selr")
        op("dve", lambda e: e.memset(selr.ap[:, 0, :], 0.0), [], [selr])
        op("dve", lambda e: e.memset(selr.ap[0:1, 0, :], 1.0), [], [selr])
        op("dve", lambda e: e.memset(selr.ap[:, 1, :], 1.0), [], [selr])
        op("dve", lambda e: e.memset(selr.ap[0:1, 1, :], 0.0), [], [selr])
        def proj_fm(wb, c0, m, ranges):
            outs = [PS.get() for _ in ranges]
            for kc in range(8):
                for ri, (tok0, ntok) in enumerate(ranges):
                    hb = hT2[tok0 // 512]
                    o = tok0 % 512
                    assert o + ntok <= 512
                    deps = [hTa[tok0 // 512], hTd[tok0 // 512]]
                    out_ps = outs[ri][0]
                    op("pe", lambda e: e.matmul(out_ps.ap[0:m, 0:ntok], lhsT=wb.ap[:, kc, c0:c0 + m], rhs=hb.ap[:, kc, o:o + ntok],
                                                start=(kc == 0), stop=(kc == 7)), [wb] + deps, [out_ps], inc=(kc == 7))
            return outs

        def proj_tm(wb, c0, n, T, out_ps):
            hb = hT2[T // 4]
            o = (T % 4) * 128
            for kc in range(8):
                op("pe", lambda e: e.matmul(out_ps.ap[:, 0:n], lhsT=hb.ap[:, kc, o:o + 128],
                                            rhs=wb.ap[:, kc, c0:c0 + n], start=(kc == 0), stop=(kc == 7)),
                   [wb, hTa[T // 4], hTd[T // 4]], [out_ps], inc=(kc == 7))

        evac_ctr = [1]

        def evac(out_t, out_ap, in_ap, in_t, scale=None):
            evac_ctr[0] += 1
            if evac_ctr[0] % 2 == 0:
                if scale is None:
                    op("act", lambda e: e.copy(out=out_ap, in_=in_ap), [in_t], [out_t])
                else:
                    op("act", lambda e: e.mul(out_ap, in_ap, scale), [in_t], [out_t])
            else:
                if scale is None:
                    op("dve", lambda e: e.tensor_copy(out=out_ap, in_=in_ap), [in_t], [out_t])
                else:
                    op("dve", lambda e: e.tensor_scalar(out=out_ap, in0=in_ap, scalar1=scale, scalar2=None, op0=ALU.mult),
                       [in_t], [out_t])

        OWN_RANGES = [(0, 512), (512, 256)]
        ALL_RANGES = [(0, 512), (512, 512), (1024, 512)]

        lrT = P.sb([17, 2, NTOK], BF16, "lrT")
        op("dve", lambda e: e.memset(lrT.ap, 1.0), [], [lrT])
        wal = P.sb([17, 2, 256], BF16, "wal")
        P.dma("pool", wal.ap, wal_d, writes=[wal])

        ckpt(1.5)
        gqT = [P.sb([128, 768], BF16, "gqT%d" % j) for j in range(2)]
        gkT = [P.sb([128, 768], BF16, "gkT%d" % j) for j in range(2)]
        gk_tm = [P.sb([128, 256], F32, "gktm%d" % t) for t in range(NTILE)]
        wb = wAB
        P.dma("pool", wb.ap[:, :, 0:512], win_d[:, 0:512].rearrange("(k p) n -> p k n", p=128), writes=[wb])
        P.dma("pool", wb.ap[:, :, 512:544], wlr_d.rearrange("p (k n) -> p k n", k=8), writes=[wb])
        gate_wb = [load_w(wmod_d, nb * 512, 512) for nb in (4, 5)]
        lr_tm = [P.sb([128, 32], BF16, "lrtm%d" % t) for t in range(NTILE)]
        gkb = [P.sb([128, 256], BF16, "gkb%d" % i) for i in range(3)]
        pend_tr = []
        pend_lr = []
        def fm_gq():
            for j in range(2):
                outs = proj_fm(wb, j * 128, 128, OWN_RANGES)
                for (ps, pid), (t0, nt) in zip(outs, OWN_RANGES):
                    evac(gqT[j], gqT[j].ap[:, t0:t0 + nt], ps.ap[:, 0:nt], ps, scale=SCALE_GLA)
                    PS.put(pid)

        for T in range(NTILE):
            if T == 6:
                fm_gq()
            ps, pid = PS.get()
            proj_tm(wb, 256, 288, T, ps)
            evac(gk_tm[T], gk_tm[T].ap, ps.ap[:, 0:256], ps)
            evac(lr_tm[T], lr_tm[T].ap, ps.ap[:, 256:288], ps)
            if T < 6:
                kb_ = gkb[T % 3]
                evac(kb_, kb_.ap, ps.ap[:, 0:256], ps)

                def do_tr(T=T, kb_=kb_):
                    pst, ptid = PS.get()
                    for j in range(2):
                        op("pe", lambda e: e.matmul(pst.ap[:, j * 128:(j + 1) * 128], lhsT=kb_.ap[:, j * 128:(j + 1) * 128],
                                                    rhs=constb.ap[:, C_ID:C_ID + 128], start=True, stop=True), [kb_, constb], [pst], inc=(j == 1))
                    for j in range(2):
                        evac(gkT[j], gkT[j].ap[:, T * 128:(T + 1) * 128], pst.ap[:, j * 128:(j + 1) * 128], pst)
                    PS.put(ptid)
                pend_tr.append(do_tr)
            PS.put(pid)
            if len(pend_tr) > 2 or (T >= 6 and pend_tr):
                pend_tr.pop(0)()
            if T % 4 == 3:
                def do_lr(T=T):
                    for d in range(2):
                        psl, pl = PS.get()
                        for t4 in range(4):
                            Tt = T - 3 + t4
                            op("pe", lambda e: e.matmul(psl.ap[0:16, t4 * 128:(t4 + 1) * 128], lhsT=lr_tm[Tt].ap[:, d * 16:(d + 1) * 16],
                                                        rhs=constb.ap[:, C_ID:C_ID + 128], start=True, stop=True), [lr_tm[Tt], constb], [psl],
                               inc=(t4 == 3))
                        evac(lrT, lrT.ap[0:16, d, (T - 3) * 128:(T + 1) * 128], psl.ap[0:16, 0:512], psl)
                        PS.put(pl)
                pend_lr.append([do_lr, 2])
            for it in list(pend_lr):
                it[1] -= 1
                if it[1] < 0 or T == NTILE - 1:
                    pend_lr.remove(it)
                    it[0]()
        P.free(wb, *lr_tm, *gkb)

        gv_tm = [P.sb([128, 512], BF16, "gvtm%d" % t) for t in range(NTILE)]
        wb = load_w(win_d, 512, 512)
        for _ in gate_rows():
            pass
        for T in range(NTILE):
            ps, pid = PS.get()
            proj_tm(wb, 0, 512, T, ps)
            evac(gv_tm[T], gv_tm[T].ap, ps.ap, ps)
            PS.put(pid)

        ckpt(1.7)
        rcos = P.sb([128, 1024], F32, "rcos")
        rsin = P.sb([128, 1024], F32, "rsin")
        P.dma("sp", rcos.ap, cos_d, writes=[rcos])
        P.dma("sp", rsin.ap, sin_d, writes=[rsin])
        NR = 4
        rx = [P.sb([128, 512], BF16, "rx%d" % i) for i in range(NR)]
        rt1 = [P.sb([128, 512], F32, "rt1_%d" % i) for i in range(NR)]
        rt2 = [P.sb([128, 512], F32, "rt2_%d" % i) for i in range(2)]
        rctr = [0]
        rbctr = [0]
        rope_pend = []

        def evac_rope(out_t, out_ap, ps, c0, nt):
            i = rctr[0] % NR
            rctr[0] += 1
            op("act", lambda e: e.copy(out=rx[i].ap[:, 0:nt], in_=ps.ap[:, 0:nt]), [ps], [rx[i]])
            op("dve", lambda e: e.tensor_tensor(out=rt1[i].ap[:, 0:nt], in0=ps.ap[:, 0:nt], in1=rcos.ap[:, c0:c0 + nt], op=ALU.mult),
               [ps, rcos], [rt1[i]])

            def part_b():
                k = rbctr[0] % 2
                rbctr[0] += 1
                ps2, pid2 = PS.get()
                op("pe", lambda e: e.matmul(ps2.ap[:, 0:nt], lhsT=constb.ap[:, C_R:C_R + 128], rhs=rx[i].ap[:, 0:nt], start=True, stop=True),
                   [constb, rx[i]], [ps2])
                op("dve", lambda e: e.tensor_tensor(out=rt2[k].ap[:, 0:nt], in0=ps2.ap[:, 0:nt], in1=rsin.ap[:, c0:c0 + nt], op=ALU.mult),
                   [ps2, rsin], [rt2[k]])
                PS.put(pid2)
                op("dve", lambda e: e.tensor_tensor(out=out_ap, in0=rt1[i].ap[:, 0:nt], in1=rt2[k].ap[:, 0:nt], op=ALU.add), [rt1[i], rt2[k]], [out_t])
            rope_pend.append(part_b)

        def rope_flush():
            while rope_pend:
                rope_pend.pop(0)()

        dqT = [P.sb([128, 768], BF16, "dqT%d" % h) for h in range(4)]
        wb = load_w(win_d, 1056, 512)
        for h in range(4):
            outs = proj_fm(wb, h * 128, 128, OWN_RANGES)
            rope_flush()
            for (ps, pid), (t0, nt) in zip(outs, OWN_RANGES):
                if t0 < 512:
                    evac(dqT[h], dqT[h].ap[:, t0:t0 + nt], ps.ap[:, 0:nt], ps)
                else:
                    evac_rope(dqT[h], dqT[h].ap[:, t0:t0 + nt], ps, t0 - 512, nt)
                PS.put(pid)

        for _ in gate_bcast():
            pass
        ckpt(1.75)
        dkT = [P.sb([128, NTOK], BF16, "dkT%d" % h) for h in range(4)]
        ostage = [P.sb([128, 512], F32, "ostage%d" % i) for i in range(3)]
        octr = [0]
        wb = load_w(win_d, 1568, 512)
        SAMPLE_RANGES = [(512, 512), (1024, 512)]
        for h in range(4):
            outs = proj_fm(wb, h * 128, 128, SAMPLE_RANGES)
            rope_flush()
            for (ps, pid), (t0, nt) in zip(outs, SAMPLE_RANGES):
                evac_rope(dkT[h], dkT[h].ap[:, t0:t0 + nt], ps, t0 - 512, nt)
                PS.put(pid)
        dkb = [P.sb([128, 512], BF16, "dkb%d" % i) for i in range(4)]
        pend_trk = []
        for T in range(4):
            ps, pid = PS.get()
            proj_tm(wb, 0, 512, T, ps)
            rope_flush()
            st = ostage[octr[0] % 3]
            octr[0] += 1
            evac(st, st.ap, ps.ap, ps)
            kb_ = dkb[T]
            evac(kb_, kb_.ap, ps.ap, ps)
            PS.put(pid)
            P.dma("sp", ndk_d[T * 128:(T + 1) * 128, :], st.ap, reads=[st])

            def do_trk(T=T, kb_=kb_):
                pst, ptid = PS.get()
                for h in range(4):
                    op("pe", lambda e: e.matmul(pst.ap[:, h * 128:(h + 1) * 128], lhsT=kb_.ap[:, h * 128:(h + 1) * 128],
                                                rhs=constb.ap[:, C_ID:C_ID + 128], start=True, stop=True), [kb_, constb], [pst], inc=(h == 3))
                for h in range(4):
                    evac(dkT[h], dkT[h].ap[:, T * 128:(T + 1) * 128], pst.ap[:, h * 128:(h + 1) * 128], pst)
                PS.put(ptid)
            pend_trk.append(do_trk)

        ckpt(1.8)
        dv1 = [P.sb([128, 4, 130], BF16, "dv1_%d" % t) for t in range(NTILE)]
        wb = load_w(win_d, 2080, 512)
        for T in range(NTILE):
            if T >= 2 and pend_trk:
                pend_trk.pop(0)()
            op("dve", lambda e: e.memset(dv1[T].ap[:, :, 128:130], 1.0), [], [dv1[T]])
            ps, pid = PS.get()
            proj_tm(wb, 0, 512, T, ps)
            evac(dv1[T], dv1[T].ap[:, :, 0:128], ps.ap.rearrange("p (h v) -> p h v", h=4), ps)
            if T < 4:
                st = ostage[octr[0] % 3]
                octr[0] += 1
                evac(st, st.ap, ps.ap, ps)
                P.dma("sp", ndv_d[T * 128:(T + 1) * 128, :], st.ap, reads=[st])
            PS.put(pid)

        assert not pend_trk
        P.free(*dkb)

        G = [P.sb([128, D], F32, "G%d" % t) for t in range(6)]
        gsl = [P.sb([128, 512], F32, "gsl%d" % i) for i in range(2)]
        for nb in range(2):
            wb = load_w(win_d, 2592 + nb * 512, 512)
            for T in range(6):
                ps, pid = PS.get()
                proj_tm(wb, 0, 512, T, ps)
                g1 = gsl[T % 2]
                op("act", lambda e: e.activation(out=g1.ap, in_=ps.ap, func=AF.Silu), [ps], [g1])
                PS.put(pid)
                op("dve", lambda e: e.tensor_tensor(out=G[T].ap[:, nb * 512:(nb + 1) * 512], in0=g1.ap,
                                                    in1=hgain.ap[:, nb * 512:(nb + 1) * 512], op=ALU.mult), [g1, hgain], [G[T]])
        ckpt(2)
        for b in range(3):
            for s_ in (hTa[b].tr, hTd[b].tr):
                for k, v in list(s_.r.items()) + ([s_.w] if s_.w is not None else []):
                    hT2[b].tr.r[k] = max(hT2[b].tr.r.get(k, 0), v)
        P.free(*hT2)
        P.free(rcos, rsin, *rx, *rt1, *rt2, gsl[0], gsl[1], hgain, ostage[0], ostage[1], ostage[2], modT)

        P.free(*wbufs)

        ckpt(2.5)
        NSLOT = 6
        oT = [P.sb([128, 8, 128], BF16, "oT%d" % t) for t in range(6)]
        sel = P.sb([128, 8], F32, "sel")
        P.dma("sp", sel.ap, sel_d, writes=[sel])
        ktil = [[P.sb([128, 128], BF16, "ktil") for j in range(2)] for s in range(NSLOT)]
        S = {(s, j): P.sb([128, 128], F32, "S") for s in range(NSLOT) for j in range(2)}
        acc = {(s, j): P.sb([128, 128], F32, "acc") for s in range(2) for j in range(2)}
        e_tm = [P.sb([128, 256], F32, "etm") for s in range(NSLOT)]
        l_tm = [P.sb([128, 256], BF16, "ltm") for s in range(NSLOT)]
        ew = [P.sb([128, 256], F32, "ew") for s in range(NSLOT)]
        khat = [P.sb([128, 256], BF16, "khat") for s in range(NSLOT)]
        Dt = [P.sb([128, 4], F32, "Dt") for s in range(NSLOT)]
        EB = [P.sb([128, 2, 128], F32, "EB") for s in range(NSLOT)]
        EBn = [P.sb([128, 2, 128], F32, "EBn") for s in range(NSLOT)]
        qtil = {(s, d, j): P.sb([128, 128], BF16, "qtil") for s in range(6) for d in range(2) for j in range(2)}
        attm = {(s, d): P.sb([128, 4, 128], BF16, "attm") for s in range(6) for d in range(2)}
        sbf = {(s, d, j, c): P.sb([128, 128], BF16, "sbf") for s in range(6) for d in range(2) for j in range(2) for c in range(2)}

        def gla_unit(u, T, d, own, chain_cb):
            sl = T
            if own:
                msk16 = C_MSF16 if d == 0 else C_MSB16
            else:
                msk16 = C_MSF16F if d == 0 else C_MSB16F
            l16 = C_LF16 if d == 0 else C_LB16
            l01 = C_LF01 if d == 0 else C_LB01
            (psz, pz), = yield from get_banks(1)
            op("pe", lambda e: e.matmul(psz.ap[:, 0:256], lhsT=lrT.ap[0:17, d, T * 128:(T + 1) * 128], rhs=wal.ap[0:17, d, :],
                                        start=True, stop=True), [lrT, wal], [psz])
            yield
            op("act", lambda e: e.activation(out=e_tm[u].ap, in_=psz.ap[:, 0:256], func=AF.Exp, scale=-1.0), [psz], [e_tm[u]])
            PS.put(pz)
            op("act", lambda e: e.activation(out=l_tm[u].ap, in_=e_tm[u].ap, func=AF.Ln, bias=1.0, scale=1.0), [e_tm[u]], [l_tm[u]])
            yield
            (psw, pw), = yield from get_banks(1)
            op("pe", lambda e: e.matmul(psw.ap[:, 0:256], lhsT=constb.ap[:, msk16:msk16 + 128], rhs=l_tm[u].ap, start=True, stop=True),
               [constb, l_tm[u]], [psw])
            if not own:
                for j in range(2):
                    op("pe", lambda e: e.matmul(psw.ap[:, 256 + j:257 + j], lhsT=l_tm[u].ap[:, j * 128:(j + 1) * 128],
                                                rhs=constb.ap[:, C_ONE16:C_ONE16 + 1], start=True, stop=True), [constb, l_tm[u]], [psw])
            if own:
                (psb, pb), = yield from get_banks(1)
                for j in range(2):
                    op("pe", lambda e: e.matmul(psb.ap[:, j * 128:(j + 1) * 128], lhsT=l_tm[u].ap[:, j * 128:(j + 1) * 128],
                                                rhs=constb.ap[:, l16:l16 + 128], start=True, stop=True), [constb, l_tm[u]], [psb])
            yield
            op("act", lambda e: e.activation(out=ew[u].ap, in_=psw.ap[:, 0:256], func=AF.Exp), [psw], [ew[u]])
            if not own:
                op("act", lambda e: e.activation(out=Dt[u].ap[:, 0:2], in_=psw.ap[:, 256:258], func=AF.Exp), [psw], [Dt[u]])
            PS.put(pw)
            if own:
                op("act", lambda e: e.activation(out=EB[u].ap.rearrange("p a b -> p (a b)"), in_=psb.ap[:, 0:256], func=AF.Exp),
                   [psb], [EB[u]])
                op("act", lambda e: e.activation(out=EBn[u].ap.rearrange("p a b -> p (a b)"), in_=psb.ap[:, 0:256], func=AF.Exp, scale=-1.0),
                   [psb], [EBn[u]])
                PS.put(pb)
            yield
            op(KH_ENG, lambda e: e.tensor_tensor(out=khat[u].ap, in0=gk_tm[T].ap, in1=ew[u].ap, op=ALU.mult),
               [gk_tm[T], ew[u]], [khat[u]])
            if own:
                for j in range(2):
                    qt = qtil[(sl, d, j)]
                    kt = ktil[u][j]
                    op(QK_ENG, lambda e: e.tensor_tensor(out=qt.ap, in0=gqT[j].ap[:, T * 128:(T + 1) * 128], in1=EB[u].ap[:, j, :], op=ALU.mult),
                       [gqT[j], EB[u]], [qt])
                    op(QK_ENG, lambda e: e.tensor_tensor(out=kt.ap, in0=gkT[j].ap[:, T * 128:(T + 1) * 128], in1=EBn[u].ap[:, j, :], op=ALU.mult),
                       [gkT[j], EBn[u]], [kt])
            yield
            if not own:
                (psA0, pA0), = yield from get_banks(1)
                for j in range(2):
                    for hp in range(2):
                        h = 2 * j + hp
                        op("pe", lambda e: e.matmul(
                            psA0.ap[hp * 64:(hp + 1) * 64, j * 128:(j + 1) * 128], lhsT=khat[u].ap[:, h * 64:(h + 1) * 64],
                            rhs=gv_tm[T].ap[:, h * 128:(h + 1) * 128], start=True, stop=True), [khat[u], gv_tm[T]], [psA0])
                yield
                for j in range(2):
                    chain_cb(0, j, psA0.ap[:, j * 128:(j + 1) * 128], Dt[u].ap[:, j:j + 1], psA0, Dt[u])
                PS.put(pA0)
                yield
                return
            (psA0, pA0), (psA1, pA1) = yield from get_banks(2)
            psA = (psA0, psA1)
            for j in range(2):
                for c in range(2):
                    for hp in range(2):
                        h = 2 * j + hp
                        op("pe", lambda e: e.matmul(
                            psA[c].ap[hp * 64:(hp + 1) * 64, j * 128:(j + 1) * 128], lhsT=khat[u].ap[c * 64:(c + 1) * 64, h * 64:(h + 1) * 64],
                            rhs=gv_tm[T].ap[c * 64:(c + 1) * 64, h * 128:(h + 1) * 128], start=True, stop=True),
                            [khat[u], gv_tm[T]], [psA[c]])
            yield
            corder = (0, 1) if d == 0 else (1, 0)
            for c in corder:
                for j in range(2):
                    if own:
                        col = c * 64 + (63 if d == 0 else 0)
                        chain_cb(c, j, psA[c].ap[:, j * 128:(j + 1) * 128], EB[u].ap[:, j, col:col + 1], psA[c], EB[u])
                    else:
                        chain_cb(c, j, psA[c].ap[:, j * 128:(j + 1) * 128], Dt[u].ap[:, j * 2 + c:j * 2 + c + 1], psA[c], Dt[u])
            PS.put(pA0)
            PS.put(pA1)
            yield
            if own:
                (psa0, pa0), (psa1, pa1) = yield from get_banks(2)
                psa = (psa0, psa1)
                for j in range(2):
                    for hp in range(2):
                        op("pe", lambda e: e.matmul(psa[hp].ap[:, j * 128:(j + 1) * 128], lhsT=ktil[u][j].ap[hp * 64:(hp + 1) * 64, :],
                                                    rhs=qtil[(sl, d, j)].ap[hp * 64:(hp + 1) * 64, :], start=True, stop=True),
                           [ktil[u][j], qtil[(sl, d, j)]], [psa[hp]])
                yield
                am = attm[(sl, d)]
                amv = am.ap.rearrange("p (j hp) t -> p hp j t", hp=2)
                for hp in range(2):
                    op("dve", lambda e: e.tensor_tensor(out=amv[:, hp], in0=psa[hp].ap[:, 0:256].rearrange("p (j t) -> p j t", j=2),
                                                        in1=constb.ap[:, l01:l01 + 128].unsqueeze(1).to_broadcast([128, 2, 128]), op=ALU.mult),
                       [psa[hp], constb], [am])
                PS.put(pa0)
                PS.put(pa1)
                yield

        def chain_step(s, j, A_ap, D_ap, psA, Dtile, save_to=None):
            st = S[(s, j)]
            if save_to is not None:
                if SV_ENG == "act":
                    op("act", lambda e: e.copy(out=save_to.ap, in_=st.ap), [st], [save_to])
                else:
                    op(SV_ENG, lambda e: e.tensor_copy(out=save_to.ap, in_=st.ap), [st], [save_to])
            op("dve", lambda e: e.scalar_tensor_tensor(out=st.ap, in0=st.ap, scalar=D_ap, in1=A_ap, op0=ALU.mult, op1=ALU.add),
               [st, psA, Dtile], [st])

        ssq = [P.sb([128, 4], F32, "ssq") for i in range(2)]
        rsh = [P.sb([128, 4], F32, "rsh") for i in range(2)]
        on1 = [P.sb([128, 4, 128], F32, "on1") for i in range(2)]
        junk2 = [junk, P.sb([128, 512], F32, "junk2")]
        ogb = [P.sb([128, 512], BF16, "ogb%d" % i) for i in range(2)]
        hctr = [0]

        def head_post(srcs, T, half, release=None, tset=None):
            if tset is None:
                i = hctr[0] % 2
                hctr[0] += 1
            else:
                i = tset
            jk = junk2[i]
            jv = jk.ap[:, 0:512].rearrange("p (h v) -> p h v", h=4)

            def hsel(ap3, heads):
                if len(heads) == 4:
                    return ap3
                return ap3.rearrange("p (j hp) v -> p hp j v", hp=2)[:, heads[0] % 2]

            for (st, sap, heads) in srcs:
                op("act", lambda e: e.activation(out=hsel(jv, heads), in_=sap, func=AF.Square), [st], [jk])
            yield
            op("dve", lambda e: e.tensor_reduce(out=ssq[i].ap, in_=jv, axis=AX.X, op=ALU.add), [jk], [ssq[i]])
            yield
            op("act", lambda e: e.activation(out=ssq[i].ap, in_=ssq[i].ap, func=AF.Ln, bias=epsb.ap, scale=1.0 / 128), [ssq[i], epsb], [ssq[i]])
            op("act", lambda e: e.activation(out=rsh[i].ap, in_=ssq[i].ap, func=AF.Exp, scale=-0.5), [ssq[i]], [rsh[i]])
            yield
            for (st, sap, heads) in srcs:
                nh = len(heads)
                rv = rsh[i].ap if nh == 4 else rsh[i].ap.rearrange("p (j hp) -> p hp j", hp=2)[:, heads[0] % 2]
                op("dve", lambda e: e.tensor_tensor(out=hsel(on1[i].ap, heads), in0=sap, in1=rv.unsqueeze(2).to_broadcast([128, nh, 128]), op=ALU.mult),
                   [st, rsh[i]], [on1[i]])
            if release is not None:
                release()
            op(OG_ENG, lambda e: e.tensor_tensor(out=ogb[i].ap, in0=on1[i].ap.rearrange("p h v -> p (h v)"),
                                                in1=G[T].ap[:, half * 512:(half + 1) * 512], op=ALU.mult), [on1[i], G[T]], [ogb[i]])
            yield
            yield
            yield
            (pst, pt), = yield from get_banks(1)
            for h in range(4):
                op("pe", lambda e: e.matmul(pst.ap[:, h * 128:(h + 1) * 128], lhsT=ogb[i].ap[:, h * 128:(h + 1) * 128],
                                            rhs=constb.ap[:, C_ID:C_ID + 128], start=True, stop=True), [ogb[i], constb], [pst])
            yield
            op("act", lambda e: e.copy(out=oT[T].ap[:, half * 4:(half + 1) * 4, :], in_=pst.ap.rearrange("p (h t) -> p h t", h=4)),
               [pst], [oT[T]])
            PS.put(pt)

        def gla_out(T, tset=None):
            sl = T
            (pso0, po0), (pso1, po1) = yield from get_banks(2)
            pso = (pso0, pso1)
            for h in range(4):
                j, hp = h // 2, h % 2
                ob = pso[hp]
                mms = []
                for d in range(2):
                    mms.append((ob.ap[:, j * 128:(j + 1) * 128], attm[(sl, d)].ap[:, h, :], gv_tm[T].ap[:, h * 128:(h + 1) * 128],
                                [attm[(sl, d)], gv_tm[T]]))
                for d in range(2):
                    for c in range(2):
                        mms.append((ob.ap[c * 64:(c + 1) * 64, j * 128:(j + 1) * 128],
                                    qtil[(sl, d, j)].ap[hp * 64:(hp + 1) * 64, c * 64:(c + 1) * 64],
                                    sbf[(sl, d, j, c)].ap[hp * 64:(hp + 1) * 64, :], [qtil[(sl, d, j)], sbf[(sl, d, j, c)]]))
                for n, (o_ap, l_ap, r_ap, rd) in enumerate(mms):
                    op("pe", lambda e: e.matmul(o_ap, lhsT=l_ap, rhs=r_ap, start=(n == 0), stop=(n >= len(mms) - 2)), rd, [ob])
            yield

            def rel():
                PS.put(po0)
                PS.put(po1)
            yield from head_post([(pso[hp], pso[hp].ap[:, 0:256].rearrange("p (j v) -> p j v", j=2), [hp, hp + 2]) for hp in range(2)], T, 0, release=rel, tset=tset)

        def prompt_chain(slot, sq_i, d):
            tiles = (2 * sq_i, 2 * sq_i + 1)
            for j in range(2):
                op("dve", lambda e: e.memset(S[(slot, j)].ap, 0.0), [], [S[(slot, j)]])
            order = tiles if d == 0 else tiles[::-1]
            for T in order:
                yield from gla_unit(slot, T, d, True,
                                    lambda c, j, A, Dp, psA, Dtile: chain_step(slot, j, A, Dp, psA, Dtile, save_to=sbf[(T, d, j, c)]))
            for j in range(2):
                P.dma("sp", ns_d[d, sq_i, j], S[(slot, j)].ap, reads=[S[(slot, j)]])


        def sample_chain(slot, d):
            for j in range(2):
                P.dma("sp", S[(slot, j)].ap, s0_d[d, j], writes=[S[(slot, j)]])
                op("dve", lambda e: e.memset(acc[(d, j)].ap, 0.0), [], [acc[(d, j)]])
            nchunks = [0]

            def sel_acc():
                i = nchunks[0] // 4
                for j in range(2):
                    op("dve", lambda e: e.scalar_tensor_tensor(out=acc[(d, j)].ap, in0=S[(slot, j)].ap, scalar=sel.ap[:, d * 4 + i:d * 4 + i + 1],
                                                               in1=acc[(d, j)].ap, op0=ALU.mult, op1=ALU.add),
                       [S[(slot, j)], sel, acc[(d, j)]], [acc[(d, j)]])

            def cb_others(c, j, A, Dp, psA, Dtile):
                if j == 0 and nchunks[0] % 4 == 0:
                    sel_acc()
                chain_step(slot, j, A, Dp, psA, Dtile)
                if j == 1:
                    nchunks[0] += 2

            order = list(range(6, 12)) if d == 0 else list(range(11, 5, -1))
            for T in order:
                yield from gla_unit(slot, T, d, False, cb_others)
            assert nchunks[0] == 12
            sel_acc()
            for j in range(2):
                op("dve", lambda e: e.tensor_copy(out=S[(slot, j)].ap, in_=acc[(d, j)].ap), [acc[(d, j)]], [S[(slot, j)]])
            order = (4, 5) if d == 0 else (5, 4)
            for T in order:
                yield from gla_unit(slot, T, d, True,
                                    lambda c, j, A, Dp, psA, Dtile: chain_step(slot, j, A, Dp, psA, Dtile, save_to=sbf[(T, d, j, c)]))

        def prompt_outs():
            for T in range(4):
                yield from gla_out(T)

        def seq_stream(sq_i, slot0):
            subs = [prompt_chain(slot0, sq_i, 0), prompt_chain(slot0 + 1, sq_i, 1)]
            while subs:
                for g in list(subs):
                    try:
                        next(g)
                    except StopIteration:
                        subs.remove(g)
                yield
            for T in (2 * sq_i, 2 * sq_i + 1):
                yield from gla_out(T)

        def sample_stream():
            subs = [sample_chain(4, 0), sample_chain(5, 1)]
            while subs:
                for g in list(subs):
                    try:
                        next(g)
                    except StopIteration:
                        subs.remove(g)
                yield

        run_streams([sample_stream(), seq_stream(0, 0), seq_stream(1, 2)])

        keep45 = [gv_tm[4], gv_tm[5]] + [v for k, v in list(qtil.items()) + list(attm.items()) + list(sbf.items()) if k[0] in (4, 5)]
        keep45_ids = set(id(t) for t in keep45)
        P.free(*[t for t in list(gqT) + list(gkT) + list(gk_tm) + list(gv_tm) + [lrT, wal] if id(t) not in keep45_ids], *[k for kk in ktil for k in kk], *S.values(), *acc.values(), sel, *e_tm, *l_tm, *ew, *khat,
               *Dt, *EB, *EBn, *[t for t in list(qtil.values()) + list(attm.values()) + list(sbf.values()) if id(t) not in keep45_ids])
        ckpt(3)
        ssq.append(P.sb([128, 4], F32, "ssq"))
        rsh.append(P.sb([128, 4], F32, "rsh"))
        on1.append(P.sb([128, 4, 128], F32, "on1"))
        junk2.append(P.sb([128, 512], F32, "junk2b"))
        ogb.append(P.sb([128, 512], BF16, "ogb2"))

        ckT = [P.sb([128, 512], BF16, "ckT%d" % h) for h in range(4)]
        for h in range(4):
            P.dma("pool", ckT[h].ap, ckT_d[h], writes=[ckT[h]])
        cv1 = [P.sb([128, 4, 130], BF16, "cv1_%d" % t) for t in range(4)]
        for t in range(4):
            op("dve", lambda e: e.memset(cv1[t].ap[:, :, 128:130], 1.0), [], [cv1[t]])
            P.dma("pool", cv1[t].ap[:, :, 0:128], cv_d[t * 128:(t + 1) * 128, :].rearrange("p (h v) -> p h v", h=4), writes=[cv1[t]])
        woutb = P.sb([128, 8, D], BF16, "woutb")
        for nb in range(2):
            P.dma("pool", woutb.ap[:, :, nb * 512:(nb + 1) * 512],
                  wout_d[:, nb * 512:(nb + 1) * 512].rearrange("(k p) n -> p k n", p=128), writes=[woutb])

        Ebuf = [P.sb([128, 2, 256], BF16, "E%d" % i) for i in range(10)]
        ectr = [0]
        od = [[P.sb([128, 4, 128], F32, "od") for qb in range(2)] for i in range(3)]
        junk3 = P.sb([128, D], F32, "junk3")
        rr = [P.sb([128, 4], F32, "rr") for i in range(2)]
        t1 = [P.sb([128, 128], F32, "t1") for i in range(2)]
        fgain = P.sb([128, D], F32, "fgain")
        P.dma("sp", fgain.ap, fgain_d.partition_broadcast(128), writes=[fgain])
        xt = [P.sb([128, D], F32, "xt%d" % i) for i in range(2)]
        xn = [P.sb([128, D], F32, "xn%d" % i) for i in range(2)]
        fss = [P.sb([128, 1], F32, "fss") for i in range(2)]
        frs = [P.sb([128, 1], F32, "frs") for i in range(2)]

        def final_tile(T):
            x_t = xt[T % 2]
            x_n = xn[T % 2]
            f_s, f_r = fss[T % 2], frs[T % 2]
            ii = 0 if T < 4 else 1
            P.dma("sp", x_t.ap, xtm_d[T * 128:(T + 1) * 128, :], writes=[x_t])
            for _ in range(3):
                yield
            for nb in range(2):
                (ps, pid), = yield from get_banks(1)
                for mc in range(8):
                    op("pe", lambda e: e.matmul(ps.ap, lhsT=oT[T].ap[:, mc, :], rhs=woutb.ap[:, mc, nb * 512:(nb + 1) * 512],
                                                start=(mc == 0), stop=(mc == 7)), [oT[T], woutb], [ps], inc=(mc == 7))
                yield
                op("dve", lambda e: e.tensor_tensor(out=x_n.ap[:, nb * 512:(nb + 1) * 512], in0=ps.ap, in1=gt_bc[ii].ap[:, nb * 512:(nb + 1) * 512], op=ALU.mult),
                   [ps, gt_bc[ii]], [x_n])
                PS.put(pid)
            op("dve", lambda e: e.tensor_tensor(out=x_n.ap, in0=x_n.ap, in1=x_t.ap, op=ALU.add), [x_n, x_t], [x_n])
            yield
            op("act", lambda e: e.activation(out=junk3.ap, in_=x_n.ap, func=AF.Square, accum_out=f_s.ap), [x_n], [junk3, f_s])
            yield
            op("act", lambda e: e.activation(out=f_s.ap, in_=f_s.ap, func=AF.Ln, bias=epsb.ap, scale=1.0 / D), [f_s, epsb], [f_s])
            op("act", lambda e: e.activation(out=f_r.ap, in_=f_s.ap, func=AF.Exp, scale=-0.5), [f_s], [f_r])
            yield
            op("dve", lambda e: e.scalar_tensor_tensor(out=x_n.ap, in0=x_n.ap, scalar=f_r.ap[:, 0:1], in1=fgain.ap, op0=ALU.mult, op1=ALU.mult),
               [x_n, f_r, fgain], [x_n])
            dst = yp_d[T * 128:(T + 1) * 128, :] if T < 4 else ys_d[(T - 4) * 128:(T - 3) * 128, :]
            P.dma("sp", dst, x_n.ap, reads=[x_n])

        ctxs = []
        for sq_i in range(2):
            ctxs.append((sq_i * 256, [("d", T) for T in (2 * sq_i, 2 * sq_i + 1)], (2 * sq_i, 2 * sq_i + 1)))
        ctxs.append((512, [("c", t) for t in range(4)] + [("d", T) for T in range(4, 12)], (4, 5)))

        def kblock(h, kind, t):
            if kind == "c":
                return (ckT[h].ap[:, t * 128:(t + 1) * 128], ckT[h], cv1[t].ap[:, h, 0:129], cv1[t])
            return (dkT[h].ap[:, t * 128:(t + 1) * 128], dkT[h], dv1[t].ap[:, h, 0:129], dv1[t])

        def attention_all():
            jobs = []
            for ci, (q0, kbs, tiles) in enumerate(ctxs):
                for h in range(4):
                    for p in range(len(kbs) // 2):
                        jobs.append((ci, h, p))
            Es = {}

            def qk(job):
                ci, h, p = job
                q0, kbs, tiles = ctxs[ci]
                q_ap = dqT[h].ap[:, q0:q0 + 256]
                Em = []
                for m in range(2):
                    (pss, psid), = yield from get_banks(1)
                    for i in range(2):
                        k_ap, k_t, _, _ = kblock(h, *kbs[2 * p + i])
                        op("pe", lambda e: e.matmul(pss.ap[:, i * 256:(i + 1) * 256], lhsT=k_ap[m * 64:(m + 1) * 64, :],
                                                    rhs=q_ap[m * 64:(m + 1) * 64, :], start=True, stop=True), [k_t, dqT[h]], [pss], inc=(i == 1))
                    E = Ebuf[ectr[0] % len(Ebuf)]
                    ectr[0] += 1
                    op("act", lambda e: e.activation(out=E.ap.rearrange("p i q -> p (i q)"), in_=pss.ap, func=AF.Exp, scale=SCALE_ATT),
                       [pss], [E])
                    PS.put(psid)
                    Em.append(E)
                Es[job] = Em

            obanks = None
            DEPTH = 3
            for k in range(min(DEPTH, len(jobs))):
                yield from qk(jobs[k])
            for n, job in enumerate(jobs):
                ci, h, p = job
                q0, kbs, tiles = ctxs[ci]
                nkb = len(kbs)
                if n + DEPTH < len(jobs):
                    yield from qk(jobs[n + DEPTH])
                if p == 0:
                    obanks = yield from get_banks(2)
                for i in range(2):
                    kb = 2 * p + i
                    _, _, v_ap, v_t = kblock(h, *kbs[kb])
                    for qb in range(2):
                        for m in range(2):
                            ob, _ = obanks[qb]
                            E = Es[job][m]
                            op("pe", lambda e: e.matmul(ob.ap[:, m * 129:(m + 1) * 129], lhsT=E.ap[:, i, qb * 128:(qb + 1) * 128], rhs=v_ap,
                                                        start=(kb == 0 and m == 0), stop=(kb == nkb - 1 and m == 1), skip_group_check=True),
                               [E, v_t], [ob], inc=(kb == nkb - 1 or (i == 1 and qb == 1 and m == 1)))
                del Es[job]
                yield
                if p == nkb // 2 - 1:
                    oo = od[ci]
                    osb = osbs[nctr[0] % 2]
                    nctr[0] += 1
                    for qb in range(2):
                        ob, idb = obanks[qb]
                        op("act", lambda e: e.copy(out=osb.ap[:, qb * 2:qb * 2 + 2, 0:129], in_=ob.ap[:, 0:258].rearrange("p (m c) -> p m c", m=2)),
                           [ob], [osb])
                        PS.put(idb)
                    for qb in range(2):
                        r_, t_ = rr[qb], t1[qb]
                        op("dve", lambda e: e.reciprocal(out=r_.ap[:, 0:1], in_=osb.ap[:, qb * 2, 128:129]), [osb], [r_])
                        op("dve", lambda e: e.reciprocal(out=r_.ap[:, 1:2], in_=osb.ap[:, qb * 2 + 1, 128:129]), [osb], [r_])
                        op("dve", lambda e: e.tensor_tensor(out=r_.ap[:, 2:3], in0=r_.ap[:, 1:2], in1=nlam.ap, op=ALU.mult), [r_, nlam], [r_])
                        op("dve", lambda e: e.tensor_scalar(out=t_.ap, in0=osb.ap[:, qb * 2, 0:128], scalar1=r_.ap[:, 0:1], scalar2=None, op0=ALU.mult),
                           [osb, r_], [t_])
                        op("dve", lambda e: e.scalar_tensor_tensor(out=oo[qb].ap[:, h, :], in0=osb.ap[:, qb * 2 + 1, 0:128], scalar=r_.ap[:, 2:3], in1=t_.ap,
                                                                   op0=ALU.mult, op1=ALU.add), [osb, r_, t_], [oo[qb]])
                    if h == 3:
                        for qb in range(2):
                            post_q.append((oo[qb], tiles[qb]))
                    yield

        post_q = []
        osbs = [P.sb([128, 4, 130], F32, "osb%d" % i) for i in range(2)]
        nctr = [0]

        def post_stream():
            def one_tile(o_t, T):
                yield from head_post([(o_t, o_t.ap, [0, 1, 2, 3])], T, 1, tset=T % 2)
                while T >= 4 and T not in gla_done:
                    yield
                yield from final_tile(T)

            done = 0
            while done < 6:
                if not post_q:
                    yield
                    continue
                subs = []
                while post_q and len(subs) < 2:
                    o_t, T = post_q.pop(0)
                    subs.append(one_tile(o_t, T))
                    done += 1
                while subs:
                    for g in list(subs):
                        try:
                            next(g)
                        except StopIteration:
                            subs.remove(g)
                    yield

        gla_done = set()

        def gla45():
            for T in (4, 5):
                yield from gla_out(T, tset=2)
                gla_done.add(T)

        run_streams([attention_all(), gla45(), post_stream()])
        P.free(*keep45)

        P.finish("sp")
        print("[build] instructions=%d waits=%d sbuf_peak=%d" % (P.n_ins, P.n_wait, P.peak))
    except _Stop:
        pass
    return nc


def _consts():
    c = np.zeros((128, NCONST), np.float32)
    s = np.arange(128)[:, None]
    t = np.arange(128)[None, :]
    same = (s // 64) == (t // 64)
    c[:, C_LF01:C_LF01 + 128] = (same & (s <= t))
    c[:, C_LB01:C_LB01 + 128] = (same & (s >= t))
    c[:, C_LF16:C_LF16 + 128] = (same & (s <= t)) * (-1.0 / 16)
    c[:, C_LB16:C_LB16 + 128] = (same & (s >= t)) * (-1.0 / 16)
    c[:, C_MSF16:C_MSF16 + 128] = (same & (s > t)) * (-1.0 / 16)
    c[:, C_MSB16:C_MSB16 + 128] = (same & (s < t)) * (-1.0 / 16)
    c[:, C_ID:C_ID + 128] = np.eye(128)
    R = np.zeros((128, 128), np.float32)
    for p in range(128):
        d = p % 32
        if d < 16:
            R[p + 16, p] = -1.0
        else:
            R[p - 16, p] = 1.0
    c[:, C_R:C_R + 128] = R
    c[:, C_ONES:C_ONES + 128] = 1.0
    c[:, C_CI16] = (np.arange(128) < 64) * (-1.0 / 16)
    c[:, C_CI16 + 1] = (np.arange(128) >= 64) * (-1.0 / 16)
    c[:, C_MSF16F:C_MSF16F + 128] = (s > t) * (-1.0 / 16)
    c[:, C_MSB16F:C_MSB16F + 128] = (s < t) * (-1.0 / 16)
    c[:, C_ONE16] = -1.0 / 16
    return c


def _rope_tables(tok_idx):
    row = (tok_idx // 64).astype(np.float32)
    col = (tok_idx % 64).astype(np.float32)
    inv_freq = (np.float32(10000.0) ** (-np.arange(16, dtype=np.float32) / np.float32(16))).astype(np.float32)
    cos = np.zeros((128, len(tok_idx)), np.float32)
    sin = np.zeros((128, len(tok_idx)), np.float32)
    for p in range(128):
        d = p % 64
        pos = row if d < 32 else col
        ang = (pos * inv_freq[d % 16]).astype(np.float32)
        cos[p] = np.cos(ang)
        sin[p] = np.sin(ang)
    return cos, sin


def make_in_maps(x_prompt, x_sample, cache_diff_k, cache_diff_v, state_gla_fwd, state_gla_bwd, c, c_ctx,
                 norm_gain, w_mod, b_mod, w_in, w_gla_alpha, b_gla_alpha, diff_lambda,
                 gla_head_gain, diff_head_gain, w_out, final_gain):
    f = lambda a: np.ascontiguousarray(np.asarray(a, dtype=np.float32))
    x_prompt, x_sample = f(x_prompt), f(x_sample)
    consts = _consts()
    wal = np.concatenate([np.transpose(f(w_gla_alpha)[0], (1, 0, 2)), f(b_gla_alpha)[0][None]], axis=0)
    hgain = np.concatenate([np.tile(f(gla_head_gain)[0], 4), np.tile(f(diff_head_gain)[0], 4)])
    shared = {
        "gainT": f(f(norm_gain)[0].reshape(8, 128).T),
        "wmod": f(w_mod)[0], "bmod": f(b_mod)[0], "win": f(w_in)[0], "wout": f(w_out)[0],
        "wlr": f(f(w_in)[0][:, 1024:1056].reshape(8, 128, 32).transpose(1, 0, 2).reshape(128, 256)),
        "walpha": f(wal), "dlam": f(diff_lambda)[0].reshape(256), "hgain": f(hgain), "fgain": f(final_gain),
        "consts": consts,
    }
    maps = []
    for j in range(N_CORES):
        b, r = j // 4, j % 4
        own = np.arange(r * 256, (r + 1) * 256)
        oth = np.concatenate([np.arange(0, r * 256), np.arange((r + 1) * 256, 1024)])
        tok = np.concatenate([own, oth])
        X = np.concatenate([x_prompt[2 * j], x_prompt[2 * j + 1], x_sample[b][tok]], axis=0)
        cv2 = np.stack([f(c_ctx), f(c)[b]], axis=0)
        cos, sin = _rope_tables(tok)
        sel = np.zeros((128, 8), np.float32)
        sel[:, r] = 1.0
        sel[:, 4 + (3 - r)] = 1.0
        m = dict(shared)
        m.update({
            "xT": f(X.T.reshape(8, 128, NTOK)),
            "xtm": f(X[:768]),
            "cvT": f(cv2.T.reshape(8, 128, 2).transpose(1, 0, 2)),
            "ckT": f(np.transpose(f(cache_diff_k)[b, 0].reshape(512, 4, 128), (1, 2, 0))),
            "cv": f(f(cache_diff_v)[b, 0].reshape(512, 512)),
            "s0": f(np.stack([f(state_gla_fwd)[b, 0].reshape(2, 128, 128), f(state_gla_bwd)[b, 0].reshape(2, 128, 128)], axis=0)),
            "sel": sel, "ropecos": cos, "ropesin": sin,
        })
        maps.append(m)
    return maps


def assemble(results):
    y_prompt = np.zeros((16, 256, D), np.float32)
    y_sample = np.zeros((2, 1024, D), np.float32)
    ndk = np.zeros((16, 1, 256, 4, 2, 64), np.float32)
    ndv = np.zeros((16, 1, 256, 4, 128), np.float32)
    nsf = np.zeros((16, 1, 4, 64, 128), np.float32)
    nsb = np.zeros((16, 1, 4, 64, 128), np.float32)
    for j, res in enumerate(results):
        b, r = j // 4, j % 4
        y_prompt[2 * j:2 * j + 2] = np.asarray(res["yp"]).reshape(2, 256, D)
        y_sample[b, r * 256:(r + 1) * 256] = np.asarray(res["ys"])
        ndk[2 * j:2 * j + 2, 0] = np.asarray(res["ndk"]).reshape(2, 256, 4, 2, 64)
        ndv[2 * j:2 * j + 2, 0] = np.asarray(res["ndv"]).reshape(2, 256, 4, 128)
        ns = np.asarray(res["ns"])
        nsf[2 * j:2 * j + 2, 0] = ns[0].reshape(2, 4, 64, 128)
        nsb[2 * j:2 * j + 2, 0] = ns[1].reshape(2, 4, 64, 128)
    return (y_prompt, y_sample, ndk, ndv, nsf, nsb)


_NC_CACHE = {}


def kernel(**inputs):
    maps = make_in_maps(**inputs)
    if "nc" not in _NC_CACHE:
        _NC_CACHE["nc"] = build_program()
    nc = _NC_CACHE["nc"]
    res = run_bass_kernel_spmd(nc, maps, core_ids=list(range(N_CORES)))
    return assemble(res.results)
```

```python
import math
from contextlib import ExitStack

import numpy as np
import concourse.bass as bass
import concourse.mybir as mybir
from concourse.bass_utils import run_bass_kernel_spmd

F32 = mybir.dt.float32
BF16 = mybir.dt.bfloat16
AF = mybir.ActivationFunctionType
ALU = mybir.AluOpType
AX = mybir.AxisListType

N_CORES = 8
D = 1024
EPS = 1e-6
NTOK = 1536
NTILE = 12
PROJ = 3616
LAM_INIT = 0.2
SCALE_GLA = 0.125
SCALE_ATT = 0.125

C_LF01, C_LB01, C_LF16, C_LB16, C_MSF16, C_MSB16, C_ID, C_R, C_ONES = [i * 128 for i in range(9)]
C_CI16 = 9 * 128
C_MSF16F = 9 * 128 + 2
C_MSB16F = C_MSF16F + 128
C_ONE16 = C_MSB16F + 128
NCONST = C_ONE16 + 1
ARENA_BYTES = 204 * 1024
GTENG = "act"
KH_ENG = "dve"
QK_ENG = "pool"
SV_ENG = "act"
OG_ENG = "dve"
SP_SLOTS = 6


class Tr:
    __slots__ = ("w", "r", "excl")

    def __init__(self, excl=False):
        self.w = None
        self.r = {}
        self.excl = excl


class TT:
    __slots__ = ("ap", "tr")

    def __init__(self, ap, tr):
        self.ap = ap
        self.tr = tr

    def __getitem__(self, idx):
        return TT(self.ap[idx], self.tr)

    def v(self, fn):
        return TT(fn(self.ap), self.tr)


class Prog:
    def __init__(self, nc, es, n_slots=6):
        self.nc = nc
        self.es = es
        self.eng = {"pe": nc.tensor, "act": nc.scalar, "dve": nc.vector, "pool": nc.gpsimd, "sp": nc.sync}
        self.sem = {}
        self.count = {}
        self.known = {k: {} for k in self.eng}
        for k in self.eng:
            self.sem[k] = es.enter_context(nc.semaphore("s_" + k))
            self.count[k] = 0
        self.slots = {}
        self.slot_next = {}
        for q in ("sp", "pool", "act"):
            self.slots[q] = []
            for i in range(SP_SLOTS if q == 'sp' else n_slots):
                key = "d_%s_%d" % (q, i)
                self.sem[key] = es.enter_context(nc.semaphore(key))
                self.count[key] = 0
                self.slots[q].append(key)
            self.slot_next[q] = 0
        self.n_wait = 0
        self.n_ins = 0
        self.freed_events = {}
        self.tcount = 0
        self.regions = {}

    def init_arena(self, nbytes):
        self.arena = self.nc.alloc_sbuf_tensor("arena", [128, nbytes], mybir.dt.uint8)
        self.free_list = [[0, nbytes, {}]]
        self.peak = 0
        self.used = 0

    def sb(self, shape, dt, name=None):
        esz = 4 if dt == F32 else 2
        per_part = esz
        for s in shape[1:]:
            per_part *= s
        size = (per_part + 63) // 64 * 64
        fl = self.free_list
        for i in range(len(fl)):
            tot = 0
            j = i
            while j < len(fl) and (j == i or fl[j][0] == fl[j - 1][1]):
                tot += fl[j][1] - fl[j][0]
                if tot >= size:
                    break
                j += 1
            if tot >= size:
                start = fl[i][0]
                evs = {}
                need = size
                k = i
                new_items = []
                while need > 0:
                    s0, e0, ev = fl[k]
                    for kk, vv in ev.items():
                        evs[kk] = max(evs.get(kk, 0), vv)
                    take = min(need, e0 - s0)
                    need -= take
                    if take < e0 - s0:
                        new_items.append([s0 + take, e0, ev])
                    k += 1
                fl[i:k] = new_items
                tr = Tr()
                tr.r = evs
                v = self.arena[:, start:start + per_part].bitcast(dt)
                if len(shape) > 2:
                    names = " ".join("d%d" % n for n in range(1, len(shape)))
                    kw = {"d%d" % n: shape[n] for n in range(1, len(shape) - 1)}
                    v = v.rearrange("p (%s) -> p %s" % (names, names), **kw)
                if shape[0] < 128:
                    v = v[0:shape[0]]
                t = TT(v, tr)
                self.regions[id(tr)] = (start, start + size)
                self.used += size
                self.peak = max(self.peak, self.used)
                return t
        raise RuntimeError("arena out of memory allocating %s %s (used=%d)" % (name, shape, self.used))

    def free(self, *tiles):
        for t in tiles:
            s0, e0 = self.regions.pop(id(t.tr))
            ev = dict(t.tr.r)
            if t.tr.w is not None:
                k, v = t.tr.w
                ev[k] = max(ev.get(k, 0), v)
            self.free_list.append([s0, e0, ev])
            self.used -= e0 - s0
        self.free_list.sort(key=lambda x: x[0])

    def _wait(self, eng, key, val):
        if self.known[eng].get(key, 0) >= val:
            return
        self.eng[eng].wait_ge(self.sem[key], val)
        self.known[eng][key] = val
        self.n_wait += 1

    def _deps(self, eng, reads, writes):
        evs = {}
        for t in reads:
            if t.tr.w is not None:
                k, v = t.tr.w
                evs[k] = max(evs.get(k, 0), v)
            if t.tr.excl:
                for k, v in t.tr.r.items():
                    if k != eng:
                        evs[k] = max(evs.get(k, 0), v)
        for t in writes:
            if t.tr.w is not None:
                k, v = t.tr.w
                evs[k] = max(evs.get(k, 0), v)
            for k, v in t.tr.r.items():
                evs[k] = max(evs.get(k, 0), v)
        for k, v in evs.items():
            if k == "pe" and eng == "pe":
                continue
            self._wait(eng, k, v)

    def _commit(self, ev, reads, writes):
        k, v = ev
        for t in reads:
            t.tr.r[k] = max(t.tr.r.get(k, 0), v)
        for t in writes:
            t.tr.w = ev
            t.tr.r = {}

    def op(self, eng, fn, reads=(), writes=(), inc=True):
        self._deps(eng, reads, writes)
        ins = fn(self.eng[eng])
        self.n_ins += 1
        if inc:
            self.count[eng] += 1
            ins.then_inc(self.sem[eng], 1)
            self._commit((eng, self.count[eng]), reads, writes)
        else:
            assert eng == "pe"
            self._commit((eng, self.count[eng] + 1), reads, writes)

    def dma(self, q, out_ap, in_ap, reads=(), writes=()):
        self._deps(q, reads, writes)
        i = self.slot_next[q]
        self.slot_next[q] = (i + 1) % len(self.slots[q])
        key = self.slots[q][i]
        if self.count[key] > 0:
            self._wait(q, key, 16 * self.count[key])
        ins = self.eng[q].dma_start(out=out_ap, in_=in_ap)
        self.count[key] += 1
        ins.then_inc(self.sem[key], 16)
        self.n_ins += 1
        self._commit((key, 16 * self.count[key]), reads, writes)

    def finish(self, eng="sp"):
        for q in self.slots:
            for key in self.slots[q]:
                if self.count[key] > 0:
                    self._wait(eng, key, 16 * self.count[key])


class PsumPool:
    def __init__(self, nc, n=8):
        self.banks = []
        for i in range(n):
            h = nc.alloc_psum_tensor("psb%d" % i, [128, 512], F32)
            self.banks.append(TT(h[:, :], Tr(excl=True)))
        self.free = list(range(n))

    def get(self):
        i = self.free.pop(0)
        b = self.banks[i]
        b_id = i
        t = TT(b.ap, b.tr)
        return t, b_id

    def put(self, b_id):
        self.free.append(b_id)


class _Stop(Exception):
    pass


def build_program(stop=99):
    nc = bass.Bass("TRN2", target_bir_lowering=False)

    def din(name, shape):
        return nc.dram_tensor(name, list(shape), F32, kind="ExternalInput").ap()

    def dout(name, shape):
        return nc.dram_tensor(name, list(shape), F32, kind="ExternalOutput").ap()

    xT_d = din("xT", [8, 128, NTOK])
    xtm_d = din("xtm", [768, D])
    cvT_d = din("cvT", [128, 8, 2])
    gainT_d = din("gainT", [128, 8])
    wmod_d = din("wmod", [D, 3 * D])
    bmod_d = din("bmod", [3 * D])
    win_d = din("win", [D, PROJ])
    wout_d = din("wout", [D, D])
    wlr_d = din("wlr", [128, 8 * 32])
    wal_d = din("walpha", [17, 2, 256])
    dlam_d = din("dlam", [256])
    hgain_d = din("hgain", [D])
    fgain_d = din("fgain", [D])
    ckT_d = din("ckT", [4, 128, 512])
    cv_d = din("cv", [512, 512])
    s0_d = din("s0", [2, 2, 128, 128])
    sel_d = din("sel", [128, 8])
    cos_d = din("ropecos", [128, 1024])
    sin_d = din("ropesin", [128, 1024])
    const_d = din("consts", [128, NCONST])

    yp_d = dout("yp", [512, D])
    ys_d = dout("ys", [256, D])
    ndk_d = dout("ndk", [512, 512])
    ndv_d = dout("ndv", [512, 512])
    ns_d = dout("ns", [2, 2, 2, 128, 128])

    es = ExitStack()
    try:
      with es:
        P = Prog(nc, es)
        P.init_arena(ARENA_BYTES)
        PS = PsumPool(nc)

        def ckpt(k):
            if stop <= k:
                P.finish("sp")
                print("[build] STOP at %s instructions=%d" % (k, P.n_ins))
                raise _Stop()

        def op(eng, fn, reads=(), writes=(), inc=True):
            P.op(eng, fn, reads, writes, inc)

        def run_streams(gens):
            gens = list(gens)
            while gens:
                for g in list(gens):
                    try:
                        next(g)
                    except StopIteration:
                        gens.remove(g)

        PS_RESERVE = [0]

        def get_banks(n):
            while len(PS.free) < n + PS_RESERVE[0]:
                yield
            return [PS.get() for _ in range(n)]

        consts = P.sb([128, NCONST], F32, "consts")
        constb = P.sb([128, NCONST], BF16, "constb")
        P.dma("sp", consts.ap, const_d, writes=[consts])
        op("dve", lambda e: e.tensor_copy(out=constb.ap, in_=consts.ap), [consts], [constb])
        epsb = P.sb([128, 1], F32, "epsb")
        op("dve", lambda e: e.memset(epsb.ap, EPS), [], [epsb])
        junk = P.sb([128, D], F32, "junk")

        cvT = P.sb([128, 8, 2], F32, "cvT")
        P.dma("sp", cvT.ap, cvT_d, writes=[cvT])
        gainT = P.sb([128, 8], F32, "gainT")
        P.dma("sp", gainT.ap, gainT_d, writes=[gainT])
        bmod2 = P.sb([2, 3 * D], F32, "bmod2")
        P.dma("sp", bmod2.ap, bmod_d.partition_broadcast(2), writes=[bmod2])
        cs = P.sb([128, 8, 2], BF16, "cs")
        op("act", lambda e: e.activation(out=cs.ap, in_=cvT.ap, func=AF.Silu), [cvT], [cs])
        mrow = P.sb([2, 3 * D], F32, "mrow")
        wmb = [P.sb([128, 8, 512], BF16, "wmb%d" % i) for i in range(4)]
        wbufs = [P.sb([128, 8, 512], BF16, "wbuf%d" % i) for i in range(3)]
        wAB = P.sb([128, 8, 544], BF16, "wAB")
        wctr = [0]

        def load_w(src_d, c0, n, wb=None, after=()):
            if wb is None:
                wb = wbufs[wctr[0] % len(wbufs)]
                wctr[0] += 1
            src = src_d[:, c0:c0 + n].rearrange("(k p) n -> p k n", p=128)
            P.dma("pool", wb.ap[:, :, 0:n], src, reads=list(after), writes=[wb])
            return wb

        xu = [P.sb([128, 8, 512], F32, "xu%d" % b) for b in range(3)]
        sqb = [P.sb([128, 8, 512], BF16, "sqb%d" % i) for i in range(2)]
        rs1 = [P.sb([128, 512], F32, "rs1_%d" % i) for i in range(2)]
        rstd = [P.sb([128, 512], F32, "rstd%d" % i) for i in range(2)]
        xT_v = xT_d.rearrange("k p t -> p k t")
        xdone = TT(None, Tr())
        for b in range(3):
            P.dma("sp", xu[b].ap, xT_v[:, :, b * 512:(b + 1) * 512], writes=[xu[b]] + ([xdone] if b == 2 else []))
        for nb in range(4):
            load_w(wmod_d, nb * 512, 512, wmb[nb], after=[xdone])

        def mod_rows(wb, nb):
            (ps, pid), = yield from get_banks(1)
            for kc in range(8):
                op("pe", lambda e: e.matmul(ps.ap[0:2, :], lhsT=cs.ap[:, kc, :], rhs=wb.ap[:, kc, :],
                                            start=(kc == 0), stop=(kc == 7)), [cs, wb], [ps], inc=(kc == 7))
            yield
            op("dve", lambda e: e.tensor_tensor(out=mrow.ap[:, nb * 512:(nb + 1) * 512], in0=ps.ap[0:2, :],
                                                in1=bmod2.ap[:, nb * 512:(nb + 1) * 512], op=ALU.add), [ps, bmod2], [mrow])
            PS.put(pid)
            yield

        def stream_x():
            for b in range(3):
                xs = xu[b]
                sq = sqb[b % 2]
                for hf in range(2):
                    op("act", lambda e: e.activation(out=sq.ap[:, hf * 4:(hf + 1) * 4, :], in_=xs.ap[:, hf * 4:(hf + 1) * 4, :], func=AF.Square), [xs], [sq])
                yield
                (ps, pid), = yield from get_banks(1)
                for kc in range(8):
                    op("pe", lambda e: e.matmul(ps.ap, lhsT=constb.ap[:, C_ONES:C_ONES + 128], rhs=sq.ap[:, kc, :],
                                                start=(kc == 0), stop=(kc == 7)), [constb, sq], [ps], inc=(kc == 7))
                yield
                op("act", lambda e: e.activation(out=rs1[b % 2].ap, in_=ps.ap, func=AF.Ln, bias=epsb.ap, scale=1.0 / D),
                   [ps, epsb], [rs1[b % 2]])
                PS.put(pid)
                op("act", lambda e: e.activation(out=rstd[b % 2].ap, in_=rs1[b % 2].ap, func=AF.Exp, scale=-0.5), [rs1[b % 2]], [rstd[b % 2]])
                yield
                for hf in range(2):
                    op("dve", lambda e: e.tensor_tensor(out=xs.ap[:, hf * 4:(hf + 1) * 4, :], in0=xs.ap[:, hf * 4:(hf + 1) * 4, :],
                                                        in1=rstd[b % 2].ap.unsqueeze(1).to_broadcast([128, 4, 512]), op=ALU.mult),
                       [xs, rstd[b % 2]], [xs])
                yield

        def stream_m():
            for nb in range(4):
                yield from mod_rows(wmb[nb], nb)

        run_streams([stream_x()])
        run_streams([stream_m()])
        P.free(sqb[0], sqb[1], rs1[0], rs1[1], rstd[0], rstd[1], *wmb)

        modT = P.sb([128, 2, 8, 2], F32, "modT")
        ps, pid = PS.get()
        for which in range(2):
            for kc in range(8):
                o = (which * 8 + kc) * 2
                c0 = which * D + kc * 128
                op("pe", lambda e: e.transpose(ps.ap[:, o:o + 2], mrow.ap[0:2, c0:c0 + 128], consts.ap[0:2, C_ID:C_ID + 2]),
                   [mrow, consts], [ps], inc=(which == 1 and kc == 7))
        op("dve", lambda e: e.tensor_copy(out=modT.ap.rearrange("p a b c -> p (a b c)"), in_=ps.ap[:, 0:32]), [ps], [modT])
        PS.put(pid)
        aT = P.sb([128, 8, 2], F32, "aT")
        for i in range(2):
            op("dve", lambda e: e.scalar_tensor_tensor(out=aT.ap[:, :, i], in0=modT.ap[:, 1, :, i], scalar=1.0,
                                                       in1=gainT.ap, op0=ALU.add, op1=ALU.mult), [modT, gainT], [aT])
        gt_bc = [P.sb([128, D], F32, "gtbc%d" % i) for i in range(2)]

        def gate_rows():
            for nb in (4, 5):
                yield from mod_rows(gate_wb[nb - 4], nb)

        def gate_bcast():
            for i in range(2):
                for nb in range(2):
                    (ps, pid), = yield from get_banks(1)
                    op("pe", lambda e: e.matmul(ps.ap, lhsT=selr.ap[:, i, :], rhs=mrow.ap[0:2, 2 * D + nb * 512:2 * D + (nb + 1) * 512],
                                                start=True, stop=True), [selr, mrow], [ps])
                    op("act", lambda e: e.copy(out=gt_bc[i].ap[:, nb * 512:(nb + 1) * 512], in_=ps.ap), [ps], [gt_bc[i]])
                    PS.put(pid)
            P.free(mrow, bmod2, selr, cvT, cs)

        hT2 = [P.sb([128, 8, 512], BF16, "hT%d" % b) for b in range(3)]
        hTa = [TT(hT2[b].ap, Tr()) for b in range(3)]
        hTd = [TT(hT2[b].ap, Tr()) for b in range(3)]
        for b in range(3):
            hTa[b].tr.r = dict(hT2[b].tr.r)
            hTd[b].tr.r = dict(hT2[b].tr.r)
        for b in range(3):
            ii = 0 if b < 1 else 1
            hb = hT2[b]
            for kc in range(8):
                if kc % 3 == 0:
                    op("act", lambda e: e.activation(out=hb.ap[:, kc, :], in_=xu[b].ap[:, kc, :], func=AF.Identity,
                                                     bias=modT.ap[:, 0, kc, ii:ii + 1], scale=aT.ap[:, kc, ii:ii + 1]), [xu[b], modT, aT], [hTa[b]])
                else:
                    op("dve", lambda e: e.tensor_scalar(out=hb.ap[:, kc, :], in0=xu[b].ap[:, kc, :], scalar1=aT.ap[:, kc, ii:ii + 1],
                                                        scalar2=modT.ap[:, 0, kc, ii:ii + 1], op0=ALU.mult, op1=ALU.add), [xu[b], modT, aT], [hTd[b]])
        P.free(*xu)
        P.free(gainT, aT)
        ckpt(1)

        dlam = P.sb([128, 256], F32, "dlam")
        P.dma("sp", dlam.ap, dlam_d.partition_broadcast(128), writes=[dlam])
        lam2 = P.sb([128, 2], F32, "lam2")
        lscr = P.sb([128, 64], F32, "lscr")
        for i in range(2):
            op("dve", lambda e: e.tensor_tensor(out=lscr.ap, in0=dlam.ap[:, (2 * i) * 64:(2 * i + 1) * 64],
                                                in1=dlam.ap[:, (2 * i + 1) * 64:(2 * i + 2) * 64], op=ALU.mult), [dlam], [lscr])
            op("dve", lambda e: e.tensor_reduce(out=lam2.ap[:, i:i + 1], in_=lscr.ap, axis=AX.X, op=ALU.add), [lscr], [lam2])
        lam2e = P.sb([128, 2], F32, "lam2e")
        op("act", lambda e: e.activation(out=lam2e.ap, in_=lam2.ap, func=AF.Exp), [lam2], [lam2e])
        nlam = P.sb([128, 1], F32, "nlam")
        op("dve", lambda e: e.tensor_tensor(out=nlam.ap, in0=lam2e.ap[:, 1:2], in1=lam2e.ap[:, 0:1], op=ALU.subtract), [lam2e], [nlam])
        op("dve", lambda e: e.tensor_scalar(out=nlam.ap, in0=nlam.ap, scalar1=-LAM_INIT, scalar2=None, op0=ALU.add), [nlam], [nlam])
        P.free(dlam, lam2, lscr, lam2e)

        hgain = P.sb([128, D], F32, "hgain")
        P.dma("sp", hgain.ap, hgain_d.partition_broadcast(128), writes=[hgain])
        op("dve", lambda e: e.tensor_scalar(out=hgain.ap[:, 512:1024], in0=hgain.ap[:, 512:1024], scalar1=1.0 - LAM_INIT,
                                            scalar2=None, op0=ALU.mult), [hgain], [hgain])

        selr = P.sb([2, 2, 128], F32, "selr")
        op("dve", lambda e: e.memset(selr.ap[:, 0, :], 0.0), [], [selr])
        op("dve", lambda e: e.memset(selr.ap[0:1, 0, :], 1.0), [], [selr])
        op("dve", lambda e: e.memset(selr.ap[:, 1, :], 1.0), [], [selr])
        op("dve", lambda e: e.memset(selr.ap[0:1, 1, :], 0.0), [], [selr])
        def proj_fm(wb, c0, m, ranges):
            outs = [PS.get() for _ in ranges]
            for kc in range(8):
                for ri, (tok0, ntok) in enumerate(ranges):
                    hb = hT2[tok0 // 512]
                    o = tok0 % 512
                    assert o + ntok <= 512
                    deps = [hTa[tok0 // 512], hTd[tok0 // 512]]
                    out_ps = outs[ri][0]
                    op("pe", lambda e: e.matmul(out_ps.ap[0:m, 0:ntok], lhsT=wb.ap[:, kc, c0:c0 + m], rhs=hb.ap[:, kc, o:o + ntok],
                                                start=(kc == 0), stop=(kc == 7)), [wb] + deps, [out_ps], inc=(kc == 7))
            return outs

        def proj_tm(wb, c0, n, T, out_ps):
            hb = hT2[T // 4]
            o = (T % 4) * 128
            for kc in range(8):
                op("pe", lambda e: e.matmul(out_ps.ap[:, 0:n], lhsT=hb.ap[:, kc, o:o + 128],
                                            rhs=wb.ap[:, kc, c0:c0 + n], start=(kc == 0), stop=(kc == 7)),
                   [wb, hTa[T // 4], hTd[T // 4]], [out_ps], inc=(kc == 7))

        evac_ctr = [1]

        def evac(out_t, out_ap, in_ap, in_t, scale=None):
            evac_ctr[0] += 1
            if evac_ctr[0] % 2 == 0:
                if scale is None:
                    op("act", lambda e: e.copy(out=out_ap, in_=in_ap), [in_t], [out_t])
                else:
                    op("act", lambda e: e.mul(out_ap, in_ap, scale), [in_t], [out_t])
            else:
                if scale is None:
                    op("dve", lambda e: e.tensor_copy(out=out_ap, in_=in_ap), [in_t], [out_t])
                else:
                    op("dve", lambda e: e.tensor_scalar(out=out_ap, in0=in_ap, scalar1=scale, scalar2=None, op0=ALU.mult),
                       [in_t], [out_t])

        OWN_RANGES = [(0, 512), (512, 256)]
        ALL_RANGES = [(0, 512), (512, 512), (1024, 512)]

        lrT = P.sb([17, 2, NTOK], BF16, "lrT")
        op("dve", lambda e: e.memset(lrT.ap, 1.0), [], [lrT])
        wal = P.sb([17, 2, 256], BF16, "wal")
        P.dma("pool", wal.ap, wal_d, writes=[wal])

        ckpt(1.5)
        gqT = [P.sb([128, 768], BF16, "gqT%d" % j) for j in range(2)]
        gkT = [P.sb([128, 768], BF16, "gkT%d" % j) for j in range(2)]
        gk_tm = [P.sb([128, 256], F32, "gktm%d" % t) for t in range(NTILE)]
        wb = wAB
        P.dma("pool", wb.ap[:, :, 0:512], win_d[:, 0:512].rearrange("(k p) n -> p k n", p=128), writes=[wb])
        P.dma("pool", wb.ap[:, :, 512:544], wlr_d.rearrange("p (k n) -> p k n", k=8), writes=[wb])
        gate_wb = [load_w(wmod_d, nb * 512, 512) for nb in (4, 5)]
        lr_tm = [P.sb([128, 32], BF16, "lrtm%d" % t) for t in range(NTILE)]
        gkb = [P.sb([128, 256], BF16, "gkb%d" % i) for i in range(3)]
        pend_tr = []
        pend_lr = []
        def fm_gq():
            for j in range(2):
                outs = proj_fm(wb, j * 128, 128, OWN_RANGES)
                for (ps, pid), (t0, nt) in zip(outs, OWN_RANGES):
                    evac(gqT[j], gqT[j].ap[:, t0:t0 + nt], ps.ap[:, 0:nt], ps, scale=SCALE_GLA)
                    PS.put(pid)

        for T in range(NTILE):
            if T == 6:
                fm_gq()
            ps, pid = PS.get()
            proj_tm(wb, 256, 288, T, ps)
            evac(gk_tm[T], gk_tm[T].ap, ps.ap[:, 0:256], ps)
            evac(lr_tm[T], lr_tm[T].ap, ps.ap[:, 256:288], ps)
            if T < 6:
                kb_ = gkb[T % 3]
                evac(kb_, kb_.ap, ps.ap[:, 0:256], ps)

                def do_tr(T=T, kb_=kb_):
                    pst, ptid = PS.get()
                    for j in range(2):
                        op("pe", lambda e: e.matmul(pst.ap[:, j * 128:(j + 1) * 128], lhsT=kb_.ap[:, j * 128:(j + 1) * 128],
                                                    rhs=constb.ap[:, C_ID:C_ID + 128], start=True, stop=True), [kb_, constb], [pst], inc=(j == 1))
                    for j in range(2):
                        evac(gkT[j], gkT[j].ap[:, T * 128:(T + 1) * 128], pst.ap[:, j * 128:(j + 1) * 128], pst)
                    PS.put(ptid)
                pend_tr.append(do_tr)
            PS.put(pid)
            if len(pend_tr) > 2 or (T >= 6 and pend_tr):
                pend_tr.pop(0)()
            if T % 4 == 3:
                def do_lr(T=T):
                    for d in range(2):
                        psl, pl = PS.get()
                        for t4 in range(4):
                            Tt = T - 3 + t4
                            op("pe", lambda e: e.matmul(psl.ap[0:16, t4 * 128:(t4 + 1) * 128], lhsT=lr_tm[Tt].ap[:, d * 16:(d + 1) * 16],
                                                        rhs=constb.ap[:, C_ID:C_ID + 128], start=True, stop=True), [lr_tm[Tt], constb], [psl],
                               inc=(t4 == 3))
                        evac(lrT, lrT.ap[0:16, d, (T - 3) * 128:(T + 1) * 128], psl.ap[0:16, 0:512], psl)
                        PS.put(pl)
                pend_lr.append([do_lr, 2])
            for it in list(pend_lr):
                it[1] -= 1
                if it[1] < 0 or T == NTILE - 1:
                    pend_lr.remove(it)
                    it[0]()
        P.free(wb, *lr_tm, *gkb)

        gv_tm = [P.sb([128, 512], BF16, "gvtm%d" % t) for t in range(NTILE)]
        wb = load_w(win_d, 512, 512)
        for _ in gate_rows():
            pass
        for T in range(NTILE):
            ps, pid = PS.get()
            proj_tm(wb, 0, 512, T, ps)
            evac(gv_tm[T], gv_tm[T].ap, ps.ap, ps)
            PS.put(pid)

        ckpt(1.7)
        rcos = P.sb([128, 1024], F32, "rcos")
        rsin = P.sb([128, 1024], F32, "rsin")
        P.dma("sp", rcos.ap, cos_d, writes=[rcos])
        P.dma("sp", rsin.ap, sin_d, writes=[rsin])
        NR = 4
        rx = [P.sb([128, 512], BF16, "rx%d" % i) for i in range(NR)]
        rt1 = [P.sb([128, 512], F32, "rt1_%d" % i) for i in range(NR)]
        rt2 = [P.sb([128, 512], F32, "rt2_%d" % i) for i in range(2)]
        rctr = [0]
        rbctr = [0]
        rope_pend = []

        def evac_rope(out_t, out_ap, ps, c0, nt):
            i = rctr[0] % NR
            rctr[0] += 1
            op("act", lambda e: e.copy(out=rx[i].ap[:, 0:nt], in_=ps.ap[:, 0:nt]), [ps], [rx[i]])
            op("dve", lambda e: e.tensor_tensor(out=rt1[i].ap[:, 0:nt], in0=ps.ap[:, 0:nt], in1=rcos.ap[:, c0:c0 + nt], op=ALU.mult),
               [ps, rcos], [rt1[i]])

            def part_b():
                k = rbctr[0] % 2
                rbctr[0] += 1
                ps2, pid2 = PS.get()
                op("pe", lambda e: e.matmul(ps2.ap[:, 0:nt], lhsT=constb.ap[:, C_R:C_R + 128], rhs=rx[i].ap[:, 0:nt], start=True, stop=True),
                   [constb, rx[i]], [ps2])
                op("dve", lambda e: e.tensor_tensor(out=rt2[k].ap[:, 0:nt], in0=ps2.ap[:, 0:nt], in1=rsin.ap[:, c0:c0 + nt], op=ALU.mult),
                   [ps2, rsin], [rt2[k]])
                PS.put(pid2)
                op("dve", lambda e: e.tensor_tensor(out=out_ap, in0=rt1[i].ap[:, 0:nt], in1=rt2[k].ap[:, 0:nt], op=ALU.add), [rt1[i], rt2[k]], [out_t])
            rope_pend.append(part_b)

        def rope_flush():
            while rope_pend:
                rope_pend.pop(0)()

        dqT = [P.sb([128, 768], BF16, "dqT%d" % h) for h in range(4)]
        wb = load_w(win_d, 1056, 512)
        for h in range(4):
            outs = proj_fm(wb, h * 128, 128, OWN_RANGES)
            rope_flush()
            for (ps, pid), (t0, nt) in zip(outs, OWN_RANGES):
                if t0 < 512:
                    evac(dqT[h], dqT[h].ap[:, t0:t0 + nt], ps.ap[:, 0:nt], ps)
                else:
                    evac_rope(dqT[h], dqT[h].ap[:, t0:t0 + nt], ps, t0 - 512, nt)
                PS.put(pid)

        for _ in gate_bcast():
            pass
        ckpt(1.75)
        dkT = [P.sb([128, NTOK], BF16, "dkT%d" % h) for h in range(4)]
        ostage = [P.sb([128, 512], F32, "ostage%d" % i) for i in range(3)]
        octr = [0]
        wb = load_w(win_d, 1568, 512)
        SAMPLE_RANGES = [(512, 512), (1024, 512)]
        for h in range(4):
            outs = proj_fm(wb, h * 128, 128, SAMPLE_RANGES)
            rope_flush()
            for (ps, pid), (t0, nt) in zip(outs, SAMPLE_RANGES):
                evac_rope(dkT[h], dkT[h].ap[:, t0:t0 + nt], ps, t0 - 512, nt)
                PS.put(pid)
        dkb = [P.sb([128, 512], BF16, "dkb%d" % i) for i in range(4)]
        pend_trk = []
        for T in range(4):
            ps, pid = PS.get()
            proj_tm(wb, 0, 512, T, ps)
            rope_flush()
            st = ostage[octr[0] % 3]
            octr[0] += 1
            evac(st, st.ap, ps.ap, ps)
            kb_ = dkb[T]
            evac(kb_, kb_.ap, ps.ap, ps)
            PS.put(pid)
            P.dma("sp", ndk_d[T * 128:(T + 1) * 128, :], st.ap, reads=[st])

            def do_trk(T=T, kb_=kb_):
                pst, ptid = PS.get()
                for h in range(4):
                    op("pe", lambda e: e.matmul(pst.ap[:, h * 128:(h + 1) * 128], lhsT=kb_.ap[:, h * 128:(h + 1) * 128],
                                                rhs=constb.ap[:, C_ID:C_ID + 128], start=True, stop=True), [kb_, constb], [pst], inc=(h == 3))
                for h in range(4):
                    evac(dkT[h], dkT[h].ap[:, T * 128:(T + 1) * 128], pst.ap[:, h * 128:(h + 1) * 128], pst)
                PS.put(ptid)
            pend_trk.append(do_trk)

        ckpt(1.8)
        dv1 = [P.sb([128, 4, 130], BF16, "dv1_%d" % t) for t in range(NTILE)]
        wb = load_w(win_d, 2080, 512)
        for T in range(NTILE):
            if T >= 2 and pend_trk:
                pend_trk.pop(0)()
            op("dve", lambda e: e.memset(dv1[T].ap[:, :, 128:130], 1.0), [], [dv1[T]])
            ps, pid = PS.get()
            proj_tm(wb, 0, 512, T, ps)
            evac(dv1[T], dv1[T].ap[:, :, 0:128], ps.ap.rearrange("p (h v) -> p h v", h=4), ps)
            if T < 4:
                st = ostage[octr[0] % 3]
                octr[0] += 1
                evac(st, st.ap, ps.ap, ps)
                P.dma("sp", ndv_d[T * 128:(T + 1) * 128, :], st.ap, reads=[st])
            PS.put(pid)

        assert not pend_trk
        P.free(*dkb)

        G = [P.sb([128, D], F32, "G%d" % t) for t in range(6)]
        gsl = [P.sb([128, 512], F32, "gsl%d" % i) for i in range(2)]
        for nb in range(2):
            wb = load_w(win_d, 2592 + nb * 512, 512)
            for T in range(6):
                ps, pid = PS.get()
                proj_tm(wb, 0, 512, T, ps)
                g1 = gsl[T % 2]
                op("act", lambda e: e.activation(out=g1.ap, in_=ps.ap, func=AF.Silu), [ps], [g1])
                PS.put(pid)
                op("dve", lambda e: e.tensor_tensor(out=G[T].ap[:, nb * 512:(nb + 1) * 512], in0=g1.ap,
                                                    in1=hgain.ap[:, nb * 512:(nb + 1) * 512], op=ALU.mult), [g1, hgain], [G[T]])
        ckpt(2)
        for b in range(3):
            for s_ in (hTa[b].tr, hTd[b].tr):
                for k, v in list(s_.r.items()) + ([s_.w] if s_.w is not None else []):
                    hT2[b].tr.r[k] = max(hT2[b].tr.r.get(k, 0), v)
        P.free(*hT2)
        P.free(rcos, rsin, *rx, *rt1, *rt2, gsl[0], gsl[1], hgain, ostage[0], ostage[1], ostage[2], modT)

        P.free(*wbufs)

        ckpt(2.5)
        NSLOT = 6
        oT = [P.sb([128, 8, 128], BF16, "oT%d" % t) for t in range(6)]
        sel = P.sb([128, 8], F32, "sel")
        P.dma("sp", sel.ap, sel_d, writes=[sel])
        ktil = [[P.sb([128, 128], BF16, "ktil") for j in range(2)] for s in range(NSLOT)]
        S = {(s, j): P.sb([128, 128], F32, "S") for s in range(NSLOT) for j in range(2)}
        acc = {(s, j): P.sb([128, 128], F32, "acc") for s in range(2) for j in range(2)}
        e_tm = [P.sb([128, 256], F32, "etm") for s in range(NSLOT)]
        l_tm = [P.sb([128, 256], BF16, "ltm") for s in range(NSLOT)]
        ew = [P.sb([128, 256], F32, "ew") for s in range(NSLOT)]
        khat = [P.sb([128, 256], BF16, "khat") for s in range(NSLOT)]
        Dt = [P.sb([128, 4], F32, "Dt") for s in range(NSLOT)]
        EB = [P.sb([128, 2, 128], F32, "EB") for s in range(NSLOT)]
        EBn = [P.sb([128, 2, 128], F32, "EBn") for s in range(NSLOT)]
        qtil = {(s, d, j): P.sb([128, 128], BF16, "qtil") for s in range(6) for d in range(2) for j in range(2)}
        attm = {(s, d): P.sb([128, 4, 128], BF16, "attm") for s in range(6) for d in range(2)}
        sbf = {(s, d, j, c): P.sb([128, 128], BF16, "sbf") for s in range(6) for d in range(2) for j in range(2) for c in range(2)}

        def gla_unit(u, T, d, own, chain_cb):
            sl = T
            if own:
                msk16 = C_MSF16 if d == 0 else C_MSB16
            else:
                msk16 = C_MSF16F if d == 0 else C_MSB16F
            l16 = C_LF16 if d == 0 else C_LB16
            l01 = C_LF01 if d == 0 else C_LB01
            (psz, pz), = yield from get_banks(1)
            op("pe", lambda e: e.matmul(psz.ap[:, 0:256], lhsT=lrT.ap[0:17, d, T * 128:(T + 1) * 128], rhs=wal.ap[0:17, d, :],
                                        start=True, stop=True), [lrT, wal], [psz])
            yield
            op("act", lambda e: e.activation(out=e_tm[u].ap, in_=psz.ap[:, 0:256], func=AF.Exp, scale=-1.0), [psz], [e_tm[u]])
            PS.put(pz)
            op("act", lambda e: e.activation(out=l_tm[u].ap, in_=e_tm[u].ap, func=AF.Ln, bias=1.0, scale=1.0), [e_tm[u]], [l_tm[u]])
            yield
            (psw, pw), = yield from get_banks(1)
            op("pe", lambda e: e.matmul(psw.ap[:, 0:256], lhsT=constb.ap[:, msk16:msk16 + 128], rhs=l_tm[u].ap, start=True, stop=True),
               [constb, l_tm[u]], [psw])
            if not own:
                for j in range(2):
                    op("pe", lambda e: e.matmul(psw.ap[:, 256 + j:257 + j], lhsT=l_tm[u].ap[:, j * 128:(j + 1) * 128],
                                                rhs=constb.ap[:, C_ONE16:C_ONE16 + 1], start=True, stop=True), [constb, l_tm[u]], [psw])
            if own:
                (psb, pb), = yield from get_banks(1)
                for j in range(2):
                    op("pe", lambda e: e.matmul(psb.ap[:, j * 128:(j + 1) * 128], lhsT=l_tm[u].ap[:, j * 128:(j + 1) * 128],
                                                rhs=constb.ap[:, l16:l16 + 128], start=True, stop=True), [constb, l_tm[u]], [psb])
            yield
            op("act", lambda e: e.activation(out=ew[u].ap, in_=psw.ap[:, 0:256], func=AF.Exp), [psw], [ew[u]])
            if not own:
                op("act", lambda e: e.activation(out=Dt[u].ap[:, 0:2], in_=psw.ap[:, 256:258], func=AF.Exp), [psw], [Dt[u]])
            PS.put(pw)
            if own:
                op("act", lambda e: e.activation(out=EB[u].ap.rearrange("p a b -> p (a b)"), in_=psb.ap[:, 0:256], func=AF.Exp),
                   [psb], [EB[u]])
                op("act", lambda e: e.activation(out=EBn[u].ap.rearrange("p a b -> p (a b)"), in_=psb.ap[:, 0:256], func=AF.Exp, scale=-1.0),
                   [psb], [EBn[u]])
                PS.put(pb)
            yield
            op(KH_ENG, lambda e: e.tensor_tensor(out=khat[u].ap, in0=gk_tm[T].ap, in1=ew[u].ap, op=ALU.mult),
               [gk_tm[T], ew[u]], [khat[u]])
            if own:
                for j in range(2):
                    qt = qtil[(sl, d, j)]
                    kt = ktil[u][j]
                    op(QK_ENG, lambda e: e.tensor_tensor(out=qt.ap, in0=gqT[j].ap[:, T * 128:(T + 1) * 128], in1=EB[u].ap[:, j, :], op=ALU.mult),
                       [gqT[j], EB[u]], [qt])
                    op(QK_ENG, lambda e: e.tensor_tensor(out=kt.ap, in0=gkT[j].ap[:, T * 128:(T + 1) * 128], in1=EBn[u].ap[:, j, :], op=ALU.mult),
                       [gkT[j], EBn[u]], [kt])
            yield
            if not own:
                (psA0, pA0), = yield from get_banks(1)
                for j in range(2):
                    for hp in range(2):
                        h = 2 * j + hp
                        op("pe", lambda e: e.matmul(
                            psA0.ap[hp * 64:(hp + 1) * 64, j * 128:(j + 1) * 128], lhsT=khat[u].ap[:, h * 64:(h + 1) * 64],
                            rhs=gv_tm[T].ap[:, h * 128:(h + 1) * 128], start=True, stop=True), [khat[u], gv_tm[T]], [psA0])
                yield
                for j in range(2):
                    chain_cb(0, j, psA0.ap[:, j * 128:(j + 1) * 128], Dt[u].ap[:, j:j + 1], psA0, Dt[u])
                PS.put(pA0)
                yield
                return
            (psA0, pA0), (psA1, pA1) = yield from get_banks(2)
            psA = (psA0, psA1)
            for j in range(2):
                for c in range(2):
                    for hp in range(2):
                        h = 2 * j + hp
                        op("pe", lambda e: e.matmul(
                            psA[c].ap[hp * 64:(hp + 1) * 64, j * 128:(j + 1) * 128], lhsT=khat[u].ap[c * 64:(c + 1) * 64, h * 64:(h + 1) * 64],
                            rhs=gv_tm[T].ap[c * 64:(c + 1) * 64, h * 128:(h + 1) * 128], start=True, stop=True),
                            [khat[u], gv_tm[T]], [psA[c]])
            yield
            corder = (0, 1) if d == 0 else (1, 0)
            for c in corder:
                for j in range(2):
                    if own:
                        col = c * 64 + (63 if d == 0 else 0)
                        chain_cb(c, j, psA[c].ap[:, j * 128:(j + 1) * 128], EB[u].ap[:, j, col:col + 1], psA[c], EB[u])
                    else:
                        chain_cb(c, j, psA[c].ap[:, j * 128:(j + 1) * 128], Dt[u].ap[:, j * 2 + c:j * 2 + c + 1], psA[c], Dt[u])
            PS.put(pA0)
            PS.put(pA1)
            yield
            if own:
                (psa0, pa0), (psa1, pa1) = yield from get_banks(2)
                psa = (psa0, psa1)
                for j in range(2):
                    for hp in range(2):
                        op("pe", lambda e: e.matmul(psa[hp].ap[:, j * 128:(j + 1) * 128], lhsT=ktil[u][j].ap[hp * 64:(hp + 1) * 64, :],
                                                    rhs=qtil[(sl, d, j)].ap[hp * 64:(hp + 1) * 64, :], start=True, stop=True),
                           [ktil[u][j], qtil[(sl, d, j)]], [psa[hp]])
                yield
                am = attm[(sl, d)]
                amv = am.ap.rearrange("p (j hp) t -> p hp j t", hp=2)
                for hp in range(2):
                    op("dve", lambda e: e.tensor_tensor(out=amv[:, hp], in0=psa[hp].ap[:, 0:256].rearrange("p (j t) -> p j t", j=2),
                                                        in1=constb.ap[:, l01:l01 + 128].unsqueeze(1).to_broadcast([128, 2, 128]), op=ALU.mult),
                       [psa[hp], constb], [am])
                PS.put(pa0)
                PS.put(pa1)
                yield

        def chain_step(s, j, A_ap, D_ap, psA, Dtile, save_to=None):
            st = S[(s, j)]
            if save_to is not None:
                if SV_ENG == "act":
                    op("act", lambda e: e.copy(out=save_to.ap, in_=st.ap), [st], [save_to])
                else:
                    op(SV_ENG, lambda e: e.tensor_copy(out=save_to.ap, in_=st.ap), [st], [save_to])
            op("dve", lambda e: e.scalar_tensor_tensor(out=st.ap, in0=st.ap, scalar=D_ap, in1=A_ap, op0=ALU.mult, op1=ALU.add),
               [st, psA, Dtile], [st])

        ssq = [P.sb([128, 4], F32, "ssq") for i in range(2)]
        rsh = [P.sb([128, 4], F32, "rsh") for i in range(2)]
        on1 = [P.sb([128, 4, 128], F32, "on1") for i in range(2)]
        junk2 = [junk, P.sb([128, 512], F32, "junk2")]
        ogb = [P.sb([128, 512], BF16, "ogb%d" % i) for i in range(2)]
        hctr = [0]

        def head_post(srcs, T, half, release=None, tset=None):
            if tset is None:
                i = hctr[0] % 2
                hctr[0] += 1
            else:
                i = tset
            jk = junk2[i]
            jv = jk.ap[:, 0:512].rearrange("p (h v) -> p h v", h=4)

            def hsel(ap3, heads):
                if len(heads) == 4:
                    return ap3
                return ap3.rearrange("p (j hp) v -> p hp j v", hp=2)[:, heads[0] % 2]

            for (st, sap, heads) in srcs:
                op("act", lambda e: e.activation(out=hsel(jv, heads), in_=sap, func=AF.Square), [st], [jk])
            yield
            op("dve", lambda e: e.tensor_reduce(out=ssq[i].ap, in_=jv, axis=AX.X, op=ALU.add), [jk], [ssq[i]])
            yield
            op("act", lambda e: e.activation(out=ssq[i].ap, in_=ssq[i].ap, func=AF.Ln, bias=epsb.ap, scale=1.0 / 128), [ssq[i], epsb], [ssq[i]])
            op("act", lambda e: e.activation(out=rsh[i].ap, in_=ssq[i].ap, func=AF.Exp, scale=-0.5), [ssq[i]], [rsh[i]])
            yield
            for (st, sap, heads) in srcs:
                nh = len(heads)
                rv = rsh[i].ap if nh == 4 else rsh[i].ap.rearrange("p (j hp) -> p hp j", hp=2)[:, heads[0] % 2]
                op("dve", lambda e: e.tensor_tensor(out=hsel(on1[i].ap, heads), in0=sap, in1=rv.unsqueeze(2).to_broadcast([128, nh, 128]), op=ALU.mult),
                   [st, rsh[i]], [on1[i]])
            if release is not None:
                release()
            op(OG_ENG, lambda e: e.tensor_tensor(out=ogb[i].ap, in0=on1[i].ap.rearrange("p h v -> p (h v)"),
                                                in1=G[T].ap[:, half * 512:(half + 1) * 512], op=ALU.mult), [on1[i], G[T]], [ogb[i]])
            yield
            yield
            yield
            (pst, pt), = yield from get_banks(1)
            for h in range(4):
                op("pe", lambda e: e.matmul(pst.ap[:, h * 128:(h + 1) * 128], lhsT=ogb[i].ap[:, h * 128:(h + 1) * 128],
                                            rhs=constb.ap[:, C_ID:C_ID + 128], start=True, stop=True), [ogb[i], constb], [pst])
            yield
            op("act", lambda e: e.copy(out=oT[T].ap[:, half * 4:(half + 1) * 4, :], in_=pst.ap.rearrange("p (h t) -> p h t", h=4)),
               [pst], [oT[T]])
            PS.put(pt)

        def gla_out(T, tset=None):
            sl = T
            (pso0, po0), (pso1, po1) = yield from get_banks(2)
            pso = (pso0, pso1)
            for h in range(4):
                j, hp = h // 2, h % 2
                ob = pso[hp]
                mms = []
                for d in range(2):
                    mms.append((ob.ap[:, j * 128:(j + 1) * 128], attm[(sl, d)].ap[:, h, :], gv_tm[T].ap[:, h * 128:(h + 1) * 128],
                                [attm[(sl, d)], gv_tm[T]]))
                for d in range(2):
                    for c in range(2):
                        mms.append((ob.ap[c * 64:(c + 1) * 64, j * 128:(j + 1) * 128],
                                    qtil[(sl, d, j)].ap[hp * 64:(hp + 1) * 64, c * 64:(c + 1) * 64],
                                    sbf[(sl, d, j, c)].ap[hp * 64:(hp + 1) * 64, :], [qtil[(sl, d, j)], sbf[(sl, d, j, c)]]))
                for n, (o_ap, l_ap, r_ap, rd) in enumerate(mms):
                    op("pe", lambda e: e.matmul(o_ap, lhsT=l_ap, rhs=r_ap, start=(n == 0), stop=(n >= len(mms) - 2)), rd, [ob])
            yield

            def rel():
                PS.put(po0)
                PS.put(po1)
            yield from head_post([(pso[hp], pso[hp].ap[:, 0:256].rearrange("p (j v) -> p j v", j=2), [hp, hp + 2]) for hp in range(2)], T, 0, release=rel, tset=tset)

        def prompt_chain(slot, sq_i, d):
            tiles = (2 * sq_i, 2 * sq_i + 1)
            for j in range(2):
                op("dve", lambda e: e.memset(S[(slot, j)].ap, 0.0), [], [S[(slot, j)]])
            order = tiles if d == 0 else tiles[::-1]
            for T in order:
                yield from gla_unit(slot, T, d, True,
                                    lambda c, j, A, Dp, psA, Dtile: chain_step(slot, j, A, Dp, psA, Dtile, save_to=sbf[(T, d, j, c)]))
            for j in range(2):
                P.dma("sp", ns_d[d, sq_i, j], S[(slot, j)].ap, reads=[S[(slot, j)]])


        def sample_chain(slot, d):
            for j in range(2):
                P.dma("sp", S[(slot, j)].ap, s0_d[d, j], writes=[S[(slot, j)]])
                op("dve", lambda e: e.memset(acc[(d, j)].ap, 0.0), [], [acc[(d, j)]])
            nchunks = [0]

            def sel_acc():
                i = nchunks[0] // 4
                for j in range(2):
                    op("dve", lambda e: e.scalar_tensor_tensor(out=acc[(d, j)].ap, in0=S[(slot, j)].ap, scalar=sel.ap[:, d * 4 + i:d * 4 + i + 1],
                                                               in1=acc[(d, j)].ap, op0=ALU.mult, op1=ALU.add),
                       [S[(slot, j)], sel, acc[(d, j)]], [acc[(d, j)]])

            def cb_others(c, j, A, Dp, psA, Dtile):
                if j == 0 and nchunks[0] % 4 == 0:
                    sel_acc()
                chain_step(slot, j, A, Dp, psA, Dtile)
                if j == 1:
                    nchunks[0] += 2

            order = list(range(6, 12)) if d == 0 else list(range(11, 5, -1))
            for T in order:
                yield from gla_unit(slot, T, d, False, cb_others)
            assert nchunks[0] == 12
            sel_acc()
            for j in range(2):
                op("dve", lambda e: e.tensor_copy(out=S[(slot, j)].ap, in_=acc[(d, j)].ap), [acc[(d, j)]], [S[(slot, j)]])
            order = (4, 5) if d == 0 else (5, 4)
            for T in order:
                yield from gla_unit(slot, T, d, True,
                                    lambda c, j, A, Dp, psA, Dtile: chain_step(slot, j, A, Dp, psA, Dtile, save_to=sbf[(T, d, j, c)]))

        def prompt_outs():
            for T in range(4):
                yield from gla_out(T)

        def seq_stream(sq_i, slot0):
            subs = [prompt_chain(slot0 + 1, sq_i, 1), prompt_chain(slot0, sq_i, 0)]
            while subs:
                for g in list(subs):
                    try:
                        next(g)
                    except StopIteration:
                        subs.remove(g)
                yield
            for T in (2 * sq_i, 2 * sq_i + 1):
                yield from gla_out(T)

        def sample_stream():
            subs = [sample_chain(4, 0), sample_chain(5, 1)]
            while subs:
                for g in list(subs):
                    try:
                        next(g)
                    except StopIteration:
                        subs.remove(g)
                yield

        run_streams([sample_stream(), seq_stream(0, 0), seq_stream(1, 2)])

        keep45 = [gv_tm[4], gv_tm[5]] + [v for k, v in list(qtil.items()) + list(attm.items()) + list(sbf.items()) if k[0] in (4, 5)]
        keep45_ids = set(id(t) for t in keep45)
        P.free(*[t for t in list(gqT) + list(gkT) + list(gk_tm) + list(gv_tm) + [lrT, wal] if id(t) not in keep45_ids], *[k for kk in ktil for k in kk], *S.values(), *acc.values(), sel, *e_tm, *l_tm, *ew, *khat,
               *Dt, *EB, *EBn, *[t for t in list(qtil.values()) + list(attm.values()) + list(sbf.values()) if id(t) not in keep45_ids])
        ckpt(3)
        ssq.append(P.sb([128, 4], F32, "ssq"))
        rsh.append(P.sb([128, 4], F32, "rsh"))
        on1.append(P.sb([128, 4, 128], F32, "on1"))
        junk2.append(P.sb([128, 512], F32, "junk2b"))
        ogb.append(P.sb([128, 512], BF16, "ogb2"))

        ckT = [P.sb([128, 512], BF16, "ckT%d" % h) for h in range(4)]
        for h in range(4):
            P.dma("pool", ckT[h].ap, ckT_d[h], writes=[ckT[h]])
        cv1 = [P.sb([128, 4, 130], BF16, "cv1_%d" % t) for t in range(4)]
        for t in range(4):
            op("dve", lambda e: e.memset(cv1[t].ap[:, :, 128:130], 1.0), [], [cv1[t]])
            P.dma("pool", cv1[t].ap[:, :, 0:128], cv_d[t * 128:(t + 1) * 128, :].rearrange("p (h v) -> p h v", h=4), writes=[cv1[t]])
        woutb = P.sb([128, 8, D], BF16, "woutb")
        for nb in range(2):
            P.dma("pool", woutb.ap[:, :, nb * 512:(nb + 1) * 512],
                  wout_d[:, nb * 512:(nb + 1) * 512].rearrange("(k p) n -> p k n", p=128), writes=[woutb])

        Ebuf = [P.sb([128, 2, 256], BF16, "E%d" % i) for i in range(10)]
        ectr = [0]
        od = [[P.sb([128, 4, 128], F32, "od") for qb in range(2)] for i in range(3)]
        junk3 = P.sb([128, D], F32, "junk3")
        rr = [P.sb([128, 4], F32, "rr") for i in range(2)]
        t1 = [P.sb([128, 128], F32, "t1") for i in range(2)]
        fgain = P.sb([128, D], F32, "fgain")
        P.dma("sp", fgain.ap, fgain_d.partition_broadcast(128), writes=[fgain])
        xt = [P.sb([128, D], F32, "xt%d" % i) for i in range(2)]
        xn = [P.sb([128, D], F32, "xn%d" % i) for i in range(2)]
        fss = [P.sb([128, 1], F32, "fss") for i in range(2)]
        frs = [P.sb([128, 1], F32, "frs") for i in range(2)]

        def final_tile(T):
            x_t = xt[T % 2]
            x_n = xn[T % 2]
            f_s, f_r = fss[T % 2], frs[T % 2]
            ii = 0 if T < 4 else 1
            P.dma("sp", x_t.ap, xtm_d[T * 128:(T + 1) * 128, :], writes=[x_t])
            for _ in range(3):
                yield
            for nb in range(2):
                (ps, pid), = yield from get_banks(1)
                for mc in range(8):
                    op("pe", lambda e: e.matmul(ps.ap, lhsT=oT[T].ap[:, mc, :], rhs=woutb.ap[:, mc, nb * 512:(nb + 1) * 512],
                                                start=(mc == 0), stop=(mc == 7)), [oT[T], woutb], [ps], inc=(mc == 7))
                yield
                op("dve", lambda e: e.tensor_tensor(out=x_n.ap[:, nb * 512:(nb + 1) * 512], in0=ps.ap, in1=gt_bc[ii].ap[:, nb * 512:(nb + 1) * 512], op=ALU.mult),
                   [ps, gt_bc[ii]], [x_n])
                PS.put(pid)
            op("dve", lambda e: e.tensor_tensor(out=x_n.ap, in0=x_n.ap, in1=x_t.ap, op=ALU.add), [x_n, x_t], [x_n])
            yield
            op("act", lambda e: e.activation(out=junk3.ap, in_=x_n.ap, func=AF.Square, accum_out=f_s.ap), [x_n], [junk3, f_s])
            yield
            op("act", lambda e: e.activation(out=f_s.ap, in_=f_s.ap, func=AF.Ln, bias=epsb.ap, scale=1.0 / D), [f_s, epsb], [f_s])
            op("act", lambda e: e.activation(out=f_r.ap, in_=f_s.ap, func=AF.Exp, scale=-0.5), [f_s], [f_r])
            yield
            op("dve", lambda e: e.scalar_tensor_tensor(out=x_n.ap, in0=x_n.ap, scalar=f_r.ap[:, 0:1], in1=fgain.ap, op0=ALU.mult, op1=ALU.mult),
               [x_n, f_r, fgain], [x_n])
            dst = yp_d[T * 128:(T + 1) * 128, :] if T < 4 else ys_d[(T - 4) * 128:(T - 3) * 128, :]
            P.dma("sp", dst, x_n.ap, reads=[x_n])

        ctxs = []
        for sq_i in range(2):
            ctxs.append((sq_i * 256, [("d", T) for T in (2 * sq_i, 2 * sq_i + 1)], (2 * sq_i, 2 * sq_i + 1)))
        ctxs.append((512, [("c", t) for t in range(4)] + [("d", T) for T in range(4, 12)], (4, 5)))

        def kblock(h, kind, t):
            if kind == "c":
                return (ckT[h].ap[:, t * 128:(t + 1) * 128], ckT[h], cv1[t].ap[:, h, 0:129], cv1[t])
            return (dkT[h].ap[:, t * 128:(t + 1) * 128], dkT[h], dv1[t].ap[:, h, 0:129], dv1[t])

        def attention_all():
            jobs = []
            for ci, (q0, kbs, tiles) in enumerate(ctxs):
                for h in range(4):
                    for p in range(len(kbs) // 2):
                        jobs.append((ci, h, p))
            Es = {}

            def qk(job):
                ci, h, p = job
                q0, kbs, tiles = ctxs[ci]
                q_ap = dqT[h].ap[:, q0:q0 + 256]
                Em = []
                for m in range(2):
                    (pss, psid), = yield from get_banks(1)
                    for i in range(2):
                        k_ap, k_t, _, _ = kblock(h, *kbs[2 * p + i])
                        op("pe", lambda e: e.matmul(pss.ap[:, i * 256:(i + 1) * 256], lhsT=k_ap[m * 64:(m + 1) * 64, :],
                                                    rhs=q_ap[m * 64:(m + 1) * 64, :], start=True, stop=True), [k_t, dqT[h]], [pss], inc=(i == 1))
                    E = Ebuf[ectr[0] % len(Ebuf)]
                    ectr[0] += 1
                    op("act", lambda e: e.activation(out=E.ap.rearrange("p i q -> p (i q)"), in_=pss.ap, func=AF.Exp, scale=SCALE_ATT),
                       [pss], [E])
                    PS.put(psid)
                    Em.append(E)
                Es[job] = Em

            obanks = None
            DEPTH = 3
            for k in range(min(DEPTH, len(jobs))):
                yield from qk(jobs[k])
            for n, job in enumerate(jobs):
                ci, h, p = job
                q0, kbs, tiles = ctxs[ci]
                nkb = len(kbs)
                if n + DEPTH < len(jobs):
                    yield from qk(jobs[n + DEPTH])
                if p == 0:
                    obanks = yield from get_banks(2)
                for i in range(2):
                    kb = 2 * p + i
                    _, _, v_ap, v_t = kblock(h, *kbs[kb])
                    for qb in range(2):
                        for m in range(2):
                            ob, _ = obanks[qb]
                            E = Es[job][m]
                            op("pe", lambda e: e.matmul(ob.ap[:, m * 129:(m + 1) * 129], lhsT=E.ap[:, i, qb * 128:(qb + 1) * 128], rhs=v_ap,
                                                        start=(kb == 0 and m == 0), stop=(kb == nkb - 1 and m == 1), skip_group_check=True),
                               [E, v_t], [ob], inc=(kb == nkb - 1 or (i == 1 and qb == 1 and m == 1)))
                del Es[job]
                yield
                if p == nkb // 2 - 1:
                    oo = od[ci]
                    osb = osbs[nctr[0] % 2]
                    nctr[0] += 1
                    for qb in range(2):
                        ob, idb = obanks[qb]
                        op("act", lambda e: e.copy(out=osb.ap[:, qb * 2:qb * 2 + 2, 0:129], in_=ob.ap[:, 0:258].rearrange("p (m c) -> p m c", m=2)),
                           [ob], [osb])
                        PS.put(idb)
                    for qb in range(2):
                        r_, t_ = rr[qb], t1[qb]
                        op("dve", lambda e: e.reciprocal(out=r_.ap[:, 0:1], in_=osb.ap[:, qb * 2, 128:129]), [osb], [r_])
                        op("dve", lambda e: e.reciprocal(out=r_.ap[:, 1:2], in_=osb.ap[:, qb * 2 + 1, 128:129]), [osb], [r_])
                        op("dve", lambda e: e.tensor_tensor(out=r_.ap[:, 2:3], in0=r_.ap[:, 1:2], in1=nlam.ap, op=ALU.mult), [r_, nlam], [r_])
                        op("dve", lambda e: e.tensor_scalar(out=t_.ap, in0=osb.ap[:, qb * 2, 0:128], scalar1=r_.ap[:, 0:1], scalar2=None, op0=ALU.mult),
                           [osb, r_], [t_])
                        op("dve", lambda e: e.scalar_tensor_tensor(out=oo[qb].ap[:, h, :], in0=osb.ap[:, qb * 2 + 1, 0:128], scalar=r_.ap[:, 2:3], in1=t_.ap,
                                                                   op0=ALU.mult, op1=ALU.add), [osb, r_, t_], [oo[qb]])
                    if h == 3:
                        for qb in range(2):
                            post_q.append((oo[qb], tiles[qb]))
                    yield

        post_q = []
        osbs = [P.sb([128, 4, 130], F32, "osb%d" % i) for i in range(2)]
        nctr = [0]

        def post_stream():
            def one_tile(o_t, T):
                yield from head_post([(o_t, o_t.ap, [0, 1, 2, 3])], T, 1, tset=T % 2)
                while T >= 4 and T not in gla_done:
                    yield
                yield from final_tile(T)

            done = 0
            while done < 6:
                if not post_q:
                    yield
                    continue
                subs = []
                while post_q and len(subs) < 2:
                    o_t, T = post_q.pop(0)
                    subs.append(one_tile(o_t, T))
                    done += 1
                while subs:
                    for g in list(subs):
                        try:
                            next(g)
                        except StopIteration:
                            subs.remove(g)
                    yield

        gla_done = set()

        def gla45():
            for T in (4, 5):
                yield from gla_out(T, tset=2)
                gla_done.add(T)

        run_streams([attention_all(), gla45(), post_stream()])
        P.free(*keep45)

        P.finish("sp")
        print("[build] instructions=%d waits=%d sbuf_peak=%d" % (P.n_ins, P.n_wait, P.peak))
    except _Stop:
        pass
    return nc


def _consts():
    c = np.zeros((128, NCONST), np.float32)
    s = np.arange(128)[:, None]
    t = np.arange(128)[None, :]
    same = (s // 64) == (t // 64)
    c[:, C_LF01:C_LF01 + 128] = (same & (s <= t))
    c[:, C_LB01:C_LB01 + 128] = (same & (s >= t))
    c[:, C_LF16:C_LF16 + 128] = (same & (s <= t)) * (-1.0 / 16)
    c[:, C_LB16:C_LB16 + 128] = (same & (s >= t)) * (-1.0 / 16)
    c[:, C_MSF16:C_MSF16 + 128] = (same & (s > t)) * (-1.0 / 16)
    c[:, C_MSB16:C_MSB16 + 128] = (same & (s < t)) * (-1.0 / 16)
    c[:, C_ID:C_ID + 128] = np.eye(128)
    R = np.zeros((128, 128), np.float32)
    for p in range(128):
        d = p % 32
        if d < 16:
            R[p + 16, p] = -1.0
        else:
            R[p - 16, p] = 1.0
    c[:, C_R:C_R + 128] = R
    c[:, C_ONES:C_ONES + 128] = 1.0
    c[:, C_CI16] = (np.arange(128) < 64) * (-1.0 / 16)
    c[:, C_CI16 + 1] = (np.arange(128) >= 64) * (-1.0 / 16)
    c[:, C_MSF16F:C_MSF16F + 128] = (s > t) * (-1.0 / 16)
    c[:, C_MSB16F:C_MSB16F + 128] = (s < t) * (-1.0 / 16)
    c[:, C_ONE16] = -1.0 / 16
    return c


def _rope_tables(tok_idx):
    row = (tok_idx // 64).astype(np.float32)
    col = (tok_idx % 64).astype(np.float32)
    inv_freq = (np.float32(10000.0) ** (-np.arange(16, dtype=np.float32) / np.float32(16))).astype(np.float32)
    cos = np.zeros((128, len(tok_idx)), np.float32)
    sin = np.zeros((128, len(tok_idx)), np.float32)
    for p in range(128):
        d = p % 64
        pos = row if d < 32 else col
        ang = (pos * inv_freq[d % 16]).astype(np.float32)
        cos[p] = np.cos(ang)
        sin[p] = np.sin(ang)
    return cos, sin


def make_in_maps(x_prompt, x_sample, cache_diff_k, cache_diff_v, state_gla_fwd, state_gla_bwd, c, c_ctx,
                 norm_gain, w_mod, b_mod, w_in, w_gla_alpha, b_gla_alpha, diff_lambda,
                 gla_head_gain, diff_head_gain, w_out, final_gain):
    f = lambda a: np.ascontiguousarray(np.asarray(a, dtype=np.float32))
    x_prompt, x_sample = f(x_prompt), f(x_sample)
    consts = _consts()
    wal = np.concatenate([np.transpose(f(w_gla_alpha)[0], (1, 0, 2)), f(b_gla_alpha)[0][None]], axis=0)
    hgain = np.concatenate([np.tile(f(gla_head_gain)[0], 4), np.tile(f(diff_head_gain)[0], 4)])
    shared = {
        "gainT": f(f(norm_gain)[0].reshape(8, 128).T),
        "wmod": f(w_mod)[0], "bmod": f(b_mod)[0], "win": f(w_in)[0], "wout": f(w_out)[0],
        "wlr": f(f(w_in)[0][:, 1024:1056].reshape(8, 128, 32).transpose(1, 0, 2).reshape(128, 256)),
        "walpha": f(wal), "dlam": f(diff_lambda)[0].reshape(256), "hgain": f(hgain), "fgain": f(final_gain),
        "consts": consts,
    }
    maps = []
    for j in range(N_CORES):
        b, r = j // 4, j % 4
        own = np.arange(r * 256, (r + 1) * 256)
        oth = np.concatenate([np.arange(0, r * 256), np.arange((r + 1) * 256, 1024)])
        tok = np.concatenate([own, oth])
        X = np.concatenate([x_prompt[2 * j], x_prompt[2 * j + 1], x_sample[b][tok]], axis=0)
        cv2 = np.stack([f(c_ctx), f(c)[b]], axis=0)
        cos, sin = _rope_tables(tok)
        sel = np.zeros((128, 8), np.float32)
        sel[:, r] = 1.0
        sel[:, 4 + (3 - r)] = 1.0
        m = dict(shared)
        m.update({
            "xT": f(X.T.reshape(8, 128, NTOK)),
            "xtm": f(X[:768]),
            "cvT": f(cv2.T.reshape(8, 128, 2).transpose(1, 0, 2)),
            "ckT": f(np.transpose(f(cache_diff_k)[b, 0].reshape(512, 4, 128), (1, 2, 0))),
            "cv": f(f(cache_diff_v)[b, 0].reshape(512, 512)),
            "s0": f(np.stack([f(state_gla_fwd)[b, 0].reshape(2, 128, 128), f(state_gla_bwd)[b, 0].reshape(2, 128, 128)], axis=0)),
            "sel": sel, "ropecos": cos, "ropesin": sin,
        })
        maps.append(m)
    return maps


def assemble(results):
    y_prompt = np.zeros((16, 256, D), np.float32)
    y_sample = np.zeros((2, 1024, D), np.float32)
    ndk = np.zeros((16, 1, 256, 4, 2, 64), np.float32)
    ndv = np.zeros((16, 1, 256, 4, 128), np.float32)
    nsf = np.zeros((16, 1, 4, 64, 128), np.float32)
    nsb = np.zeros((16, 1, 4, 64, 128), np.float32)
    for j, res in enumerate(results):
        b, r = j // 4, j % 4
        y_prompt[2 * j:2 * j + 2] = np.asarray(res["yp"]).reshape(2, 256, D)
        y_sample[b, r * 256:(r + 1) * 256] = np.asarray(res["ys"])
        ndk[2 * j:2 * j + 2, 0] = np.asarray(res["ndk"]).reshape(2, 256, 4, 2, 64)
        ndv[2 * j:2 * j + 2, 0] = np.asarray(res["ndv"]).reshape(2, 256, 4, 128)
        ns = np.asarray(res["ns"])
        nsf[2 * j:2 * j + 2, 0] = ns[0].reshape(2, 4, 64, 128)
        nsb[2 * j:2 * j + 2, 0] = ns[1].reshape(2, 4, 64, 128)
    return (y_prompt, y_sample, ndk, ndv, nsf, nsb)


_NC_CACHE = {}


def kernel(**inputs):
    maps = make_in_maps(**inputs)
    if "nc" not in _NC_CACHE:
        _NC_CACHE["nc"] = build_program()
    nc = _NC_CACHE["nc"]
    res = run_bass_kernel_spmd(nc, maps, core_ids=list(range(N_CORES)))
    return assemble(res.results)
```
